# Optimizing a Trainium2 kernel written in Bass

```python
import math
import jax, jax.numpy as jnp
from jax import lax
import numpy as np


D_MODEL = 4096
BATCH = 1
SEQ = 8192
DEPTH = 2

N_A = max(1, DEPTH // 2)
N_B = DEPTH - N_A
HG_EXPAND = 128
HG_HEADS = D_MODEL // HG_EXPAND
HG_FDIM = HG_HEADS * HG_EXPAND
HG_HEAD_V = D_MODEL // HG_HEADS
HG_CHUNK = 64
HG_IN_DIM = 2 * HG_FDIM + 2 * D_MODEL
MLA_HEADS = D_MODEL // 64
Q_LORA = D_MODEL // 4
KV_LORA = 512
NOPE_DIM = 128
ROPE_DIM = 64
V_HEAD_DIM = 128
QK_HEAD_DIM = NOPE_DIM + ROPE_DIM
ROPE_THETA = 10000.0
Q_BLOCK = 128
ATTN_SCALE = 1.0 / math.sqrt(QK_HEAD_DIM)
FFN_HIDDEN = -(-8 * D_MODEL // (3 * 256)) * 256
NORM_EPS = 1e-6
POS_OFFSET_MAX = 4096

kernel_name = 'yoco_hgrn2_mla_hybrid'


def rms_norm(x, g):
    xf = x.astype(jnp.float32)
    y = xf * lax.rsqrt(jnp.mean(xf * xf, axis=-1, keepdims=True) + NORM_EPS)
    return (y * g.astype(jnp.float32)).astype(x.dtype)


def modulate(h, shift, scale):
    return h * (1 + scale[:, None, :]) + shift[:, None, :]


def swiglu(h, w_gate, w_up, w_down):
    return (jax.nn.silu(h @ w_gate) * (h @ w_up)) @ w_down


def rope_tables(positions):
    inv_freq = 1.0 / (ROPE_THETA ** (jnp.arange(0, ROPE_DIM, 2, dtype=jnp.float32) / ROPE_DIM))
    ang = positions.astype(jnp.float32)[..., None] * inv_freq
    return jnp.cos(ang)[:, :, None, :], jnp.sin(ang)[:, :, None, :]


def apply_rope(t, cos, sin):
    t1, t2 = jnp.split(t, 2, axis=-1)
    cos = cos.astype(t.dtype)
    sin = sin.astype(t.dtype)
    return jnp.concatenate([t1 * cos - t2 * sin, t2 * cos + t1 * sin], axis=-1)


def chunk_gated_recurrence(q, k, v, log_f):
    B, S, H, _ = q.shape
    Dv = v.shape[-1]
    n_chunks = S // HG_CHUNK

    def to_chunks(t):
        return t.astype(jnp.float32).reshape(B, n_chunks, HG_CHUNK, H, -1).transpose(1, 0, 3, 2, 4)

    qc, kc, vc = to_chunks(q), to_chunks(k), to_chunks(v)
    bc = jnp.cumsum(to_chunks(log_f), axis=3)
    causal = jnp.tril(jnp.ones((HG_CHUNK, HG_CHUNK), dtype=bool))[:, :, None]

    def step(state, inp):
        q_, k_, v_, b_ = inp
        rel = b_[:, :, :, None, :] - b_[:, :, None, :, :]
        decay = jnp.exp(jnp.where(causal, rel, -jnp.inf))
        attn = jnp.einsum('bhtd,bhsd,bhtsd->bhts', q_, k_, decay)
        o = (jnp.einsum('bhts,bhsv->bhtv', attn, v_)
             + jnp.einsum('bhtd,bhdv->bhtv', q_ * jnp.exp(b_), state))
        b_last = b_[:, :, -1:, :]
        state = (jnp.exp(b_last)[:, :, 0, :, None] * state
                 + jnp.einsum('bhsd,bhsv->bhdv', k_ * jnp.exp(b_last - b_), v_))
        return state, o

    state0 = jnp.zeros((B, H, q.shape[-1], Dv), jnp.float32)
    _, o = lax.scan(step, state0, (qc, kc, vc, bc))
    return o.transpose(1, 0, 3, 2, 4).reshape(B, S, H, Dv).astype(q.dtype)


def hgrn2_mixer(h, w_in, lb, norm_g, w_out):
    B, S, _ = h.shape
    proj = h @ w_in
    q, f_logit, i, g = jnp.split(proj, [HG_FDIM, 2 * HG_FDIM, 2 * HG_FDIM + D_MODEL], axis=-1)
    q = jax.nn.silu(q)
    f = lb + (1.0 - lb) * jax.nn.sigmoid(f_logit.astype(jnp.float32))
    log_f = jnp.log(f)
    k = (1.0 - f).astype(h.dtype)
    shp = (B, S, HG_HEADS, -1)
    o = chunk_gated_recurrence(q.reshape(shp), k.reshape(shp), i.reshape(shp), log_f.reshape(shp))
    o = rms_norm(o, norm_g) * jax.nn.silu(g.reshape(shp))
    return o.reshape(B, S, D_MODEL) @ w_out


def shared_mla_kv(x, kv_norm_in_g, w_dkv, kv_norm_g, w_ukv, cos, sin):
    B, S, _ = x.shape
    src = rms_norm(x, kv_norm_in_g)
    ckv = src @ w_dkv
    c_kv, k_pe = jnp.split(ckv, [KV_LORA], axis=-1)
    c_kv = rms_norm(c_kv, kv_norm_g)
    kv = (c_kv @ w_ukv).reshape(B, S, MLA_HEADS, NOPE_DIM + V_HEAD_DIM)
    k_nope, v = jnp.split(kv, [NOPE_DIM], axis=-1)
    k_pe = apply_rope(k_pe[:, :, None, :], cos, sin)
    k = jnp.concatenate([k_nope, jnp.broadcast_to(k_pe, (B, S, MLA_HEADS, ROPE_DIM))], axis=-1)
    return k, v


def causal_block_attention(q, k, v):
    B, S, H, Dqk = q.shape
    n_blocks = S // Q_BLOCK
    qb = q.reshape(B, n_blocks, Q_BLOCK, H, Dqk).transpose(1, 0, 2, 3, 4)
    starts = jnp.arange(n_blocks, dtype=jnp.int32) * Q_BLOCK
    kpos = jnp.arange(S, dtype=jnp.int32)
    qoff = jnp.arange(Q_BLOCK, dtype=jnp.int32)

    def one_block(args):
        qblk, start = args
        s = jnp.einsum('bqhd,bkhd->bhqk', qblk, k).astype(jnp.float32) * ATTN_SCALE
        mask = (start + qoff)[:, None] >= kpos[None, :]
        p = jax.nn.softmax(jnp.where(mask, s, -jnp.inf), axis=-1)
        return jnp.einsum('bhqk,bkhv->bqhv', p.astype(v.dtype), v)

    o = lax.map(one_block, (qb, starts))
    return o.transpose(1, 0, 2, 3, 4).reshape(B, S, H, v.shape[-1])


def mla_mixer(h, w_dq, q_norm_g, w_uq, w_o, k, v, cos, sin):
    B, S, _ = h.shape
    q = (rms_norm(h @ w_dq, q_norm_g) @ w_uq).reshape(B, S, MLA_HEADS, QK_HEAD_DIM)
    q_nope, q_pe = jnp.split(q, [NOPE_DIM], axis=-1)
    q = jnp.concatenate([q_nope, apply_rope(q_pe, cos, sin)], axis=-1)
    o = causal_block_attention(q, k, v)
    return o.reshape(B, S, MLA_HEADS * V_HEAD_DIM) @ w_o


def _normal(key, shape, scale):
    return jax.random.normal(key, shape, jnp.float32) * scale


def setup_inputs(seed: int = 0) -> dict:
    key = jax.random.key(seed)
    ks = jax.random.split(key, 21)
    D = D_MODEL
    x = _normal(ks[0], (BATCH, SEQ, D), 1.0)
    c = _normal(ks[1], (BATCH, D), 1.0)
    positions = (jax.random.randint(ks[2], (BATCH, 1), 0, POS_OFFSET_MAX, jnp.int32)
                 + jnp.arange(SEQ, dtype=jnp.int32)[None, :])
    ada_w = _normal(ks[3], (DEPTH, D, 6 * D), D ** -0.5)
    ada_b = _normal(ks[4], (DEPTH, 6 * D), 0.02)
    norm_g = 1.0 + _normal(ks[5], (DEPTH, 4, D), 0.05)
    ffn_w_gate = _normal(ks[6], (DEPTH, D, FFN_HIDDEN), D ** -0.5)
    ffn_w_up = _normal(ks[7], (DEPTH, D, FFN_HIDDEN), D ** -0.5)
    ffn_w_down = _normal(ks[8], (DEPTH, FFN_HIDDEN, D), FFN_HIDDEN ** -0.5)
    hg_w_in = _normal(ks[9], (N_A, D, HG_IN_DIM), D ** -0.5)
    hg_lb_logits = _normal(ks[10], (N_A + 1, HG_FDIM), 0.5)
    hg_norm_g = 1.0 + _normal(ks[11], (N_A, HG_HEAD_V), 0.05)
    hg_w_out = _normal(ks[12], (N_A, D, D), D ** -0.5)
    mla_w_dq = _normal(ks[13], (N_B, D, Q_LORA), D ** -0.5)
    mla_q_norm_g = 1.0 + _normal(ks[14], (N_B, Q_LORA), 0.05)
    mla_w_uq = _normal(ks[15], (N_B, Q_LORA, MLA_HEADS * QK_HEAD_DIM), Q_LORA ** -0.5)
    mla_w_o = _normal(ks[16], (N_B, MLA_HEADS * V_HEAD_DIM, D), (MLA_HEADS * V_HEAD_DIM) ** -0.5)
    kv_norm_in_g = 1.0 + _normal(ks[17], (D,), 0.05)
    kv_w_dkv = _normal(ks[18], (D, KV_LORA + ROPE_DIM), D ** -0.5)
    kv_norm_g = 1.0 + _normal(ks[19], (KV_LORA,), 0.05)
    kv_w_ukv = _normal(ks[20], (KV_LORA, MLA_HEADS * (NOPE_DIM + V_HEAD_DIM)), KV_LORA ** -0.5)
    return {'x': x, 'c': c, 'positions': positions,
            'ada_w': ada_w, 'ada_b': ada_b, 'norm_g': norm_g,
            'ffn_w_gate': ffn_w_gate, 'ffn_w_up': ffn_w_up, 'ffn_w_down': ffn_w_down,
            'hg_w_in': hg_w_in, 'hg_lb_logits': hg_lb_logits, 'hg_norm_g': hg_norm_g, 'hg_w_out': hg_w_out,
            'mla_w_dq': mla_w_dq, 'mla_q_norm_g': mla_q_norm_g, 'mla_w_uq': mla_w_uq, 'mla_w_o': mla_w_o,
            'kv_norm_in_g': kv_norm_in_g, 'kv_w_dkv': kv_w_dkv, 'kv_norm_g': kv_norm_g, 'kv_w_ukv': kv_w_ukv}


def reference(x, c, positions, ada_w, ada_b, norm_g, ffn_w_gate, ffn_w_up, ffn_w_down,
              hg_w_in, hg_lb_logits, hg_norm_g, hg_w_out,
              mla_w_dq, mla_q_norm_g, mla_w_uq, mla_w_o,
              kv_norm_in_g, kv_w_dkv, kv_norm_g, kv_w_ukv):
    cos, sin = rope_tables(positions)
    lower_bounds = jnp.cumsum(jax.nn.softmax(hg_lb_logits.astype(jnp.float32), axis=0), axis=0)
    c_act = jax.nn.silu(c)
    k_shared = None
    v_shared = None
    for l in range(DEPTH):
        mod = c_act @ ada_w[l] + ada_b[l]
        sh_m, sc_m, g_m, sh_f, sc_f, g_f = jnp.split(mod, 6, axis=-1)
        h = modulate(rms_norm(x, norm_g[l, 0]), sh_m, sc_m)
        if l < N_A:
            y = hgrn2_mixer(h, hg_w_in[l], lower_bounds[l], hg_norm_g[l], hg_w_out[l])
        else:
            j = l - N_A
            y = mla_mixer(h, mla_w_dq[j], mla_q_norm_g[j], mla_w_uq[j], mla_w_o[j],
                          k_shared, v_shared, cos, sin)
        x = x + g_m[:, None, :] * rms_norm(y, norm_g[l, 1])
        h = modulate(rms_norm(x, norm_g[l, 2]), sh_f, sc_f)
        y = swiglu(h, ffn_w_gate[l], ffn_w_up[l], ffn_w_down[l])
        x = x + g_f[:, None, :] * rms_norm(y, norm_g[l, 3])
        if l == N_A - 1:
            k_shared, v_shared = shared_mla_kv(x, kv_norm_in_g, kv_w_dkv, kv_norm_g, kv_w_ukv, cos, sin)
    return x
```

```python
import contextlib
import numpy as np
import ml_dtypes
import concourse.bass as bass
import concourse.mybir as mybir
from concourse.bass_utils import run_bass_kernel_spmd

F32 = mybir.dt.float32
BF16 = mybir.dt.bfloat16
I32 = mybir.dt.int32
AF = mybir.ActivationFunctionType
ALU = mybir.AluOpType
AX = mybir.AxisListType

NCORE = 8
D = 4096
S = 8192
T = S // NCORE
NB = T // 128
DC = D // 128
HG_H = 32
FFN = 11008
FC = FFN // 128
MLA_H = 64
HPC = MLA_H // NCORE
QL = 1024
KVL = 512
EPS = 1e-6
SCALE = 1.0 / float(np.sqrt(192.0))

ENGS = ("pe", "act", "dve", "pool", "sp")


class Trk:
    __slots__ = ("w", "r")

    def __init__(self):
        self.w = []
        self.r = []


def trks(n):
    return [Trk() for _ in range(n)]


class Op:
    __slots__ = ("eng", "fn", "deps", "dma_sem", "waited_on", "val", "idx", "key")


class Scope:
    def __init__(self, p):
        self.p = p
        self.st = contextlib.ExitStack()

    def sbuf(self, name, shape, dt):
        self.p.nalloc += 1
        return self.st.enter_context(self.p.nc.sbuf_tensor(f"{name}_{self.p.nalloc}", list(shape), dt))

    def __enter__(self):
        return self

    def __exit__(self, *a):
        self.p.barrier()
        self.st.close()
        return False


class Prog:
    def __init__(self, nc):
        self.nc = nc
        self.ops = []
        self.stack = contextlib.ExitStack()
        self.last = {}
        self.pending = {e: set() for e in ENGS}
        self.nalloc = 0
        self.ps = [self.stack.enter_context(nc.psum_tensor(f"psb{i}", [128, 512], F32)) for i in range(8)]
        self.tps = trks(8)
        self.ndma = 0
        self.sem_map = {}
        self.kcount = {}
        self.epoch = {}

    def scope(self):
        return Scope(self)

    def barrier(self):
        snap = set(self.last.values())
        for e in ENGS:
            self.pending[e] |= snap
        self.sem_map = {}
        for base, n in self.kcount.items():
            if n > 24000:
                self.epoch[base] = self.epoch.get(base, 0) + 1
                self.kcount[base] = 0

    def op(self, eng, fn, r=(), w=(), dma_sem=None):
        if dma_sem is not None:
            if dma_sem not in self.sem_map:
                self.sem_map[dma_sem] = f"ds{len(self.sem_map)}"
            dma_sem = self.sem_map[dma_sem]
        base = dma_sem if dma_sem is not None else "e_" + eng
        self.kcount[base] = self.kcount.get(base, 0) + (16 if dma_sem is not None else 1)
        key = f"{base}#{self.epoch.get(base, 0)}"
        o = Op()
        o.eng, o.fn, o.dma_sem = eng, fn, dma_sem
        o.key = key
        o.waited_on = False
        o.val = None
        o.idx = len(self.ops)
        deps = set(self.pending[eng])
        self.pending[eng] = set()
        for t in r:
            deps.update(t.w)
        for t in w:
            deps.update(t.w)
            deps.update(t.r)
        o.deps = deps
        self.ops.append(o)
        for t in r:
            t.r.append(o.idx)
        for t in w:
            t.w = [o.idx]
            t.r = []
        self.last[key] = o.idx
        return o

    def dma(self, eng, out, in_, r=(), w=(), sem=None, **kw):
        if sem is None:
            sem = f"dq{self.ndma % 6}_{eng}"
            self.ndma += 1
        return self.op(eng, lambda e: e.dma_start(out=out, in_=in_, **kw), r, w, dma_sem=sem)

    def mm(self, out, lhsT, rhs, start, stop, r=(), w=()):
        return self.op("pe", lambda e: e.matmul(out, lhsT=lhsT, rhs=rhs, start=start, stop=stop), r, w)

    def tr(self, out, in_, ident, r=(), w=()):
        return self.op("pe", lambda e: e.transpose(out, in_, ident), r, w)

    def act(self, out, in_, func, r=(), w=(), **kw):
        return self.op("act", lambda e: e.activation(out=out, in_=in_, func=func, **kw), r, w)

    def tt(self, eng, out, a, b, op, r=(), w=()):
        return self.op(eng, lambda e: e.tensor_tensor(out=out, in0=a, in1=b, op=op), r, w)

    def ts(self, eng, out, a, s1, s2, op0, op1=None, r=(), w=()):
        if op1 is None:
            return self.op(eng, lambda e: e.tensor_scalar(out=out, in0=a, scalar1=s1, scalar2=None, op0=op0), r, w)
        return self.op(eng, lambda e: e.tensor_scalar(out=out, in0=a, scalar1=s1, scalar2=s2, op0=op0, op1=op1), r, w)

    def cp(self, eng, out, in_, r=(), w=()):
        if eng == "act":
            return self.op("act", lambda e: e.copy(out=out, in_=in_), r, w)
        return self.op(eng, lambda e: e.tensor_copy(out=out, in_=in_), r, w)

    def emit(self, final_ops=()):
        nc = self.nc
        ops = self.ops
        for o in ops:
            nd = set()
            best = {}
            for d in o.deps:
                pr = ops[d]
                if pr.eng == "pe" and o.eng == "pe" and pr.dma_sem is None and o.dma_sem is None:
                    continue
                if pr.dma_sem is None:
                    if best.get(pr.key, -1) < d:
                        best[pr.key] = d
                else:
                    nd.add(d)
            nd.update(best.values())
            for d in nd:
                ops[d].waited_on = True
            o.deps = nd
        for d in final_ops:
            d.waited_on = True
        for o in ops:
            if o.dma_sem is not None:
                o.waited_on = True
        cnt = {}
        for o in ops:
            if not o.waited_on:
                continue
            key = o.key
            inc = 16 if o.dma_sem is not None else 1
            cnt[key] = cnt.get(key, 0) + inc
            o.val = (key, cnt[key], inc)
        self.sem_max = dict(cnt)
        sems = {k: self.stack.enter_context(nc.semaphore(k)) for k in cnt}
        per = {e: [] for e in ENGS}
        for o in ops:
            per[o.eng].append(o)
        final = {}
        for d in final_ops:
            k, v, _ = d.val
            final[k] = max(final.get(k, 0), v)

        def run(engname, eobj):
            waited = {}
            for o in per[engname]:
                need = {}
                for d in o.deps:
                    k, v, _ = ops[d].val
                    if waited.get(k, 0) >= v:
                        continue
                    need[k] = max(need.get(k, 0), v)
                for k, v in need.items():
                    eobj.wait_ge(sems[k], v)
                    waited[k] = v
                ins = o.fn(eobj)
                if o.val is not None:
                    ins.then_inc(sems[o.val[0]], o.val[2])
            if engname == "sp":
                for k, v in final.items():
                    if waited.get(k, 0) < v:
                        eobj.wait_ge(sems[k], v)

        with nc.Block() as block:
            @block.tensor
            def _(e):
                run("pe", e)

            @block.scalar
            def _(e):
                run("act", e)

            @block.vector
            def _(e):
                run("dve", e)

            @block.gpsimd
            def _(e):
                run("pool", e)

            @block.sync
            def _(e):
                run("sp", e)
        self.stats = {e: len(per[e]) for e in ENGS}
        self.stack.close()


class Consts:
    def __init__(self, p, sc, cdram):
        self.t = Trk()
        self.idf = sc.sbuf("idf", [128, 128], F32)
        self.idb = sc.sbuf("idb", [128, 128], BF16)
        self.ones = sc.sbuf("ones", [128, 128], F32)
        self.tri = sc.sbuf("tri", [128, 128], F32)
        p.dma("sp", self.idf[:], cdram["c_idf"], w=[self.t], sem="cst0")
        p.dma("sp", self.ones[:], cdram["c_ones"], w=[Trk()], sem="cst0")
        p.dma("sp", self.tri[:], cdram["c_tri"], w=[Trk()], sem="cst0")
        o = p.dma("pool", self.idb[:], cdram["c_idf"], w=[Trk()], sem="cst1")
        p.barrier()


def load_rowbc(p, sc, name, dram_row_ap, n, eng="sp"):
    t = sc.sbuf(name, [128, n], F32)
    tk = Trk()
    p.dma(eng, t[:], dram_row_ap.partition_broadcast(128), w=[tk])
    return t, tk


def rms_rstd(p, ss, rs, tk_ss, tk_rs, n):
    p.ts("dve", rs, ss, 1.0 / n, EPS, ALU.mult, ALU.add, r=[tk_ss], w=[tk_rs])
    p.act(rs, rs, AF.Sqrt, r=[tk_rs], w=[tk_rs])
    p.op("dve", lambda e: e.reciprocal(out=rs, in_=rs), r=[tk_rs], w=[tk_rs])


def norm_to_featmajor(p, cst, x_dram, hT, t_hT, A, B, t_AB, nblk=NB, ps_banks=(0, 1), nslots=3):
    with p.scope() as sc:
        xt = [sc.sbuf("xt", [128, D], F32) for _ in range(nslots)]
        junk = sc.sbuf("junk", [128, D], BF16)
        ss = [sc.sbuf("ss", [128, 1], F32) for _ in range(nslots)]
        rs = [sc.sbuf("rs", [128, 1], F32) for _ in range(nslots)]
        t_x, t_ss, t_rs = trks(nslots), trks(nslots), trks(nslots)
        t_j = Trk()
        for b in range(nblk):
            s = b % nslots
            p.dma("sp", xt[s][:], x_dram[b * 128:(b + 1) * 128, :], w=[t_x[s]], sem=f"nx{s}")
            p.act(junk[:], xt[s][:], AF.Square, r=[t_x[s]], w=[t_j, t_ss[s]], accum_out=ss[s][:])
            rms_rstd(p, ss[s][:], rs[s][:], t_ss[s], t_rs[s], D)
            p.ts("dve", xt[s][:], xt[s][:], rs[s][:, 0:1], None, ALU.mult, r=[t_x[s], t_rs[s]], w=[t_x[s]])
            for c4 in range(DC // 4):
                bk = ps_banks[c4 % 2]
                for j in range(4):
                    c = c4 * 4 + j
                    p.tr(p.ps[bk][:, j * 128:(j + 1) * 128], xt[s][:, c * 128:(c + 1) * 128], cst.idf[:],
                         r=[t_x[s], cst.t], w=[p.tps[bk]])
                for j in range(4):
                    c = c4 * 4 + j
                    o_ap = hT[:, c, b * 128:(b + 1) * 128]
                    i_ap = p.ps[bk][:, j * 128:(j + 1) * 128]
                    if j % 2 == 0:
                        if B is None:
                            p.act(o_ap, i_ap, AF.Copy, r=[p.tps[bk], t_AB], w=[t_hT], scale=A[:, c:c + 1])
                        else:
                            p.act(o_ap, i_ap, AF.Identity, r=[p.tps[bk], t_AB], w=[t_hT],
                                  scale=A[:, c:c + 1], bias=B[:, c:c + 1])
                    else:
                        if B is None:
                            p.ts("dve", o_ap, i_ap, A[:, c:c + 1], None, ALU.mult, r=[p.tps[bk], t_AB], w=[t_hT])
                        else:
                            p.ts("dve", o_ap, i_ap, A[:, c:c + 1], B[:, c:c + 1], ALU.mult, ALU.add,
                                 r=[p.tps[bk], t_AB], w=[t_hT])


class WRing:
    def __init__(self, p, sc, nslots=4, kc=16, name="wr", dt=BF16):
        self.p = p
        self.kc = kc
        self.tiles = [sc.sbuf(name, [128, kc, 512], dt) for _ in range(nslots)]
        self.trk = trks(nslots)
        self.i = 0
        self.name = name
        self.dt = dt

    def load(self, wv, k0, kc, c0, nc_):
        s = self.i % len(self.tiles)
        self.i += 1
        eng = "pool" if self.dt == BF16 else "sp"
        self.p.dma(eng, self.tiles[s][:, 0:kc, 0:nc_], wv[:, k0:k0 + kc, c0:c0 + nc_],
                   w=[self.trk[s]], sem=f"{self.name}{s}")
        return self.tiles[s], self.trk[s]


def kunits(KC, ku=16):
    return [(k0, min(ku, KC - k0)) for k0 in range(0, KC, ku)]


def linear(p, ring, wv, KC, col_groups, act_fn, mode, evac, ntok=T, pf=2):
    units = []
    for gi, (c0, ncol) in enumerate(col_groups):
        for (k0, kc) in kunits(KC, ring.kc):
            units.append((gi, c0, ncol, k0, kc))
    loaded = {}
    ntt = ntok // 512
    nblk = ntok // 128
    for i in range(-pf, len(units)):
        j = i + pf
        if j < len(units):
            gi, c0, ncol, k0, kc = units[j]
            loaded[j] = ring.load(wv, k0, kc, c0, ncol)
        if i < 0:
            continue
        gi, c0, ncol, k0, kc = units[i]
        wt, tw = loaded.pop(i)
        first = (k0 == 0)
        last = (k0 + kc == KC)
        if mode == "F":
            nch = (ncol + 127) // 128
            for c in range(nch):
                m = min(128, ncol - c * 128)
                for tt in range(ntt):
                    bk = c * ntt + tt
                    for k in range(kc):
                        a_ap, ta = act_fn(k0 + k)
                        p.mm(p.ps[bk][0:m, :], wt[:, k, c * 128:c * 128 + m], a_ap[:, tt * 512:(tt + 1) * 512],
                             start=(first and k == 0), stop=(last and k == kc - 1), r=[tw, ta], w=[p.tps[bk]])
        else:
            for b in range(nblk):
                for k in range(kc):
                    a_ap, ta = act_fn(k0 + k)
                    p.mm(p.ps[b][:, 0:ncol], a_ap[:, b * 128:(b + 1) * 128], wt[:, k, 0:ncol],
                         start=(first and k == 0), stop=(last and k == kc - 1), r=[tw, ta], w=[p.tps[b]])
        if last:
            evac(gi, c0, ncol)


def cgroups(n0, n, step=512):
    return [(c, min(step, n0 + n - c)) for c in range(n0, n0 + n, step)]


def wview(w_ap):
    return w_ap.rearrange("(c p) n -> p c n", p=128)


def resid_pass(p, x_in, y_dram, G, t_G, x_out, nblk=NB):
    with p.scope() as sc:
        xt = [sc.sbuf("rx", [128, D], F32) for _ in range(4)]
        yt = [sc.sbuf("ry", [128, D], F32) for _ in range(4)]
        junk = sc.sbuf("rjunk", [128, D], BF16)
        ss = [sc.sbuf("rss", [128, 1], F32) for _ in range(4)]
        rs = [sc.sbuf("rrs", [128, 1], F32) for _ in range(4)]
        t_x, t_y, t_ss, t_rs = trks(4), trks(4), trks(4), trks(4)
        t_j = Trk()
        outs = []
        for b in range(nblk):
            s = b % 4
            p.dma("sp", xt[s][:], x_in[b * 128:(b + 1) * 128, :], w=[t_x[s]], sem=f"rpx{s}")
            p.dma("sp", yt[s][:], y_dram[b * 128:(b + 1) * 128, :], w=[t_y[s]], sem=f"rpy{s}")
            p.act(junk[:], yt[s][:], AF.Square, r=[t_y[s]], w=[t_j, t_ss[s]], accum_out=ss[s][:])
            rms_rstd(p, ss[s][:], rs[s][:], t_ss[s], t_rs[s], D)
            p.op("dve", lambda e, o_=yt[s][:], sc_=rs[s][:, 0:1], g_=G[:]: e.scalar_tensor_tensor(out=o_, in0=o_, scalar=sc_, in1=g_,
                                                               op0=ALU.mult, op1=ALU.mult),
                 r=[t_y[s], t_rs[s], t_G], w=[t_y[s]])
            p.tt("pool", xt[s][:], xt[s][:], yt[s][:], ALU.add, r=[t_x[s], t_y[s]], w=[t_x[s]])
            o = p.dma("sp", x_out[b * 128:(b + 1) * 128, :], xt[s][:], r=[t_x[s]], w=[], sem=f"rpo{s}")
            t_x[s].r.append(o.idx)
            outs.append(o)
        return outs


def y_evac_to_dram(p, sc, y_dram, name="yst"):
    st = [sc.sbuf(name, [128, 512], F32) for _ in range(4)]
    t_st = trks(4)
    cnt = [0]

    def evac(gi, c0, ncol):
        for b in range(NB):
            s = cnt[0] % 4
            cnt[0] += 1
            if b % 2 == 0:
                p.cp("dve", st[s][:, 0:ncol], p.ps[b][:, 0:ncol], r=[p.tps[b]], w=[t_st[s]])
            else:
                p.cp("act", st[s][:, 0:ncol], p.ps[b][:, 0:ncol], r=[p.tps[b]], w=[t_st[s]])
            o = p.dma("sp", y_dram[b * 128:(b + 1) * 128, c0:c0 + ncol], st[s][:, 0:ncol], r=[t_st[s]], sem=f"{name}{s}")
            t_st[s].r.append(o.idx)
    return evac


def ffn_stage(p, cst, x_in, x_out, wg, wu, wd, A, B, t_AB, G, t_G, hid_dram, y_dram):
    with p.scope() as sc:
        hT = sc.sbuf("hT", [128, DC, T], BF16)
        t_hT = Trk()
        norm_to_featmajor(p, cst, x_in, hT, t_hT, A, B, t_AB)
        with p.scope() as sc2:
            ring = WRing(p, sc2, nslots=4, kc=16, name="wf")
            gt = sc2.sbuf("gt", [128, 8, 512], BF16)
            t_gt = trks(8)
            hs = [sc2.sbuf("hs", [128, 512], BF16) for _ in range(4)]
            t_hs = trks(4)
            cnt = [0]
            hv = hid_dram.rearrange("(c p) t -> p c t", p=128)

            def act_fn(k):
                return hT[:, k, :], t_hT

            def evac_g(gi, c0, ncol):
                for c in range((ncol + 127) // 128):
                    for tt in range(2):
                        bk = c * 2 + tt
                        p.act(gt[:, bk, :], p.ps[bk][:], AF.Silu, r=[p.tps[bk]], w=[t_gt[bk]])

            def evac_u(gi, c0, ncol):
                for c in range((ncol + 127) // 128):
                    for tt in range(2):
                        bk = c * 2 + tt
                        s = cnt[0] % 4
                        cnt[0] += 1
                        p.tt("dve", hs[s][:], p.ps[bk][:], gt[:, bk, :], ALU.mult, r=[p.tps[bk], t_gt[bk]], w=[t_hs[s]])
                        o = p.dma("sp", hv[:, c0 // 128 + c, tt * 512:(tt + 1) * 512], hs[s][:], r=[t_hs[s]], sem=f"hs{s}")
                        t_hs[s].r.append(o.idx)

            wgv, wuv = wview(wg), wview(wu)
            for (c0, ncol) in cgroups(0, FFN):
                linear(p, ring, wgv, DC, [(c0, ncol)], act_fn, "F", evac_g)
                linear(p, ring, wuv, DC, [(c0, ncol)], act_fn, "F", evac_u)
    with p.scope() as sc:
        ring = WRing(p, sc, nslots=3, kc=16, name="wd")
        hring = [sc.sbuf("hr", [128, 16, T], BF16) for _ in range(3)]
        t_hr = trks(3)
        hv = hid_dram.rearrange("(c p) t -> p c t", p=128)
        kus = kunits(FC, 16)
        evac = y_evac_to_dram(p, sc, y_dram)
        wdv = wview(wd)
        units = []
        for gi, (c0, ncol) in enumerate(cgroups(0, D)):
            for (k0, kc) in kus:
                units.append((gi, c0, ncol, k0, kc))
        loaded = {}
        hi = [0]

        def load(j):
            gi, c0, ncol, k0, kc = units[j]
            wt, tw = ring.load(wdv, k0, kc, c0, ncol)
            s = hi[0] % 3
            hi[0] += 1
            p.dma("sp", hring[s][:, 0:kc, :], hv[:, k0:k0 + kc, :], w=[t_hr[s]], sem=f"hr{s}")
            return wt, tw, hring[s], t_hr[s]

        pf = 2
        for i in range(-pf, len(units)):
            if i + pf < len(units):
                loaded[i + pf] = load(i + pf)
            if i < 0:
                continue
            gi, c0, ncol, k0, kc = units[i]
            wt, tw, ht, th = loaded.pop(i)
            for b in range(NB):
                for k in range(kc):
                    p.mm(p.ps[b][:, 0:ncol], ht[:, k, b * 128:(b + 1) * 128], wt[:, k, 0:ncol],
                         start=(k0 == 0 and k == 0), stop=(k0 + kc == FC and k == kc - 1), r=[tw, th], w=[p.tps[b]])
            if k0 + kc == FC:
                evac(gi, c0, ncol)
    return resid_pass(p, x_in, y_dram, G, t_G, x_out)


def dram_in(nc, name, shape, dt=F32):
    return nc.dram_tensor(name, list(shape), dt, kind="ExternalInput").ap()


def dram_out(nc, name, shape, dt=F32):
    return nc.dram_tensor(name, list(shape), dt, kind="ExternalOutput").ap()


def dram_tmp(nc, name, shape, dt=F32):
    return nc.dram_tensor(name, list(shape), dt, kind="Internal").ap()


CONST_SPECS = {"c_idf": [128, 128], "c_ones": [128, 128], "c_tri": [128, 128]}


def const_arrays():
    tri = (np.arange(128)[:, None] <= np.arange(128)[None, :]).astype(np.float32)
    return {"c_idf": np.eye(128, dtype=np.float32), "c_ones": np.ones((128, 128), np.float32), "c_tri": tri}


def build_ada():
    NCOL = 6 * D // NCORE
    nc = bass.Bass("TRN2", target_bir_lowering=False)
    w = dram_in(nc, "w", [2, D, NCOL])
    bias = dram_in(nc, "b", [2, NCOL])
    c_pc = dram_in(nc, "c_pc", [128, DC])
    out = dram_out(nc, "mod", [2, NCOL])
    p = Prog(nc)
    finals = []
    with p.scope() as sc:
        cs = sc.sbuf("cs", [128, DC], F32)
        t_cs = Trk()
        p.dma("sp", cs[:], c_pc, w=[t_cs])
        p.act(cs[:], cs[:], AF.Silu, r=[t_cs], w=[t_cs])
        bt = sc.sbuf("bt", [1, 2, NCOL], F32)
        t_bt = Trk()
        p.dma("sp", bt[:], bias.rearrange("(o l) n -> o l n", o=1), w=[t_bt])
        ring = WRing(p, sc, nslots=3, kc=16, name="wa", dt=F32)
        ot = [sc.sbuf("ot", [1, 512], F32) for _ in range(2)]
        t_ot = trks(2)
        units = []
        for l in range(2):
            wv = wview(w[l])
            for (c0, ncol) in cgroups(0, NCOL):
                for (k0, kc) in kunits(DC, 16):
                    units.append((l, wv, c0, ncol, k0, kc))
        loaded = {}
        pf = 2
        n = 0
        for i in range(-pf, len(units)):
            if i + pf < len(units):
                l, wv, c0, ncol, k0, kc = units[i + pf]
                loaded[i + pf] = ring.load(wv, k0, kc, c0, ncol)
            if i < 0:
                continue
            l, wv, c0, ncol, k0, kc = units[i]
            wt, tw = loaded.pop(i)
            bk = (n // 2) % 2
            for k in range(kc):
                p.mm(p.ps[bk][0:1, 0:ncol], cs[:, k0 + k:k0 + k + 1], wt[:, k, 0:ncol],
                     start=(k0 == 0 and k == 0), stop=(k0 + kc == DC and k == kc - 1), r=[tw, t_cs], w=[p.tps[bk]])
            n += 1
            if k0 + kc == DC:
                s = bk
                p.tt("dve", ot[s][:, 0:ncol], p.ps[bk][0:1, 0:ncol], bt[:, l, c0:c0 + ncol], ALU.add,
                     r=[p.tps[bk], t_bt], w=[t_ot[s]])
                o = p.dma("sp", out[l:l + 1, c0:c0 + ncol], ot[s][:, 0:ncol], r=[t_ot[s]], sem=f"ao{s}")
                t_ot[s].r.append(o.idx)
                finals.append(o)
    p.emit(finals)
    return nc


T2 = 512
NCH = T2 // 64


def prep_affine(p, sc, vec, t_vec, ig, isc, ish, name):
    A = sc.sbuf(name, [128, DC], F32)
    tA = Trk()
    p.ts("dve", A[:], vec[:, isc, :], 1.0, None, ALU.add, r=[t_vec], w=[tA])
    p.tt("dve", A[:], A[:], vec[:, ig, :], ALU.mult, r=[tA, t_vec], w=[tA])
    return A, vec[:, ish, :], tA


def hgrn2_pass(p, cst, H, x_rows, tok0, first_pass, finals):
    A, B, t_AB = H["A"], H["B"], H["t_AB"]
    lb, oml, t_lb = H["lb"], H["oml"], H["t_lb"]
    smask, t_sm = H["smask"], H["t_sm"]
    Sall, t_S, Bc, t_Bc = H["Sall"], H["t_S"], H["Bc"], H["t_Bc"]
    wv, olocT, qhatT, sgT = H["wv"], H["olocT"], H["qhatT"], H["sgT"]
    with p.scope() as sc:
        hT = sc.sbuf("hT", [128, DC, T2], BF16)
        t_hT = Trk()
        norm_to_featmajor(p, cst, x_rows, hT, t_hT, A, B, t_AB, nblk=T2 // 128, ps_banks=(0, 1), nslots=2)
        ring = WRing(p, sc, nslots=5, kc=8, name="wi")
        qt = sc.sbuf("qt", [128, 4, T2], BF16)
        fl = sc.sbuf("fl", [128, 4, T2], F32)
        kt = sc.sbuf("kt", [128, 4, T2], BF16)
        kh = sc.sbuf("kh", [64, NCH, 4, 128], BF16)
        vt = sc.sbuf("vt", [64, NCH, 512], BF16)
        vb = [sc.sbuf("vb", [128, 512], BF16) for _ in range(2)]
        t_vb = trks(2)
        gs = [sc.sbuf("gs", [128, T2], BF16) for _ in range(2)]
        ot = [sc.sbuf("ot", [128, T2], F32) for _ in range(4)]
        lf = sc.sbuf("lf", [128, T2], F32)
        bt = sc.sbuf("bt", [128, T2], F32)
        eb = sc.sbuf("eb", [128, T2], F32)
        khT = sc.sbuf("khT", [128, T2], BF16)
        qh = [sc.sbuf("qh", [128, T2], BF16) for _ in range(2)]
        bl = sc.sbuf("bl", [128, NCH], F32)
        ebl = sc.sbuf("ebl", [128, 4, NCH], F32)
        binc = sc.sbuf("binc", [128, NCH], F32)
        EB = sc.sbuf("EB", [128, NCH], F32)
        Sb = sc.sbuf("Sb", [128, 4, 128], BF16)
        at = [sc.sbuf("at", [64, 64], BF16) for _ in range(4)]
        t_qt, t_fl, t_kt, t_kh, t_ot, t_Sb, t_at, t_ebl = trks(4), trks(4), trks(4), trks(4), trks(4), trks(4), trks(4), trks(4)
        t_vt, t_lf, t_bt, t_eb, t_khT, t_bl, t_binc, t_EB = Trk(), Trk(), Trk(), Trk(), Trk(), Trk(), Trk(), Trk()
        t_gs, t_qh = trks(2), trks(2)
        t_pA, t_pO, t_pS = trks(4), trks(4), trks(4)
        psT = p.ps[7][:].bitcast(BF16)
        gcnt = [0]

        def act_fn(k):
            return hT[:, k, :], t_hT

        for hgp in range(HG_H // 4):
            c0 = hgp * 512

            def evac_q(gi, cc0, ncol):
                for c in range(4):
                    p.act(qt[:, c, :], p.ps[c][:], AF.Silu, r=[p.tps[c]], w=[t_qt[c]])

            def evac_f(gi, cc0, ncol):
                for c in range(4):
                    h = hgp * 4 + c
                    p.act(fl[:, c, :], p.ps[c][:], AF.Sigmoid, r=[p.tps[c]], w=[t_fl[c]])
                    p.ts("dve", fl[:, c, :], fl[:, c, :], oml[:, h:h + 1], lb[:, h:h + 1], ALU.mult, ALU.add,
                         r=[t_fl[c], t_lb], w=[t_fl[c]])

            def evac_g(gi, cc0, ncol):
                for c in range(4):
                    h = hgp * 4 + c
                    s = gcnt[0] % 2
                    gcnt[0] += 1
                    p.act(gs[s][:], p.ps[c][:], AF.Silu, r=[p.tps[c]], w=[t_gs[s]])
                    o = p.dma("sp", sgT[h * 128:(h + 1) * 128, tok0:tok0 + T2], gs[s][:], r=[t_gs[s]], sem=f"gs{s}")
                    t_gs[s].r.append(o.idx)
                    finals.append(o)

            linear(p, ring, wv, DC, [(c0, 512)], act_fn, "F", evac_q, ntok=T2)
            linear(p, ring, wv, DC, [(D + c0, 512)], act_fn, "F", evac_f, ntok=T2)
            linear(p, ring, wv, DC, [(3 * D + c0, 512)], act_fn, "F", evac_g, ntok=T2)
            wts = [ring.load(wv, k0, 8, 2 * D + c0, 512) for k0 in range(0, DC, 8)]
            for j in range(NCH):
                bk = j % 2
                for k in range(DC):
                    wt, tw = wts[k // 8]
                    p.mm(p.ps[bk][0:64, :], hT[:, k, j * 64:(j + 1) * 64], wt[:, k % 8, :],
                         start=(k == 0), stop=(k == DC - 1), r=[tw, t_hT], w=[p.tps[bk]])
                if j % 2 == 0:
                    p.cp("act", vt[:, j, :], p.ps[bk][0:64, :], r=[p.tps[bk]], w=[t_vt])
                else:
                    p.cp("dve", vt[:, j, :], p.ps[bk][0:64, :], r=[p.tps[bk]], w=[t_vt])
            for c in range(4):
                h = hgp * 4 + c
                p.act(lf[:], fl[:, c, :], AF.Ln, r=[t_fl[c]], w=[t_lf])
                p.op("dve", lambda e, o_=bt[:], d0=smask[:], d1=lf[:]: e.tensor_tensor_scan(out=o_, data0=d0, data1=d1, initial=0.0,
                                                            op0=ALU.mult, op1=ALU.add),
                     r=[t_lf, t_sm], w=[t_bt])
                p.ts("pool", fl[:, c, :], fl[:, c, :], -1.0, 1.0, ALU.mult, ALU.add, r=[t_fl[c]], w=[t_fl[c]])
                p.act(eb[:], bt[:], AF.Exp, r=[t_bt], w=[t_eb])
                p.tt("dve", qt[:, c, :], qt[:, c, :], eb[:], ALU.mult, r=[t_qt[c], t_eb], w=[t_qt[c]])
                p.act(lf[:], bt[:], AF.Exp, r=[t_bt], w=[t_lf], scale=-1.0)
                p.tt("dve", fl[:, c, :], fl[:, c, :], lf[:], ALU.mult, r=[t_fl[c], t_lf], w=[t_fl[c]])
                p.cp("pool", kt[:, c, :], fl[:, c, :], r=[t_fl[c]], w=[t_kt[c]])
                btv = bt[:].rearrange("p (j t) -> p j t", t=64)
                p.cp("dve", bl[:], btv[:, :, 63], r=[t_bt], w=[t_bl])
                p.act(ebl[:, c, :], bl[:], AF.Exp, r=[t_bl], w=[t_ebl[c]])
                p.tt("dve", khT[:].rearrange("p (j t) -> p j t", t=64), fl[:, c, :].rearrange("p (j t) -> p j t", t=64),
                     ebl[:, c, :].unsqueeze(2).broadcast_to([128, NCH, 64]), ALU.mult,
                     r=[t_fl[c], t_ebl[c]], w=[t_khT])
                p.op("dve", lambda e, o_=binc[:], d0=cst.ones[:, 0:NCH], d1=bl[:], ini=Bc[:, h:h + 1]: e.tensor_tensor_scan(
                    out=o_, data0=d0, data1=d1, initial=ini, op0=ALU.mult, op1=ALU.add),
                     r=[t_bl, t_Bc[h], cst.t], w=[t_binc])
                p.tt("dve", EB[:], binc[:], bl[:], ALU.subtract, r=[t_binc, t_bl], w=[t_EB])
                p.act(EB[:], EB[:], AF.Exp, r=[t_EB], w=[t_EB])
                p.cp("dve", Bc[:, h:h + 1], binc[:, NCH - 1:NCH], r=[t_binc], w=[t_Bc[h]])
                s = h % 2
                if qhatT is not None:
                    p.tt("pool", qh[s][:].rearrange("p (j t) -> p j t", t=64), qt[:, c, :].rearrange("p (j t) -> p j t", t=64),
                         EB[:].unsqueeze(2).broadcast_to([128, NCH, 64]), ALU.mult, r=[t_qt[c], t_EB], w=[t_qh[s]])
                    o = p.dma("sp", qhatT[h * 128:(h + 1) * 128, tok0:tok0 + T2], qh[s][:], r=[t_qh[s]], sem=f"qh{s}")
                    t_qh[s].r.append(o.idx)
                    finals.append(o)
                for j in range(NCH):
                    p.tr(psT[0:64, j * 128:(j + 1) * 128], khT[:, j * 64:(j + 1) * 64], cst.idb[:],
                         r=[t_khT, cst.t], w=[p.tps[7]])
                p.cp("act", kh[:, :, c, :], psT[0:64, 0:NCH * 128].rearrange("p (j d) -> p j d", d=128),
                     r=[p.tps[7]], w=[t_kh[c]])
                p.cp("pool", Sb[:, c, :], Sall[:, h, :], r=[t_S[h]], w=[t_Sb[c]])
            for j in range(NCH):
                first_chunk = (first_pass and j == 0)
                for c in range(4):
                    h = hgp * 4 + c
                    tsl = slice(j * 64, (j + 1) * 64)
                    pA = p.ps[4][0:64, c * 64:(c + 1) * 64]
                    pO = p.ps[5][:, c * 64:(c + 1) * 64]
                    pS = p.ps[6][:, c * 128:(c + 1) * 128]
                    p.mm(pA, kt[:, c, tsl], qt[:, c, tsl], True, True, r=[t_kt[c], t_qt[c], p.tps[4]], w=[t_pA[c]])
                    p.tt("dve", at[c][:], pA, cst.tri[0:64, 0:64], ALU.mult, r=[t_pA[c], cst.t], w=[t_at[c]])
                    if not first_chunk:
                        p.mm(pO, Sb[:, c, :], qt[:, c, tsl], True, False, r=[t_Sb[c], t_qt[c], p.tps[5]], w=[t_pO[c]])
                    p.mm(pO, vt[:, j, c * 128:(c + 1) * 128], at[c][:], first_chunk, True,
                         r=[t_vt, t_at[c], p.tps[5]], w=[t_pO[c]])
                    p.cp("act", ot[c][:, tsl], pO, r=[t_pO[c]], w=[t_ot[c]])
                    p.mm(pS, kh[:, j, c, :], vt[:, j, c * 128:(c + 1) * 128], True, True,
                         r=[t_kh[c], t_vt, p.tps[6]], w=[t_pS[c]])
                    if first_chunk:
                        p.cp("dve", Sall[:, h, :], pS, r=[t_pS[c]], w=[t_S[h]])
                    else:
                        p.op("dve", lambda e, o_=Sall[:, h, :], sc_=ebl[:, c, j:j + 1], pS=pS: e.scalar_tensor_tensor(
                            out=o_, in0=o_, scalar=sc_, in1=pS,
                            op0=ALU.mult, op1=ALU.add), r=[t_S[h], t_ebl[c], t_pS[c]], w=[t_S[h]])
                    if j < NCH - 1:
                        p.cp("pool", Sb[:, c, :], Sall[:, h, :], r=[t_S[h]], w=[t_Sb[c]])
            for c in range(4):
                h = hgp * 4 + c
                o = p.dma("sp", olocT[h * 128:(h + 1) * 128, tok0:tok0 + T2], ot[c][:], r=[t_ot[c]], sem=f"ot{c}")
                t_ot[c].r.append(o.idx)
                finals.append(o)


def build_l0a():
    nc = bass.Bass("TRN2", target_bir_lowering=False)
    x = dram_in(nc, "x", [T, D])
    vec_d = dram_in(nc, "vec_pc", [128, 5, DC])
    w_in = dram_in(nc, "w_in", [D, 4 * D])
    smask_d = dram_in(nc, "c_smask", [128, T2])
    cd = {k: dram_in(nc, k, v) for k, v in CONST_SPECS.items()}
    olocT = dram_out(nc, "olocT", [D, T])
    qhatT = dram_out(nc, "qhatT", [D, T], BF16)
    sgT = dram_out(nc, "sgT", [D, T], BF16)
    S_end = dram_out(nc, "S_end", [128, HG_H, 128])
    ebtot = dram_out(nc, "ebtot", [128, HG_H])
    p = Prog(nc)
    finals = []
    wv = wview(w_in)
    with p.scope() as sc0:
        cst = Consts(p, sc0, cd)
        vec = sc0.sbuf("vec", [128, 5, DC], F32)
        t_vec = Trk()
        p.dma("sp", vec[:], vec_d, w=[t_vec])
        smask = sc0.sbuf("smask", [128, T2], F32)
        t_sm = Trk()
        p.dma("sp", smask[:], smask_d, w=[t_sm])
        A, B, t_AB = prep_affine(p, sc0, vec, t_vec, 0, 1, 2, "A0")
        lb = sc0.sbuf("lb", [128, DC], F32)
        oml = sc0.sbuf("oml", [128, DC], F32)
        t_lb = Trk()
        p.tt("dve", lb[:], vec[:, 3, :], vec[:, 4, :], ALU.subtract, r=[t_vec], w=[t_lb])
        p.act(lb[:], lb[:], AF.Sigmoid, r=[t_lb], w=[t_lb])
        p.ts("dve", oml[:], lb[:], -1.0, 1.0, ALU.mult, ALU.add, r=[t_lb], w=[t_lb])
        Sall = sc0.sbuf("Sall", [128, HG_H, 128], F32)
        t_S = trks(HG_H)
        Bc = sc0.sbuf("Bc", [128, HG_H], F32)
        t_Bc = trks(HG_H)
        p.op("dve", lambda e: e.memset(Bc[:], 0.0), w=t_Bc)
        p.op("pool", lambda e: e.memset(Sall[:], 0.0), w=t_S)
        p.barrier()
        H = dict(A=A, B=B, t_AB=t_AB, lb=lb, oml=oml, t_lb=t_lb, smask=smask, t_sm=t_sm, Sall=Sall, t_S=t_S, Bc=Bc, t_Bc=t_Bc,
                 wv=wv, olocT=olocT, qhatT=qhatT, sgT=sgT)
        for tp in range(T // T2):
            hgrn2_pass(p, cst, H, x[tp * T2:(tp + 1) * T2, :], tp * T2, tp == 0, finals)
        o = p.dma("sp", S_end, Sall[:], r=t_S, sem="fin0")
        finals.append(o)
        p.act(Bc[:], Bc[:], AF.Exp, r=t_Bc, w=t_Bc)
        o = p.dma("sp", ebtot, Bc[:], r=t_Bc, sem="fin1")
        finals.append(o)
    p.emit(finals)
    return nc


def linear_T_streamed(p, sc, wv, av, KC, col_groups, evac, name="ls"):
    ring = WRing(p, sc, nslots=3, kc=16, name=name + "w")
    hring = [sc.sbuf(name + "h", [128, 16, T], BF16) for _ in range(3)]
    t_hr = trks(3)
    units = []
    for gi, (c0, ncol) in enumerate(col_groups):
        for (k0, kc) in kunits(KC, 16):
            units.append((gi, c0, ncol, k0, kc))
    loaded = {}
    hi = [0]

    def load(j):
        gi, c0, ncol, k0, kc = units[j]
        wt, tw = ring.load(wv, k0, kc, c0, ncol)
        s = hi[0] % 3
        hi[0] += 1
        p.dma("sp", hring[s][:, 0:kc, :], av[:, k0:k0 + kc, :], w=[t_hr[s]], sem=f"{name}h{s}")
        return wt, tw, hring[s], t_hr[s]

    pf = 2
    for i in range(-pf, len(units)):
        if i + pf < len(units):
            loaded[i + pf] = load(i + pf)
        if i < 0:
            continue
        gi, c0, ncol, k0, kc = units[i]
        wt, tw, ht, th = loaded.pop(i)
        for b in range(NB):
            for k in range(kc):
                p.mm(p.ps[b][:, 0:ncol], ht[:, k, b * 128:(b + 1) * 128], wt[:, k, 0:ncol],
                     start=(k0 == 0 and k == 0), stop=(k0 + kc == KC and k == kc - 1), r=[tw, th], w=[p.tps[b]])
        if k0 + kc == KC:
            evac(gi, c0, ncol)


def tokmajor_norm_transpose(p, cst, sc, src, t_src, ncols, gain, t_gain, outT, t_outT, b, name):
    nchunk = ncols // 128
    junk = sc.sbuf(name + "j", [128, ncols], BF16)
    ss = sc.sbuf(name + "ss", [128, 1], F32)
    rs = sc.sbuf(name + "rs", [128, 1], F32)
    nb = sc.sbuf(name + "nb", [128, ncols], BF16)
    t_j, t_ss, t_rs, t_nb = Trk(), Trk(), Trk(), Trk()
    p.act(junk[:], src, AF.Square, r=[t_src], w=[t_j, t_ss], accum_out=ss[:])
    rms_rstd(p, ss[:], rs[:], t_ss, t_rs, ncols)
    p.op("dve", lambda e, o_=nb[:], i_=src, s_=rs[:, 0:1], g_=gain: e.scalar_tensor_tensor(
        out=o_, in0=i_, scalar=s_, in1=g_, op0=ALU.mult, op1=ALU.mult), r=[t_src, t_rs, t_gain], w=[t_nb])
    psT = p.ps[7][:].bitcast(BF16)
    for c in range(nchunk):
        p.tr(psT[:, c * 128:(c + 1) * 128], nb[:, c * 128:(c + 1) * 128], cst.idb[:], r=[t_nb, cst.t], w=[p.tps[7]])
    p.cp("act", outT[:, 0:nchunk, b * 128:(b + 1) * 128], psT[:, 0:nchunk * 128].rearrange("p (c t) -> p c t", t=128),
         r=[p.tps[7]], w=[t_outT])


def latents_stage(p, cst, x2, kvin_pc, t_vec, w_dkv, kvg_d, pos_d, invf_d, ckvT_dst, kpeT_dst, finals):
    with p.scope() as sc:
        hT = sc.sbuf("hT", [128, DC, T], BF16)
        t_hT = Trk()
        norm_to_featmajor(p, cst, x2, hT, t_hT, kvin_pc, None, t_vec)
        ring = WRing(p, sc, nslots=4, kc=16, name="wk")
        ck = sc.sbuf("ck", [128, NB, KVL + 64], F32)
        t_ck = trks(NB)
        kvg, t_kvg = load_rowbc(p, sc, "kvg", kvg_d, KVL)
        ckT = sc.sbuf("ckT", [128, KVL // 128, T], BF16)
        t_ckT = Trk()

        def act_fn(k):
            return hT[:, k, :], t_hT

        def evac_kv(gi, c0, ncol):
            for b in range(NB):
                p.cp("dve" if b % 2 else "act", ck[:, b, c0:c0 + ncol], p.ps[b][:, 0:ncol], r=[p.tps[b]], w=[t_ck[b]])

        linear(p, ring, wview(w_dkv), DC, cgroups(0, KVL + 64), act_fn, "T", evac_kv)
        for b in range(NB):
            with p.scope() as scb:
                tokmajor_norm_transpose(p, cst, scb, ck[:, b, 0:KVL], t_ck[b], KVL, kvg[:], t_kvg, ckT, t_ckT, b, "kvn")
        o = p.dma("sp", ckvT_dst, ckT[:], r=[t_ckT], sem="fo0")
        finals.append(o)
        posi = sc.sbuf("posi", [128, NB], I32)
        posf = sc.sbuf("posf", [128, NB], F32)
        invf = sc.sbuf("invf", [128, 32], F32)
        ang = sc.sbuf("ang", [128, NB, 64], F32)
        kk = sc.sbuf("kk", [128, NB, 64], F32)
        tmp = sc.sbuf("tmp", [128, NB, 64], F32)
        kr = sc.sbuf("kr", [128, NB, 64], F32)
        krb = sc.sbuf("krb", [128, NB, 64], BF16)
        kpT = sc.sbuf("kpT", [64, T], BF16)
        t_r = Trk()
        p.dma("sp", posi[:], pos_d, w=[t_r])
        p.dma("sp", invf[:], invf_d, w=[t_r])
        p.cp("dve", posf[:], posi[:], r=[t_r], w=[t_r])
        p.tt("dve", ang[:, :, 0:32], posf[:].unsqueeze(2).broadcast_to([128, NB, 32]),
             invf[:].unsqueeze(1).broadcast_to([128, NB, 32]), ALU.mult, r=[t_r], w=[t_r])
        p.ts("dve", ang[:, :, 32:64], ang[:, :, 0:32], float(np.pi / 2), None, ALU.add, r=[t_r], w=[t_r])
        rope_reduce_sin(p, ang[:], kk[:], tmp[:], t_r)
        sn = ang[:, :, 0:32]
        cs = ang[:, :, 32:64]
        k1 = ck[:, :, KVL:KVL + 32]
        k2 = ck[:, :, KVL + 32:KVL + 64]
        p.tt("dve", kr[:, :, 0:32], k1, cs, ALU.mult, r=[t_r] + t_ck, w=[t_r])
        p.tt("dve", tmp[:, :, 0:32], k2, sn, ALU.mult, r=[t_r], w=[t_r])
        p.tt("dve", kr[:, :, 0:32], kr[:, :, 0:32], tmp[:, :, 0:32], ALU.subtract, r=[t_r], w=[t_r])
        p.tt("dve", kr[:, :, 32:64], k2, cs, ALU.mult, r=[t_r], w=[t_r])
        p.tt("dve", tmp[:, :, 0:32], k1, sn, ALU.mult, r=[t_r], w=[t_r])
        p.tt("dve", kr[:, :, 32:64], kr[:, :, 32:64], tmp[:, :, 0:32], ALU.add, r=[t_r], w=[t_r])
        p.cp("dve", krb[:], kr[:], r=[t_r], w=[t_r])
        psT = p.ps[7][:].bitcast(BF16)
        for b in range(NB):
            p.tr(psT[0:64, b * 128:(b + 1) * 128], krb[:, b, :], cst.idb[:], r=[t_r, cst.t], w=[p.tps[7]])
        p.cp("act", kpT[:], psT[0:64, 0:T], r=[p.tps[7]], w=[t_r])
        o = p.dma("sp", kpeT_dst, kpT[:], r=[t_r], sem="fo1")
        finals.append(o)


def qlat_stage(p, cst, x2, A_1, B_1, t_AB1, w_dq, qng_d, qlatT_dst, finals):
    with p.scope() as sc:
        hT = sc.sbuf("hT", [128, DC, T], BF16)
        t_hT = Trk()
        norm_to_featmajor(p, cst, x2, hT, t_hT, A_1, B_1, t_AB1)
        ring = WRing(p, sc, nslots=4, kc=16, name="wq")
        ql = sc.sbuf("ql", [128, NB, QL], F32)
        t_ql = trks(NB)
        qng, t_qng = load_rowbc(p, sc, "qng", qng_d, QL)
        qlT = sc.sbuf("qlT", [128, QL // 128, T], BF16)
        t_qlT = Trk()

        def act_fn(k):
            return hT[:, k, :], t_hT

        def evac_q(gi, c0, ncol):
            for b in range(NB):
                p.cp("dve" if b % 2 else "act", ql[:, b, c0:c0 + ncol], p.ps[b][:, 0:ncol], r=[p.tps[b]], w=[t_ql[b]])

        linear(p, ring, wview(w_dq), DC, cgroups(0, QL), act_fn, "T", evac_q)
        for b in range(NB):
            with p.scope() as scb:
                tokmajor_norm_transpose(p, cst, scb, ql[:, b, :], t_ql[b], QL, qng[:], t_qng, qlT, t_qlT, b, "qn")
        o = p.dma("sp", qlatT_dst, qlT[:], r=[t_qlT], sem="fo2")
        finals.append(o)


NV0B = 12


def build_l0b():
    nc = bass.Bass("TRN2", target_bir_lowering=False)
    x = dram_in(nc, "x", [T, D])
    olocT = dram_in(nc, "olocT", [D, T])
    qhatT = dram_in(nc, "qhatT", [D, T], BF16)
    sgT = dram_in(nc, "sgT", [D, T], BF16)
    Sprev = dram_in(nc, "Sprev", [NCORE - 1, 128, HG_H, 128])
    ebprev = dram_in(nc, "ebprev", [128, NCORE - 1, HG_H])
    vec_d = dram_in(nc, "vec_pc", [128, 7, DC])
    hng_d = dram_in(nc, "hng", [128, 1])
    rows_d = dram_in(nc, "rows", [4, D])
    kvg_d = dram_in(nc, "kvg", [KVL])
    qng_d = dram_in(nc, "qng", [QL])
    pos_d = dram_in(nc, "pos_pb", [128, NB], I32)
    invf_d = dram_in(nc, "c_invf", [128, 32])
    w_out = dram_in(nc, "w_out", [D, D])
    wg = dram_in(nc, "wg", [D, FFN])
    wu = dram_in(nc, "wu", [D, FFN])
    wd = dram_in(nc, "wd", [FFN, D])
    w_dkv = dram_in(nc, "w_dkv", [D, KVL + 64])
    w_dq = dram_in(nc, "w_dq", [D, QL])
    cd = {k: dram_in(nc, k, v) for k, v in CONST_SPECS.items()}
    x2 = dram_out(nc, "x2", [T, D])
    ckvT_o = dram_out(nc, "ckvT", [KVL, T], BF16)
    kpeT_o = dram_out(nc, "kpeT", [64, T], BF16)
    qlatT_o = dram_out(nc, "qlatT", [QL, T], BF16)
    y_s = dram_tmp(nc, "y_s", [T, D])
    x1_s = dram_tmp(nc, "x1_s", [T, D])
    hid_s = dram_tmp(nc, "hid_s", [FFN, T], BF16)
    onT_s = dram_tmp(nc, "onT_s", [D, T], BF16)
    p = Prog(nc)
    finals = []
    with p.scope() as sc0:
        cst = Consts(p, sc0, cd)
        vec = sc0.sbuf("vec", [128, 7, DC], F32)
        t_vec = Trk()
        p.dma("sp", vec[:], vec_d, w=[t_vec])
        hng = sc0.sbuf("hng", [128, 1], F32)
        p.dma("sp", hng[:], hng_d, w=[t_vec])
        A_f, B_f, t_ABf = prep_affine(p, sc0, vec, t_vec, 0, 1, 2, "Af")
        A_1, B_1, t_AB1 = prep_affine(p, sc0, vec, t_vec, 4, 5, 6, "A1")
        p.barrier()
        with p.scope() as sc:
            Sin = sc.sbuf("Sin", [128, HG_H, 128], F32)
            Sinb = sc.sbuf("Sinb", [128, HG_H, 128], BF16)
            ebp = sc.sbuf("ebp", [128, NCORE - 1, HG_H], F32)
            Sp = [sc.sbuf("Sp", [128, HG_H, 128], F32) for _ in range(2)]
            t_Sin, t_ebp = Trk(), Trk()
            t_Sp = trks(2)
            p.dma("sp", ebp[:], ebprev, w=[t_ebp])
            p.op("dve", lambda e, o_=Sin[:]: e.memset(o_, 0.0), w=[t_Sin])
            for j in range(NCORE - 1):
                s = j % 2
                p.dma("sp", Sp[s][:], Sprev[j], w=[t_Sp[s]], sem=f"sp{s}")
                p.tt("dve", Sin[:], Sin[:], ebp[:, j, :].unsqueeze(2).broadcast_to([128, HG_H, 128]), ALU.mult,
                     r=[t_ebp], w=[t_Sin])
                p.tt("dve", Sin[:], Sin[:], Sp[s][:], ALU.add, r=[t_Sp[s]], w=[t_Sin])
            p.cp("act", Sinb[:], Sin[:], r=[t_Sin], w=[t_Sin])
            qh = [sc.sbuf("qh", [128, 512], BF16) for _ in range(4)]
            ol = [sc.sbuf("ol", [128, 512], F32) for _ in range(4)]
            sg = [sc.sbuf("sg", [128, 512], BF16) for _ in range(4)]
            sq = [sc.sbuf("sq", [128, 512], F32) for _ in range(4)]
            rt = [sc.sbuf("rt", [128, 512], F32) for _ in range(4)]
            on = [sc.sbuf("on", [128, 512], BF16) for _ in range(4)]
            t_qh, t_ol, t_sg, t_sq, t_rt, t_on = trks(4), trks(4), trks(4), trks(4), trks(4), trks(4)
            n = 0
            for h in range(HG_H):
                for tt in range(T // 512):
                    s = n % 4
                    n += 1
                    rsl = slice(h * 128, (h + 1) * 128)
                    tsl = slice(tt * 512, (tt + 1) * 512)
                    p.dma("pool", qh[s][:], qhatT[rsl, tsl], w=[t_qh[s]], sem=f"bq{s}")
                    p.dma("sp", ol[s][:], olocT[rsl, tsl], w=[t_ol[s]], sem=f"bo{s}")
                    p.dma("pool", sg[s][:], sgT[rsl, tsl], w=[t_sg[s]], sem=f"bg{s}")
                    b0, b1 = s, 4 + s
                    p.mm(p.ps[b0][:], Sinb[:, h, :], qh[s][:], True, True, r=[t_Sin, t_qh[s]], w=[p.tps[b0]])
                    p.tt("dve", ol[s][:], ol[s][:], p.ps[b0][:], ALU.add, r=[t_ol[s], p.tps[b0]], w=[t_ol[s]])
                    p.act(sq[s][:], ol[s][:], AF.Square, r=[t_ol[s]], w=[t_sq[s]])
                    p.mm(p.ps[b1][:], cst.ones[:], sq[s][:], True, True, r=[cst.t, t_sq[s]], w=[p.tps[b1]])
                    p.ts("dve", rt[s][:], p.ps[b1][:], 1.0 / 128, EPS, ALU.mult, ALU.add, r=[p.tps[b1]], w=[t_rt[s]])
                    p.act(rt[s][:], rt[s][:], AF.Sqrt, r=[t_rt[s]], w=[t_rt[s]])
                    p.op("dve", lambda e, o_=rt[s][:]: e.reciprocal(out=o_, in_=o_), r=[t_rt[s]], w=[t_rt[s]])
                    p.op("dve", lambda e, o_=ol[s][:], s_=hng[:, 0:1], r_=rt[s][:]: e.scalar_tensor_tensor(
                        out=o_, in0=o_, scalar=s_, in1=r_, op0=ALU.mult, op1=ALU.mult),
                        r=[t_ol[s], t_rt[s], t_vec], w=[t_ol[s]])
                    p.tt("pool", on[s][:], ol[s][:], sg[s][:], ALU.mult, r=[t_ol[s], t_sg[s]], w=[t_on[s]])
                    o = p.dma("sp", onT_s[rsl, tsl], on[s][:], r=[t_on[s]], sem=f"bn{s}")
                    t_on[s].r.append(o.idx)
        with p.scope() as sc:
            evac = y_evac_to_dram(p, sc, y_s)
            linear_T_streamed(p, sc, wview(w_out), onT_s.rearrange("(c p) t -> p c t", p=128), DC, cgroups(0, D), evac, name="wo")
        with p.scope() as sc:
            G, t_G = load_rowbc(p, sc, "Gm", rows_d[0], D)
            G2, t_G2 = load_rowbc(p, sc, "Gm2", rows_d[1], D)
            p.tt("dve", G[:], G[:], G2[:], ALU.mult, r=[t_G, t_G2], w=[t_G])
            resid_pass(p, x, y_s, G, t_G, x1_s)
        with p.scope() as sc:
            G, t_G = load_rowbc(p, sc, "Gf", rows_d[2], D)
            with p.scope() as scg:
                G2, t_G2 = load_rowbc(p, scg, "Gf2", rows_d[3], D)
                p.tt("dve", G[:], G[:], G2[:], ALU.mult, r=[t_G, t_G2], w=[t_G])
            finals += ffn_stage(p, cst, x1_s, x2, wg, wu, wd, A_f, B_f, t_ABf, G, t_G, hid_s, y_s)
        latents_stage(p, cst, x2, vec[:, 3, :], t_vec, w_dkv, kvg_d, pos_d, invf_d,
                      ckvT_o.rearrange("(c p) t -> p c t", p=128), kpeT_o, finals)
        qlat_stage(p, cst, x2, A_1, B_1, t_AB1, w_dq, qng_d, qlatT_o.rearrange("(c p) t -> p c t", p=128), finals)
    p.emit(finals)
    return nc


TWO_PI_HI = 6.28125
TWO_PI_LO = float(2 * np.pi - 6.28125)
MAGIC = 12582912.0


def rope_reduce_sin(p, ang, kk, tmp, t_r, eng="dve"):
    p.ts(eng, kk, ang, float(1.0 / (2 * np.pi)), MAGIC, ALU.mult, ALU.add, r=[t_r], w=[t_r])
    p.ts(eng, kk, kk, MAGIC, None, ALU.subtract, r=[t_r], w=[t_r])
    p.ts(eng, tmp, kk, TWO_PI_HI, None, ALU.mult, r=[t_r], w=[t_r])
    p.tt(eng, ang, ang, tmp, ALU.subtract, r=[t_r], w=[t_r])
    p.ts(eng, tmp, kk, TWO_PI_LO, None, ALU.mult, r=[t_r], w=[t_r])
    p.tt(eng, ang, ang, tmp, ALU.subtract, r=[t_r], w=[t_r])
    p.ts(eng, ang, ang, 3.141592, -3.141592, ALU.min, ALU.max, r=[t_r], w=[t_r])
    p.act(ang, ang, AF.Sin, r=[t_r], w=[t_r])


NQT = S // 512
NKB = S // 128


def build_attn():
    nc = bass.Bass("TRN2", target_bir_lowering=False)
    qlatT = dram_in(nc, "qlatT", [QL, S], BF16)
    ckvT = dram_in(nc, "ckvT", [KVL, S], BF16)
    kpeT = dram_in(nc, "kpeT", [64, S], BF16)
    pos_d = dram_in(nc, "pos", [S], I32)
    invf_d = dram_in(nc, "invf_p", [64, 1])
    w_qn = dram_in(nc, "w_qn", [QL, HPC * 128])
    w_qp = dram_in(nc, "w_qp", [QL, HPC * 64])
    w_k = dram_in(nc, "w_k", [KVL, HPC * 128])
    w_v = dram_in(nc, "w_v", [KVL, HPC * 128])
    rm_d = dram_in(nc, "c_rm", [64, 64])
    cd = {k: dram_in(nc, k, v) for k, v in CONST_SPECS.items()}
    oT = dram_out(nc, "oT", [HPC * 128, S], BF16)
    tabs = dram_tmp(nc, "tabs", [2, 64, S])
    p = Prog(nc)
    finals = []
    with p.scope() as sc0:
        cst = Consts(p, sc0, cd)
        ckv = sc0.sbuf("ckv", [128, 4, S], BF16)
        kpe = sc0.sbuf("kpe", [64, S], BF16)
        wqn = sc0.sbuf("wqn", [128, 8, HPC * 128], BF16)
        wqp = sc0.sbuf("wqp", [128, 8, HPC * 64], BF16)
        wk = sc0.sbuf("wk", [128, 4, HPC * 128], BF16)
        wvv = sc0.sbuf("wvv", [128, 4, HPC * 128], BF16)
        rm = sc0.sbuf("rm", [64, 64], BF16)
        tribf = sc0.sbuf("tribf", [128, 128], BF16)
        t_c = Trk()
        p.dma("sp", ckv[:], ckvT.rearrange("(c p) t -> p c t", p=128), w=[t_c], sem="a0")
        p.dma("sp", kpe[:], kpeT, w=[Trk()], sem="a1")
        p.dma("pool", wqn[:], wview(w_qn), w=[Trk()], sem="a2")
        p.dma("pool", wqp[:], wview(w_qp), w=[Trk()], sem="a3")
        p.dma("pool", wk[:], wview(w_k), w=[Trk()], sem="a4")
        p.dma("pool", wvv[:], wview(w_v), w=[Trk()], sem="a5")
        p.dma("pool", rm[:], rm_d, w=[Trk()], sem="a6")
        p.dma("pool", tribf[:], cd["c_tri"], w=[Trk()], sem="a7")
        onesb = sc0.sbuf("onesb", [128, 128], BF16)
        p.dma("pool", onesb[:], cd["c_ones"], w=[Trk()], sem="a10")
        with p.scope() as sc:
            invf = sc.sbuf("invf", [64, 1], F32)
            t_r = Trk()
            p.dma("sp", invf[:], invf_d, w=[t_r])
            posi = sc.sbuf("posi", [64, 2048], I32)
            ang = sc.sbuf("ang", [64, 2, 2048], F32)
            kk = sc.sbuf("kk", [64, 2, 2048], F32)
            tmp = sc.sbuf("tmp", [64, 2, 2048], F32)
            for q4 in range(S // 2048):
                tsl = slice(q4 * 2048, (q4 + 1) * 2048)
                p.dma("sp", posi[:], pos_d[tsl].partition_broadcast(64), w=[t_r], sem="a8")
                p.cp("dve", ang[:, 0, :], posi[:], r=[t_r], w=[t_r])
                p.ts("dve", ang[:, 0, :], ang[:, 0, :], invf[:, 0:1], None, ALU.mult, r=[t_r], w=[t_r])
                p.ts("dve", ang[:, 1, :], ang[:, 0, :], float(np.pi / 2), None, ALU.add, r=[t_r], w=[t_r])
                rope_reduce_sin(p, ang[:], kk[:], tmp[:], t_r)
                o = p.dma("sp", tabs[:, :, tsl].rearrange("a p t -> p a t"), ang[:], r=[t_r], sem="a9")
                t_r.r.append(o.idx)
        p.barrier()
        KnT = sc0.sbuf("KnT", [128, S], BF16)
        V = sc0.sbuf("V", [128, NKB, 128], BF16)
        t_Kn, t_V = Trk(), Trk()
        qlr = [sc0.sbuf("qlr", [128, 8, 512], BF16) for _ in range(2)]
        t_qlr = trks(2)
        sct = [sc0.sbuf("sct", [64, 2, 512], F32) for _ in range(2)]
        t_sct = trks(2)
        qn = [sc0.sbuf("qn", [128, 512], BF16) for _ in range(2)]
        qp = [sc0.sbuf("qp", [64, 512], BF16) for _ in range(2)]
        t_qn, t_qp = trks(2), trks(2)
        qpf = sc0.sbuf("qpf", [64, 512], F32)
        qpb = sc0.sbuf("qpb", [64, 512], BF16)
        t1 = sc0.sbuf("t1", [64, 512], F32)
        t2 = sc0.sbuf("t2", [64, 512], F32)
        t_qpf, t_qpb, t_t1, t_t2 = Trk(), Trk(), Trk(), Trk()
        pT = [sc0.sbuf("pT", [128, 512], BF16) for _ in range(4)]
        t_pT = trks(4)
        acc = [sc0.sbuf("acc", [128, 512], F32) for _ in range(2)]
        t_acc = trks(2)
        rden = sc0.sbuf("rden", [128, 512], F32)
        t_rden = Trk()
        ob = [sc0.sbuf("ob", [128, 512], BF16) for _ in range(2)]
        t_ob = trks(2)
        dsb = sc0.sbuf("dsb", [128, 512], F32)
        t_dsb = Trk()
        nq = 0
        for h in range(HPC):
            hs = slice(h * 128, (h + 1) * 128)
            for kt in range(S // 512):
                bk = kt % 2
                for kc in range(4):
                    p.mm(p.ps[bk][:], wk[:, kc, hs], ckv[:, kc, kt * 512:(kt + 1) * 512], kc == 0, kc == 3,
                         r=[t_c], w=[p.tps[bk]])
                p.cp("act" if kt % 2 else "dve", KnT[:, kt * 512:(kt + 1) * 512], p.ps[bk][:], r=[p.tps[bk]], w=[t_Kn])
            for k4 in range(NKB // 4):
                bk = k4 % 2
                for j in range(4):
                    kb = k4 * 4 + j
                    for kc in range(4):
                        p.mm(p.ps[bk][:, j * 128:(j + 1) * 128], ckv[:, kc, kb * 128:(kb + 1) * 128], wvv[:, kc, hs],
                             kc == 0, kc == 3, r=[t_c], w=[p.tps[bk]])
                p.cp("act" if k4 % 2 else "dve", V[:, k4 * 4:(k4 + 1) * 4, :],
                     p.ps[bk][:].rearrange("p (j v) -> p j v", v=128), r=[p.tps[bk]], w=[t_V])
            def prologue_dma(i, s):
                tsl = slice(i * 512, (i + 1) * 512)
                p.dma("sp", qlr[s][:], qlatT.rearrange("(c p) t -> p c t", p=128)[:, :, tsl], w=[t_qlr[s]], sem=f"ql{s}")
                p.dma("sp", sct[s][:], tabs[:, :, tsl].rearrange("a p t -> p a t"), w=[t_sct[s]], sem=f"sc{s}")

            def prologue(i, s):
                for kc in range(8):
                    p.mm(p.ps[0][:], wqn[:, kc, hs], qlr[s][:, kc, :], kc == 0, kc == 7, r=[t_qlr[s]], w=[p.tps[0]])
                p.act(qn[s][:], p.ps[0][:], AF.Copy, r=[p.tps[0]], w=[t_qn[s]], scale=SCALE)
                for kc in range(8):
                    p.mm(p.ps[1][0:64, :], wqp[:, kc, h * 64:(h + 1) * 64], qlr[s][:, kc, :], kc == 0, kc == 7,
                         r=[t_qlr[s]], w=[p.tps[1]])
                p.ts("dve", qpf[:], p.ps[1][0:64, :], SCALE, None, ALU.mult, r=[p.tps[1]], w=[t_qpf])
                p.cp("pool", qpb[:], qpf[:], r=[t_qpf], w=[t_qpb])
                p.mm(p.ps[1][0:64, :], rm[:], qpb[:], True, True, r=[t_qpb], w=[p.tps[1]])
                p.tt("dve", t1[:], qpf[:], sct[s][:, 1, :], ALU.mult, r=[t_qpf, t_sct[s]], w=[t_t1])
                p.tt("dve", t2[:], p.ps[1][0:64, :], sct[s][:, 0, :], ALU.mult, r=[p.tps[1], t_sct[s]], w=[t_t2])
                p.tt("pool", qp[s][:], t1[:], t2[:], ALU.add, r=[t_t1, t_t2], w=[t_qp[s]])

            slot0 = nq
            steps = [(i, kb) for i in range(NQT) for kb in range(4 * i + 4)]
            LAG = 2
            prologue_dma(0, slot0 % 2)
            prologue(0, slot0 % 2)
            prologue_dma(1, (slot0 + 1) % 2)

            def s_step(g):
                i, kb = steps[g]
                s = (slot0 + i) % 2
                j = kb - 4 * i
                c0 = max(j, 0) * 128
                bk = 2 + g % 3
                ks = slice(kb * 128, (kb + 1) * 128)
                p.mm(p.ps[bk][:, c0:512], KnT[:, ks], qn[s][:, c0:512], True, False, r=[t_Kn, t_qn[s]], w=[p.tps[bk]])
                p.mm(p.ps[bk][:, c0:512], kpe[:, ks], qp[s][:, c0:512], False, True, r=[t_c, t_qp[s]], w=[p.tps[bk]])
                p.act(pT[g % 4][:, c0:512], p.ps[bk][:, c0:512], AF.Exp, r=[p.tps[bk]], w=[t_pT[g % 4]])
                if j >= 0:
                    p.tt("dve", pT[g % 4][:, c0:c0 + 128], pT[g % 4][:, c0:c0 + 128], tribf[:], ALU.mult,
                         r=[t_pT[g % 4]], w=[t_pT[g % 4]])
                if kb == (4 * i + 4) // 2 and i + 1 < NQT:
                    prologue(i + 1, (slot0 + i + 1) % 2)
                    if i + 2 < NQT:
                        prologue_dma(i + 2, (slot0 + i + 2) % 2)

            def pv_step(g):
                i, kb = steps[g]
                s = (slot0 + i) % 2
                nkb = 4 * i + 4
                po = 6 + (i % 2)
                j = kb - 4 * i
                c0 = max(j, 0) * 128
                p.mm(p.ps[po][:, c0:512], V[:, kb, :], pT[g % 4][:, c0:512], kb == 0, kb == nkb - 1,
                     r=[t_V, t_pT[g % 4]], w=[p.tps[po]])
                p.mm(p.ps[5][:, c0:512], onesb[:], pT[g % 4][:, c0:512], kb == 0, kb == nkb - 1,
                     r=[t_pT[g % 4]], w=[p.tps[5]])
                if kb == nkb - 1:
                    tsl = slice(i * 512, (i + 1) * 512)
                    p.cp("act", dsb[:], p.ps[5][:], r=[p.tps[5]], w=[t_dsb])
                    p.op("dve", lambda e, o_=rden[:], i_=dsb[:]: e.reciprocal(out=o_, in_=i_), r=[t_dsb], w=[t_rden])
                    p.tt("dve", ob[s][:], p.ps[po][:], rden[:], ALU.mult, r=[p.tps[po], t_rden], w=[t_ob[s]])
                    o = p.dma("sp", oT[hs, tsl], ob[s][:], r=[t_ob[s]], sem=f"ob{s}")
                    t_ob[s].r.append(o.idx)
                    finals.append(o)

            for g in range(len(steps) + LAG):
                if g < len(steps):
                    s_step(g)
                if g - LAG >= 0:
                    pv_step(g - LAG)
            nq += NQT
    p.emit(finals)
    return nc


def build_l1b():
    nc = bass.Bass("TRN2", target_bir_lowering=False)
    x2 = dram_in(nc, "x2", [T, D])
    oT = dram_in(nc, "oT", [MLA_H * 128, T], BF16)
    vec_d = dram_in(nc, "vec_pc", [128, 3, DC])
    rows_d = dram_in(nc, "rows", [4, D])
    w_o = dram_in(nc, "w_o", [MLA_H * 128, D])
    wg = dram_in(nc, "wg", [D, FFN])
    wu = dram_in(nc, "wu", [D, FFN])
    wd = dram_in(nc, "wd", [FFN, D])
    cd = {k: dram_in(nc, k, v) for k, v in CONST_SPECS.items()}
    out = dram_out(nc, "out", [T, D])
    y_s = dram_tmp(nc, "y_s", [T, D])
    x3_s = dram_tmp(nc, "x3_s", [T, D])
    hid_s = dram_tmp(nc, "hid_s", [FFN, T], BF16)
    p = Prog(nc)
    finals = []
    with p.scope() as sc0:
        cst = Consts(p, sc0, cd)
        vec = sc0.sbuf("vec", [128, 3, DC], F32)
        t_vec = Trk()
        p.dma("sp", vec[:], vec_d, w=[t_vec])
        A_f, B_f, t_ABf = prep_affine(p, sc0, vec, t_vec, 0, 1, 2, "Af")
        p.barrier()
        with p.scope() as sc:
            evac = y_evac_to_dram(p, sc, y_s)
            linear_T_streamed(p, sc, wview(w_o), oT.rearrange("(c p) t -> p c t", p=128), MLA_H, cgroups(0, D), evac, name="wo")
        with p.scope() as sc:
            G, t_G = load_rowbc(p, sc, "Gm", rows_d[0], D)
            G2, t_G2 = load_rowbc(p, sc, "Gm2", rows_d[1], D)
            p.tt("dve", G[:], G[:], G2[:], ALU.mult, r=[t_G, t_G2], w=[t_G])
            resid_pass(p, x2, y_s, G, t_G, x3_s)
        with p.scope() as sc:
            G, t_G = load_rowbc(p, sc, "Gf", rows_d[2], D)
            with p.scope() as scg:
                G2, t_G2 = load_rowbc(p, scg, "Gf2", rows_d[3], D)
                p.tt("dve", G[:], G[:], G2[:], ALU.mult, r=[t_G, t_G2], w=[t_G])
            finals += ffn_stage(p, cst, x3_s, out, wg, wu, wd, A_f, B_f, t_ABf, G, t_G, hid_s, y_s)
    p.emit(finals)
    return nc


def _pc(v):
    return np.ascontiguousarray(np.asarray(v, np.float32).reshape(-1, 128).T)


def _run(nc, in_maps):
    res = run_bass_kernel_spmd(nc, in_maps, core_ids=list(range(NCORE)))
    return [{k: np.asarray(v) for k, v in r.items()} for r in res.results]


def _inv_freq():
    return (1.0 / (10000.0 ** (np.arange(0, 64, 2, dtype=np.float32) / 64))).astype(np.float32)


def kernel_unfused(x, c, positions, ada_w, ada_b, norm_g, ffn_w_gate, ffn_w_up, ffn_w_down,
           hg_w_in, hg_lb_logits, hg_norm_g, hg_w_out,
           mla_w_dq, mla_q_norm_g, mla_w_uq, mla_w_o,
           kv_norm_in_g, kv_w_dkv, kv_norm_g, kv_w_ukv):
    f32 = np.float32
    x = np.asarray(x, f32)
    cst = const_arrays()
    ng = np.asarray(norm_g, f32)
    ca = np.ascontiguousarray
    NCOL = 6 * D // NCORE
    c_pc = _pc(np.asarray(c, f32)[0])
    ada_w = np.asarray(ada_w, f32)
    ada_b = np.asarray(ada_b, f32)
    maps = [{"w": ca(ada_w[:, :, i * NCOL:(i + 1) * NCOL]), "b": ca(ada_b[:, i * NCOL:(i + 1) * NCOL]), "c_pc": c_pc}
            for i in range(NCORE)]
    r = _run(build_ada(), maps)
    mod = np.concatenate([o["mod"] for o in r], axis=1)
    sh_m0, sc_m0, g_m0, sh_f0, sc_f0, g_f0 = np.split(mod[0], 6)
    sh_m1, sc_m1, g_m1, sh_f1, sc_f1, g_f1 = np.split(mod[1], 6)
    lbl = np.asarray(hg_lb_logits, f32)
    vec = ca(np.stack([_pc(ng[0, 0]), _pc(sc_m0), _pc(sh_m0), _pc(lbl[0]), _pc(lbl[1])], axis=1))
    smask = np.ones((128, T2), f32)
    smask[:, ::64] = 0
    w_in = np.asarray(hg_w_in, f32)[0]
    maps = []
    for i in range(NCORE):
        m = {"x": ca(x[0, i * T:(i + 1) * T]), "vec_pc": vec, "w_in": w_in, "c_smask": smask}
        m.update(cst)
        maps.append(m)
    l0a = _run(build_l0a(), maps)
    vec = ca(np.stack([_pc(ng[0, 2]), _pc(sc_f0), _pc(sh_f0), _pc(kv_norm_in_g), _pc(ng[1, 0]), _pc(sc_m1), _pc(sh_m1)], axis=1))
    rows = ca(np.stack([g_m0, ng[0, 1], g_f0, ng[0, 3]]).astype(f32))
    invf = ca(np.tile(_inv_freq()[None, :], (128, 1)))
    pos = np.asarray(positions, np.int32)
    maps = []
    for ci in range(NCORE):
        Sprev = np.zeros((NCORE - 1, 128, HG_H, 128), f32)
        ebprev = np.zeros((128, NCORE - 1, HG_H), f32)
        for j in range(NCORE - 1):
            src = ci - (NCORE - 1) + j
            if src >= 0:
                Sprev[j] = l0a[src]["S_end"]
                ebprev[:, j, :] = l0a[src]["ebtot"]
        m = {"x": ca(x[0, ci * T:(ci + 1) * T]), "olocT": l0a[ci]["olocT"], "qhatT": l0a[ci]["qhatT"], "sgT": l0a[ci]["sgT"],
             "Sprev": Sprev, "ebprev": ebprev, "vec_pc": vec, "hng": ca(np.asarray(hg_norm_g, f32)[0].reshape(128, 1)),
             "rows": rows, "kvg": np.asarray(kv_norm_g, f32), "qng": np.asarray(mla_q_norm_g, f32)[0],
             "pos_pb": ca(pos[0, ci * T:(ci + 1) * T].reshape(NB, 128).T), "c_invf": invf,
             "w_out": np.asarray(hg_w_out, f32)[0], "wg": np.asarray(ffn_w_gate, f32)[0], "wu": np.asarray(ffn_w_up, f32)[0],
             "wd": np.asarray(ffn_w_down, f32)[0], "w_dkv": np.asarray(kv_w_dkv, f32), "w_dq": np.asarray(mla_w_dq, f32)[0]}
        m.update(cst)
        maps.append(m)
    l0b = _run(build_l0b(), maps)
    del l0a
    qlatT = np.concatenate([o["qlatT"] for o in l0b], axis=1)
    ckvT = np.concatenate([o["ckvT"] for o in l0b], axis=1)
    kpeT = np.concatenate([o["kpeT"] for o in l0b], axis=1)
    inv = _inv_freq()
    invf_p = np.concatenate([inv, inv]).reshape(64, 1).astype(f32)
    rm = np.zeros((64, 64), f32)
    for i in range(32):
        rm[i + 32, i] = -1.0
        rm[i, i + 32] = 1.0
    wuq = np.asarray(mla_w_uq, f32)[0].reshape(QL, MLA_H, 192)
    wukv = np.asarray(kv_w_ukv, f32).reshape(KVL, MLA_H, 256)
    maps = []
    for ci in range(NCORE):
        hs = slice(HPC * ci, HPC * (ci + 1))
        m = {"qlatT": qlatT, "ckvT": ckvT, "kpeT": kpeT, "pos": ca(pos[0]), "invf_p": invf_p,
             "w_qn": ca(wuq[:, hs, :128]).reshape(QL, HPC * 128), "w_qp": ca(wuq[:, hs, 128:]).reshape(QL, HPC * 64),
             "w_k": ca(wukv[:, hs, :128]).reshape(KVL, HPC * 128), "w_v": ca(wukv[:, hs, 128:]).reshape(KVL, HPC * 128),
             "c_rm": rm}
        m.update(cst)
        maps.append(m)
    att = _run(build_attn(), maps)
    oT_all = np.concatenate([o["oT"] for o in att], axis=0)
    del att
    vec = ca(np.stack([_pc(ng[1, 2]), _pc(sc_f1), _pc(sh_f1)], axis=1))
    rows = ca(np.stack([g_m1, ng[1, 1], g_f1, ng[1, 3]]).astype(f32))
    maps = []
    for ci in range(NCORE):
        m = {"x2": l0b[ci]["x2"], "oT": ca(oT_all[:, ci * T:(ci + 1) * T]), "vec_pc": vec, "rows": rows,
             "w_o": np.asarray(mla_w_o, f32)[0], "wg": np.asarray(ffn_w_gate, f32)[1], "wu": np.asarray(ffn_w_up, f32)[1],
             "wd": np.asarray(ffn_w_down, f32)[1]}
        m.update(cst)
        maps.append(m)
    l1b = _run(build_l1b(), maps)
    out = np.concatenate([o["out"] for o in l1b], axis=0).reshape(1, S, D).astype(f32)
    return out


def ada_stage(p, cst, w, bias, c_pc, mod_s, ncols):
    with p.scope() as sc:
        cs = sc.sbuf("cs", [128, DC], F32)
        t_cs = Trk()
        p.dma("sp", cs[:], c_pc, w=[t_cs])
        p.act(cs[:], cs[:], AF.Silu, r=[t_cs], w=[t_cs])
        ring = WRing(p, sc, nslots=3, kc=16, name="wa", dt=F32)
        ot = [sc.sbuf("ot", [1, 512], F32) for _ in range(2)]
        bt = [sc.sbuf("bt", [1, 512], F32) for _ in range(2)]
        t_ot, t_bt = trks(2), trks(2)
        units = []
        for l in range(2):
            wv = wview(w[l])
            for (c0, ncol) in cgroups(0, ncols):
                for (k0, kc) in kunits(DC, 16):
                    units.append((l, wv, c0, ncol, k0, kc))
        loaded = {}
        pf = 2
        n = 0
        for i in range(-pf, len(units)):
            if i + pf < len(units):
                l, wv, c0, ncol, k0, kc = units[i + pf]
                loaded[i + pf] = ring.load(wv, k0, kc, c0, ncol)
            if i < 0:
                continue
            l, wv, c0, ncol, k0, kc = units[i]
            wt, tw = loaded.pop(i)
            bk = (n // 2) % 2
            if k0 == 0:
                p.dma("sp", bt[bk][:, 0:ncol], bias[l:l + 1, c0:c0 + ncol], w=[t_bt[bk]], sem=f"ab{bk}")
            for k in range(kc):
                p.mm(p.ps[bk][0:1, 0:ncol], cs[:, k0 + k:k0 + k + 1], wt[:, k, 0:ncol],
                     start=(k0 == 0 and k == 0), stop=(k0 + kc == DC and k == kc - 1), r=[tw, t_cs], w=[p.tps[bk]])
            n += 1
            if k0 + kc == DC:
                s = bk
                p.tt("dve", ot[s][:, 0:ncol], p.ps[bk][0:1, 0:ncol], bt[s][:, 0:ncol], ALU.add,
                     r=[p.tps[bk], t_bt[s]], w=[t_ot[s]])
                o = p.dma("sp", mod_s[l:l + 1, c0:c0 + ncol], ot[s][:, 0:ncol], r=[t_ot[s]], sem=f"ao{s}")
                t_ot[s].r.append(o.idx)


def gate_stage(p, cst, olocT, sgT, onT_s, hng, t_hng):
    with p.scope() as sc:
        ol = [sc.sbuf("ol", [128, 512], F32) for _ in range(2)]
        sg = [sc.sbuf("sg", [128, 512], BF16) for _ in range(2)]
        sq = [sc.sbuf("sq", [128, 512], F32) for _ in range(2)]
        rt = [sc.sbuf("rt", [128, 512], F32) for _ in range(2)]
        on = [sc.sbuf("on", [128, 512], BF16) for _ in range(2)]
        t_ol, t_sg, t_sq, t_rt, t_on = trks(2), trks(2), trks(2), trks(2), trks(2)
        n = 0
        for h in range(HG_H):
            for tt in range(T // 512):
                s = n % 2
                n += 1
                rsl = slice(h * 128, (h + 1) * 128)
                tsl = slice(tt * 512, (tt + 1) * 512)
                p.dma("sp", ol[s][:], olocT[rsl, tsl], w=[t_ol[s]], sem=f"bo{s}")
                p.dma("sp", sg[s][:], sgT[rsl, tsl], w=[t_sg[s]], sem=f"bg{s}")
                b1 = 2 + s
                p.act(sq[s][:], ol[s][:], AF.Square, r=[t_ol[s]], w=[t_sq[s]])
                p.mm(p.ps[b1][:], cst.ones[:], sq[s][:], True, True, r=[cst.t, t_sq[s]], w=[p.tps[b1]])
                p.ts("dve", rt[s][:], p.ps[b1][:], 1.0 / 128, EPS, ALU.mult, ALU.add, r=[p.tps[b1]], w=[t_rt[s]])
                p.act(rt[s][:], rt[s][:], AF.Sqrt, r=[t_rt[s]], w=[t_rt[s]])
                p.op("dve", lambda e, o_=rt[s][:]: e.reciprocal(out=o_, in_=o_), r=[t_rt[s]], w=[t_rt[s]])
                p.op("dve", lambda e, o_=ol[s][:], s_=hng[:, 0:1], r_=rt[s][:]: e.scalar_tensor_tensor(
                    out=o_, in0=o_, scalar=s_, in1=r_, op0=ALU.mult, op1=ALU.mult),
                    r=[t_ol[s], t_rt[s], t_hng], w=[t_ol[s]])
                p.tt("pool", on[s][:], ol[s][:], sg[s][:], ALU.mult, r=[t_ol[s], t_sg[s]], w=[t_on[s]])
                o = p.dma("sp", onT_s[rsl, tsl], on[s][:], r=[t_on[s]], sem=f"bn{s}")
                t_on[s].r.append(o.idx)


def select_pass(p, x_src, x_acc, selcol, t_sel, first):
    with p.scope() as sc:
        a = [sc.sbuf("sa", [128, D], F32) for _ in range(2)]
        o = [sc.sbuf("so", [128, D], F32) for _ in range(2)]
        t_a, t_o = trks(2), trks(2)
        for b in range(NB):
            s = b % 2
            rows = slice(b * 128, (b + 1) * 128)
            p.dma("sp", a[s][:], x_src[rows, :], w=[t_a[s]], sem=f"sa{s}")
            if first:
                p.ts("dve", o[s][:], a[s][:], selcol, None, ALU.mult, r=[t_a[s], t_sel], w=[t_o[s]])
            else:
                p.dma("sp", o[s][:], x_acc[rows, :], w=[t_o[s]], sem=f"so{s}")
                p.op("dve", lambda e, o_=o[s][:], a_=a[s][:], c_=selcol: e.scalar_tensor_tensor(
                    out=o_, in0=a_, scalar=c_, in1=o_, op0=ALU.mult, op1=ALU.add), r=[t_a[s], t_sel, t_o[s]], w=[t_o[s]])
            d = p.dma("sp", x_acc[rows, :], o[s][:], r=[t_o[s]], sem=f"sw{s}")
            t_o[s].r.append(d.idx)


def attn_own_stage(p, cst, qlatT_own, ckvT_all, kpeT_all, pos_own_d, invf_p_d, w_qn, w_qp, w_k, w_v, rm_d, dm_d, thr_d, tri_d, oT_own):
    NT_Q = T // 512
    with p.scope() as sc0:
        ckv = sc0.sbuf("ckv", [128, 4, S], BF16)
        kpe = sc0.sbuf("kpe", [64, S], BF16)
        qlat = sc0.sbuf("qlat", [128, 8, T], BF16)
        rm = sc0.sbuf("rm", [64, 64], BF16)
        Dm = sc0.sbuf("Dm", [128, 512], F32)
        thr = sc0.sbuf("thr", [128, NT_Q * NKB], F32)
        tabs = sc0.sbuf("tabs", [64, 2, T], F32)
        t_c = Trk()
        p.dma("sp", ckv[:], ckvT_all.rearrange("(c p) t -> p c t", p=128), w=[t_c], sem="a0")
        p.dma("sp", kpe[:], kpeT_all, w=[Trk()], sem="a1")
        p.dma("sp", qlat[:], qlatT_own.rearrange("(c p) t -> p c t", p=128), w=[Trk()], sem="a2")
        p.dma("pool", rm[:], rm_d, w=[Trk()], sem="a6")
        p.dma("sp", Dm[:], dm_d, w=[Trk()], sem="a7")
        p.dma("sp", thr[:], thr_d, w=[Trk()], sem="a3")
        with p.scope() as sc:
            invf = sc.sbuf("invf", [64, 1], F32)
            t_r = Trk()
            p.dma("sp", invf[:], invf_p_d, w=[t_r])
            posi = sc.sbuf("posi", [64, T], I32)
            kk = sc.sbuf("kk", [64, 2, T], F32)
            tmp = sc.sbuf("tmp", [64, 2, T], F32)
            p.dma("sp", posi[:], pos_own_d.partition_broadcast(64), w=[t_r], sem="a8")
            p.cp("dve", tabs[:, 0, :], posi[:], r=[t_r], w=[t_r])
            p.ts("dve", tabs[:, 0, :], tabs[:, 0, :], invf[:, 0:1], None, ALU.mult, r=[t_r], w=[t_r])
            p.ts("dve", tabs[:, 1, :], tabs[:, 0, :], float(np.pi / 2), None, ALU.add, r=[t_r], w=[t_r])
            rope_reduce_sin(p, tabs[:], kk[:], tmp[:], t_r)
        p.barrier()
        KnT = sc0.sbuf("KnT", [128, S], BF16)
        V = sc0.sbuf("V", [128, NKB, 128], BF16)
        t_Kn, t_V = Trk(), Trk()
        wqn = [sc0.sbuf("wqn", [128, 8, 128], BF16) for _ in range(2)]
        wqp = [sc0.sbuf("wqp", [128, 8, 64], BF16) for _ in range(2)]
        wk = [sc0.sbuf("wk", [128, 4, 128], BF16) for _ in range(2)]
        wvv = [sc0.sbuf("wvv", [128, 4, 128], BF16) for _ in range(2)]
        t_w = trks(2)
        qn = [sc0.sbuf("qn", [128, 512], BF16) for _ in range(2)]
        qp = [sc0.sbuf("qp", [64, 512], BF16) for _ in range(2)]
        t_qn, t_qp = trks(2), trks(2)
        qpf = sc0.sbuf("qpf", [64, 512], F32)
        qpb = sc0.sbuf("qpb", [64, 512], BF16)
        t1 = sc0.sbuf("t1", [64, 512], F32)
        t2 = sc0.sbuf("t2", [64, 512], F32)
        t_qpf, t_qpb, t_t1, t_t2 = Trk(), Trk(), Trk(), Trk()
        pT = [sc0.sbuf("pT", [128, 512], BF16) for _ in range(4)]
        t_pT = trks(4)
        acc = sc0.sbuf("acc", [128, 512], F32)
        t_acc = Trk()
        rden = sc0.sbuf("rden", [128, 512], F32)
        t_rden = Trk()
        ob = [sc0.sbuf("ob", [128, 512], BF16) for _ in range(2)]
        t_ob = trks(2)
        nq = 0
        wqnv, wqpv, wkv, wvvw = wview(w_qn), wview(w_qp), wview(w_k), wview(w_v)
        for h in range(MLA_H):
            ws = h % 2
            hs = slice(h * 128, (h + 1) * 128)
            o1 = p.dma("pool", wqn[ws][:], wqnv[:, :, hs], w=[t_w[ws]], sem=f"hw{ws}")
            keep = list(t_w[ws].w)
            o2 = p.dma("pool", wqp[ws][:], wqpv[:, :, h * 64:(h + 1) * 64], sem=f"hw{ws}")
            o3 = p.dma("pool", wk[ws][:], wkv[:, :, hs], sem=f"hw{ws}")
            o4 = p.dma("pool", wvv[ws][:], wvvw[:, :, hs], sem=f"hw{ws}")
            for o_ in (o2, o3, o4):
                o_.deps |= o1.deps
            t_w[ws].w = keep + [o2.idx, o3.idx, o4.idx]
            for kt in range(S // 512):
                bk = kt % 2
                for kc in range(4):
                    p.mm(p.ps[bk][:], wk[ws][:, kc, :], ckv[:, kc, kt * 512:(kt + 1) * 512], kc == 0, kc == 3,
                         r=[t_c, t_w[ws]], w=[p.tps[bk]])
                p.cp("act" if kt % 2 else "dve", KnT[:, kt * 512:(kt + 1) * 512], p.ps[bk][:], r=[p.tps[bk]], w=[t_Kn])
            for k4 in range(NKB // 4):
                bk = k4 % 2
                for j in range(4):
                    kb = k4 * 4 + j
                    for kc in range(4):
                        p.mm(p.ps[bk][:, j * 128:(j + 1) * 128], ckv[:, kc, kb * 128:(kb + 1) * 128], wvv[ws][:, kc, :],
                             kc == 0, kc == 3, r=[t_c, t_w[ws]], w=[p.tps[bk]])
                p.cp("act" if k4 % 2 else "dve", V[:, k4 * 4:(k4 + 1) * 4, :],
                     p.ps[bk][:].rearrange("p (j v) -> p j v", v=128), r=[p.tps[bk]], w=[t_V])
            for i in range(NT_Q):
                s = nq % 2
                nq += 1
                tsl = slice(i * 512, (i + 1) * 512)
                for kc in range(8):
                    p.mm(p.ps[0][:], wqn[ws][:, kc, :], qlat[:, kc, tsl], kc == 0, kc == 7, r=[t_w[ws]], w=[p.tps[0]])
                p.act(qn[s][:], p.ps[0][:], AF.Copy, r=[p.tps[0]], w=[t_qn[s]], scale=SCALE)
                for kc in range(8):
                    p.mm(p.ps[1][0:64, :], wqp[ws][:, kc, :], qlat[:, kc, tsl], kc == 0, kc == 7, r=[t_w[ws]], w=[p.tps[1]])
                p.ts("dve", qpf[:], p.ps[1][0:64, :], SCALE, None, ALU.mult, r=[p.tps[1]], w=[t_qpf])
                p.cp("pool", qpb[:], qpf[:], r=[t_qpf], w=[t_qpb])
                p.mm(p.ps[1][0:64, :], rm[:], qpb[:], True, True, r=[t_qpb], w=[p.tps[1]])
                p.tt("dve", t1[:], qpf[:], tabs[:, 1, tsl], ALU.mult, r=[t_qpf], w=[t_t1])
                p.tt("dve", t2[:], p.ps[1][0:64, :], tabs[:, 0, tsl], ALU.mult, r=[p.tps[1]], w=[t_t2])
                p.tt("pool", qp[s][:], t1[:], t2[:], ALU.add, r=[t_t1, t_t2], w=[t_qp[s]])
                p.op("pool", lambda e, a_=acc[:]: e.memset(a_, 0.0), w=[t_acc])
                po = 6 + (nq % 2)
                LAG = 2

                def s_step(kb):
                    bk = 2 + kb % 4
                    ks = slice(kb * 128, (kb + 1) * 128)
                    col = i * NKB + kb
                    p.mm(p.ps[bk][:], KnT[:, ks], qn[s][:], True, False, r=[t_Kn, t_qn[s]], w=[p.tps[bk]])
                    p.mm(p.ps[bk][:], kpe[:, ks], qp[s][:], False, True, r=[t_c, t_qp[s]], w=[p.tps[bk]])
                    p.act(pT[kb % 4][:], p.ps[bk][:], AF.Exp, r=[p.tps[bk]], w=[t_pT[kb % 4]])
                    p.op("dve", lambda e, o_=pT[kb % 4][:], d_=Dm[:], c_=thr[:, col:col + 1]: e.scalar_tensor_tensor(
                        out=o_, in0=d_, scalar=c_, in1=o_, op0=ALU.is_ge, op1=ALU.mult), r=[t_pT[kb % 4]], w=[t_pT[kb % 4]])
                    p.tt("pool", acc[:], acc[:], pT[kb % 4][:], ALU.add, r=[t_pT[kb % 4]], w=[t_acc])

                def pv_step(kb):
                    p.mm(p.ps[po][:], V[:, kb, :], pT[kb % 4][:], kb == 0, kb == NKB - 1,
                         r=[t_V, t_pT[kb % 4]], w=[p.tps[po]])

                for n in range(NKB + LAG):
                    if n < NKB:
                        s_step(n)
                    if n - LAG >= 0:
                        pv_step(n - LAG)
                p.mm(p.ps[0][:], cst.ones[:], acc[:], True, True, r=[t_acc, cst.t], w=[p.tps[0]])
                p.op("dve", lambda e, o_=rden[:], i_=p.ps[0][:]: e.reciprocal(out=o_, in_=i_), r=[p.tps[0]], w=[t_rden])
                p.tt("dve", ob[s][:], p.ps[po][:], rden[:], ALU.mult, r=[p.tps[po], t_rden], w=[t_ob[s]])
                o = p.dma("sp", oT_own[hs, tsl], ob[s][:], r=[t_ob[s]], sem=f"ob{s}")
                t_ob[s].r.append(o.idx)


FUSED_IN = None


def build_fused():
    nc = bass.Bass("TRN2", target_bir_lowering=False)
    x = dram_in(nc, "x", [S, D])
    c_pc = dram_in(nc, "c_pc", [128, DC])
    ada_w = dram_in(nc, "ada_w", [2, D, 6 * D])
    ada_b = dram_in(nc, "ada_b", [2, 6 * D])
    ng_pc = dram_in(nc, "ng_pc", [128, 8, DC])
    ng_rows = dram_in(nc, "ng_rows", [8, D])
    lbl_pc = dram_in(nc, "lbl_pc", [128, 2, DC])
    kvin_pc = dram_in(nc, "kvin_pc", [128, DC])
    hng_d = dram_in(nc, "hng", [128, 1])
    kvg_d = dram_in(nc, "kvg", [KVL])
    qng_d = dram_in(nc, "qng", [QL])
    pos_pb = dram_in(nc, "pos_pb", [128, S // 128], I32)
    pos_own = dram_in(nc, "pos_own", [T], I32)
    invf_d = dram_in(nc, "c_invf", [128, 32])
    invf_p_d = dram_in(nc, "invf_p", [64, 1])
    smask_d = dram_in(nc, "c_smask", [128, T2])
    rm_d = dram_in(nc, "c_rm", [64, 64])
    dm_d = dram_in(nc, "c_dm", [128, 512])
    sel_d = dram_in(nc, "sel", [128, NCORE])
    thr_d = dram_in(nc, "thr", [128, (T // 512) * NKB])
    w_in = dram_in(nc, "w_in", [D, 4 * D])
    w_out = dram_in(nc, "w_out", [D, D])
    wg = dram_in(nc, "wg", [2, D, FFN])
    wu = dram_in(nc, "wu", [2, D, FFN])
    wd = dram_in(nc, "wd", [2, FFN, D])
    w_dkv = dram_in(nc, "w_dkv", [D, KVL + 64])
    w_dq = dram_in(nc, "w_dq", [D, QL])
    w_qn = dram_in(nc, "w_qn", [QL, MLA_H * 128])
    w_qp = dram_in(nc, "w_qp", [QL, MLA_H * 64])
    w_k = dram_in(nc, "w_k", [KVL, MLA_H * 128])
    w_v = dram_in(nc, "w_v", [KVL, MLA_H * 128])
    w_o = dram_in(nc, "w_o", [MLA_H * 128, D])
    cd = {k: dram_in(nc, k, v) for k, v in CONST_SPECS.items()}
    out = dram_out(nc, "out", [T, D])
    mod_s = dram_tmp(nc, "mod_s", [2, 6 * D])
    olocT_s = dram_tmp(nc, "olocT_s", [D, T])
    sgT_s = dram_tmp(nc, "sgT_s", [D, T], BF16)
    onT_s = dram_tmp(nc, "onT_s", [D, T], BF16)
    y_s = dram_tmp(nc, "y_s", [T, D])
    x1_s = dram_tmp(nc, "x1_s", [T, D])
    x2_s = dram_tmp(nc, "x2_s", [T, D])
    x2own = dram_tmp(nc, "x2own", [T, D])
    x3_s = dram_tmp(nc, "x3_s", [T, D])
    hid_s = dram_tmp(nc, "hid_s", [FFN, T], BF16)
    ckvT_all = dram_tmp(nc, "ckvT_all", [KVL, S], BF16)
    kpeT_all = dram_tmp(nc, "kpeT_all", [64, S], BF16)
    qlatT_own = dram_tmp(nc, "qlatT_own", [QL, T], BF16)
    oT_own = dram_tmp(nc, "oT_own", [MLA_H * 128, T], BF16)
    p = Prog(nc)
    dummy = []
    with p.scope() as sc0:
        cst = Consts(p, sc0, cd)
        ada_stage(p, cst, ada_w, ada_b, c_pc, mod_s, 6 * D)
        modpc = sc0.sbuf("modpc", [128, 2, 192], F32)
        t_mod = Trk()
        with p.scope() as sc:
            mr = [sc.sbuf("mr", [96, 128], F32) for _ in range(2)]
            t_mr = trks(2)
            n = 0
            for l in range(2):
                mv = mod_s[l].rearrange("(r q) -> r q", q=128)
                for hf in range(2):
                    s = n % 2
                    n += 1
                    p.dma("sp", mr[s][:], mv[hf * 96:(hf + 1) * 96, :], w=[t_mr[s]], sem=f"mr{s}")
                    p.tr(p.ps[s][:, 0:96], mr[s][:], cst.idf[0:96, 0:96], r=[t_mr[s], cst.t], w=[p.tps[s]])
                    p.cp("dve", modpc[:, l, hf * 96:(hf + 1) * 96], p.ps[s][:, 0:96], r=[p.tps[s]], w=[t_mod])
        ngp = sc0.sbuf("ngp", [128, 8, DC], F32)
        lblp = sc0.sbuf("lblp", [128, 2, DC], F32)
        kvin = sc0.sbuf("kvin", [128, DC], F32)
        hng = sc0.sbuf("hng", [128, 1], F32)
        sel = sc0.sbuf("sel", [128, NCORE], F32)
        smask = sc0.sbuf("smask", [128, T2], F32)
        t_vec, t_sm = Trk(), Trk()
        p.dma("sp", ngp[:], ng_pc, w=[t_vec], sem="v0")
        p.dma("sp", lblp[:], lbl_pc, w=[Trk()], sem="v1")
        p.dma("sp", kvin[:], kvin_pc, w=[Trk()], sem="v2")
        p.dma("sp", hng[:], hng_d, w=[Trk()], sem="v3")
        p.dma("sp", sel[:], sel_d, w=[Trk()], sem="v4")
        p.dma("sp", smask[:], smask_d, w=[t_sm], sem="v5")
        p.barrier()

        def affine(l, k, isc, ish, name):
            A = sc0.sbuf(name, [128, DC], F32)
            tA = Trk()
            p.ts("dve", A[:], modpc[:, l, isc * 32:(isc + 1) * 32], 1.0, None, ALU.add, r=[t_mod], w=[tA])
            p.tt("dve", A[:], A[:], ngp[:, l * 4 + k, :], ALU.mult, r=[tA, t_vec], w=[tA])
            return A, modpc[:, l, ish * 32:(ish + 1) * 32], tA

        A_m0, B_m0, t_m0 = affine(0, 0, 1, 0, "Am0")
        A_f0, B_f0, t_f0 = affine(0, 2, 4, 3, "Af0")
        A_m1, B_m1, t_m1 = affine(1, 0, 1, 0, "Am1")
        A_f1, B_f1, t_f1 = affine(1, 2, 4, 3, "Af1")
        lb = sc0.sbuf("lb", [128, DC], F32)
        oml = sc0.sbuf("oml", [128, DC], F32)
        t_lb = Trk()
        p.tt("dve", lb[:], lblp[:, 0, :], lblp[:, 1, :], ALU.subtract, w=[t_lb])
        p.act(lb[:], lb[:], AF.Sigmoid, r=[t_lb], w=[t_lb])
        p.ts("dve", oml[:], lb[:], -1.0, 1.0, ALU.mult, ALU.add, r=[t_lb], w=[t_lb])
        Sall = sc0.sbuf("Sall", [128, HG_H, 128], F32)
        t_S = trks(HG_H)
        Bc = sc0.sbuf("Bc", [128, HG_H], F32)
        t_Bc = trks(HG_H)
        p.op("dve", lambda e: e.memset(Bc[:], 0.0), w=t_Bc)
        p.op("pool", lambda e: e.memset(Sall[:], 0.0), w=t_S)
        p.barrier()
        H = dict(A=A_m0, B=B_m0, t_AB=t_m0, lb=lb, oml=oml, t_lb=t_lb, smask=smask, t_sm=t_sm, Sall=Sall, t_S=t_S, Bc=Bc, t_Bc=t_Bc,
                 wv=wview(w_in), olocT=olocT_s, qhatT=None, sgT=sgT_s)

        def gvec(sc, l, iv, k, name):
            G, t_G = load_rowbc(p, sc, name, mod_s[l, iv * D:(iv + 1) * D], D)
            with p.scope() as scg:
                G2, t_G2 = load_rowbc(p, scg, name + "2", ng_rows[l * 4 + k], D)
                p.tt("dve", G[:], G[:], G2[:], ALU.mult, r=[t_G, t_G2], w=[t_G])
            return G, t_G

        for j in range(NCORE):
            xj = x[j * T:(j + 1) * T, :]
            for tp in range(T // T2):
                hgrn2_pass(p, cst, H, xj[tp * T2:(tp + 1) * T2, :], tp * T2, (j == 0 and tp == 0), dummy)
            gate_stage(p, cst, olocT_s, sgT_s, onT_s, hng, t_vec)
            with p.scope() as sc:
                evac = y_evac_to_dram(p, sc, y_s)
                linear_T_streamed(p, sc, wview(w_out), onT_s.rearrange("(c p) t -> p c t", p=128), DC, cgroups(0, D), evac, name="wo")
            with p.scope() as sc:
                G, t_G = gvec(sc, 0, 2, 1, "Gm")
                resid_pass(p, xj, y_s, G, t_G, x1_s)
            with p.scope() as sc:
                G, t_G = gvec(sc, 0, 5, 3, "Gf")
                ffn_stage(p, cst, x1_s, x2_s, wg[0], wu[0], wd[0], A_f0, B_f0, t_f0, G, t_G, hid_s, y_s)
            p.barrier()
            select_pass(p, x2_s, x2own, sel[:, j:j + 1], t_vec, j == 0)
            latents_stage(p, cst, x2_s, kvin[:], t_vec, w_dkv, kvg_d, pos_pb[:, j * NB:(j + 1) * NB], invf_d,
                          ckvT_all.rearrange("(c p) t -> p c t", p=128)[:, :, j * T:(j + 1) * T], kpeT_all[:, j * T:(j + 1) * T], dummy)
        qlat_stage(p, cst, x2own, A_m1, B_m1, t_m1, w_dq, qng_d, qlatT_own.rearrange("(c p) t -> p c t", p=128), dummy)
        attn_own_stage(p, cst, qlatT_own, ckvT_all, kpeT_all, pos_own, invf_p_d, w_qn, w_qp, w_k, w_v, rm_d, dm_d, thr_d,
                       cd["c_tri"], oT_own)
        with p.scope() as sc:
            evac = y_evac_to_dram(p, sc, y_s)
            linear_T_streamed(p, sc, wview(w_o), oT_own.rearrange("(c p) t -> p c t", p=128), MLA_H, cgroups(0, D), evac, name="wo1")
        with p.scope() as sc:
            G, t_G = gvec(sc, 1, 2, 1, "Gm1")
            resid_pass(p, x2own, y_s, G, t_G, x3_s)
        with p.scope() as sc:
            G, t_G = gvec(sc, 1, 5, 3, "Gf1")
            finals = ffn_stage(p, cst, x3_s, out, wg[1], wu[1], wd[1], A_f1, B_f1, t_f1, G, t_G, hid_s, y_s)
    p.emit(finals)
    return nc


def kernel_fused(x, c, positions, ada_w, ada_b, norm_g, ffn_w_gate, ffn_w_up, ffn_w_down,
                 hg_w_in, hg_lb_logits, hg_norm_g, hg_w_out,
                 mla_w_dq, mla_q_norm_g, mla_w_uq, mla_w_o,
                 kv_norm_in_g, kv_w_dkv, kv_norm_g, kv_w_ukv):
    f32 = np.float32
    ca = np.ascontiguousarray
    ng = np.asarray(norm_g, f32)
    pos = np.asarray(positions, np.int32)[0]
    inv = _inv_freq()
    rm = np.zeros((64, 64), f32)
    for i in range(32):
        rm[i + 32, i] = -1.0
        rm[i, i + 32] = 1.0
    dm = (np.arange(512)[None, :] - np.arange(128)[:, None]).astype(f32)
    smask = np.ones((128, T2), f32)
    smask[:, ::64] = 0
    wuq = np.asarray(mla_w_uq, f32)[0].reshape(QL, MLA_H, 192)
    wukv = np.asarray(kv_w_ukv, f32).reshape(KVL, MLA_H, 256)
    lbl = np.asarray(hg_lb_logits, f32)
    shared = {
        "x": ca(np.asarray(x, f32)[0]), "c_pc": _pc(np.asarray(c, f32)[0]),
        "ada_w": np.asarray(ada_w, f32), "ada_b": np.asarray(ada_b, f32),
        "ng_pc": ca(np.stack([_pc(ng[l, k]) for l in range(2) for k in range(4)], axis=1)),
        "ng_rows": ca(ng.reshape(8, D)), "lbl_pc": ca(np.stack([_pc(lbl[0]), _pc(lbl[1])], axis=1)),
        "kvin_pc": _pc(kv_norm_in_g), "hng": ca(np.asarray(hg_norm_g, f32)[0].reshape(128, 1)),
        "kvg": np.asarray(kv_norm_g, f32), "qng": np.asarray(mla_q_norm_g, f32)[0],
        "pos_pb": ca(pos.reshape(S // 128, 128).T), "c_invf": ca(np.tile(inv[None, :], (128, 1))),
        "invf_p": np.concatenate([inv, inv]).reshape(64, 1).astype(f32), "c_smask": smask, "c_rm": rm, "c_dm": dm,
        "w_in": np.asarray(hg_w_in, f32)[0], "w_out": np.asarray(hg_w_out, f32)[0],
        "wg": np.asarray(ffn_w_gate, f32), "wu": np.asarray(ffn_w_up, f32), "wd": np.asarray(ffn_w_down, f32),
        "w_dkv": np.asarray(kv_w_dkv, f32), "w_dq": np.asarray(mla_w_dq, f32)[0],
        "w_qn": ca(wuq[:, :, :128]).reshape(QL, MLA_H * 128), "w_qp": ca(wuq[:, :, 128:]).reshape(QL, MLA_H * 64),
        "w_k": ca(wukv[:, :, :128]).reshape(KVL, MLA_H * 128), "w_v": ca(wukv[:, :, 128:]).reshape(KVL, MLA_H * 128),
        "w_o": np.asarray(mla_w_o, f32)[0],
    }
    shared.update(const_arrays())
    maps = []
    for ci in range(NCORE):
        m = dict(shared)
        sel = np.zeros((128, NCORE), f32)
        sel[:, ci] = 1.0
        thr = np.zeros((128, (T // 512) * NKB), f32)
        for i in range(T // 512):
            for kb in range(NKB):
                thr[:, i * NKB + kb] = kb * 128 - (ci * T + i * 512)
        m["sel"] = sel
        m["thr"] = thr
        m["pos_own"] = ca(pos[ci * T:(ci + 1) * T])
        maps.append(m)
    r = _run(build_fused(), maps)
    return np.concatenate([o["out"] for o in r], axis=0).reshape(1, S, D).astype(f32)


kernel = kernel_unfused
```

```python
import contextlib
import numpy as np
import ml_dtypes
import concourse.bass as bass
import concourse.mybir as mybir
from concourse.bass_utils import run_bass_kernel_spmd

F32 = mybir.dt.float32
BF16 = mybir.dt.bfloat16
I32 = mybir.dt.int32
AF = mybir.ActivationFunctionType
ALU = mybir.AluOpType
AX = mybir.AxisListType

NCORE = 8
D = 4096
S = 8192
T = S // NCORE
NB = T // 128
DC = D // 128
HG_H = 32
FFN = 11008
FC = FFN // 128
MLA_H = 64
HPC = MLA_H // NCORE
QL = 1024
KVL = 512
EPS = 1e-6
SCALE = 1.0 / float(np.sqrt(192.0))

ENGS = ("pe", "act", "dve", "pool", "sp")


class Trk:
    __slots__ = ("w", "r")

    def __init__(self):
        self.w = []
        self.r = []


def trks(n):
    return [Trk() for _ in range(n)]


class Op:
    __slots__ = ("eng", "fn", "deps", "dma_sem", "waited_on", "val", "idx", "key")


class Scope:
    def __init__(self, p):
        self.p = p
        self.st = contextlib.ExitStack()

    def sbuf(self, name, shape, dt):
        self.p.nalloc += 1
        return self.st.enter_context(self.p.nc.sbuf_tensor(f"{name}_{self.p.nalloc}", list(shape), dt))

    def __enter__(self):
        return self

    def __exit__(self, *a):
        self.p.barrier()
        self.st.close()
        return False


class Prog:
    def __init__(self, nc):
        self.nc = nc
        self.ops = []
        self.stack = contextlib.ExitStack()
        self.last = {}
        self.pending = {e: set() for e in ENGS}
        self.nalloc = 0
        self.ps = [self.stack.enter_context(nc.psum_tensor(f"psb{i}", [128, 512], F32)) for i in range(8)]
        self.tps = trks(8)
        self.ndma = 0
        self.sem_map = {}
        self.kcount = {}
        self.epoch = {}

    def scope(self):
        return Scope(self)

    def barrier(self):
        snap = set(self.last.values())
        for e in ENGS:
            self.pending[e] |= snap
        self.sem_map = {}
        for base, n in self.kcount.items():
            if n > 24000:
                self.epoch[base] = self.epoch.get(base, 0) + 1
                self.kcount[base] = 0

    def op(self, eng, fn, r=(), w=(), dma_sem=None):
        if dma_sem is not None:
            if dma_sem not in self.sem_map:
                self.sem_map[dma_sem] = f"ds{len(self.sem_map)}"
            dma_sem = self.sem_map[dma_sem]
        base = dma_sem if dma_sem is not None else "e_" + eng
        self.kcount[base] = self.kcount.get(base, 0) + (16 if dma_sem is not None else 1)
        key = f"{base}#{self.epoch.get(base, 0)}"
        o = Op()
        o.eng, o.fn, o.dma_sem = eng, fn, dma_sem
        o.key = key
        o.waited_on = False
        o.val = None
        o.idx = len(self.ops)
        deps = set(self.pending[eng])
        self.pending[eng] = set()
        for t in r:
            deps.update(t.w)
        for t in w:
            deps.update(t.w)
            deps.update(t.r)
        o.deps = deps
        self.ops.append(o)
        for t in r:
            t.r.append(o.idx)
        for t in w:
            t.w = [o.idx]
            t.r = []
        self.last[key] = o.idx
        return o

    def dma(self, eng, out, in_, r=(), w=(), sem=None, **kw):
        if sem is None:
            sem = f"dq{self.ndma % 6}_{eng}"
            self.ndma += 1
        return self.op(eng, lambda e: e.dma_start(out=out, in_=in_, **kw), r, w, dma_sem=sem)

    def mm(self, out, lhsT, rhs, start, stop, r=(), w=()):
        return self.op("pe", lambda e: e.matmul(out, lhsT=lhsT, rhs=rhs, start=start, stop=stop), r, w)

    def tr(self, out, in_, ident, r=(), w=()):
        return self.op("pe", lambda e: e.transpose(out, in_, ident), r, w)

    def act(self, out, in_, func, r=(), w=(), **kw):
        return self.op("act", lambda e: e.activation(out=out, in_=in_, func=func, **kw), r, w)

    def tt(self, eng, out, a, b, op, r=(), w=()):
        return self.op(eng, lambda e: e.tensor_tensor(out=out, in0=a, in1=b, op=op), r, w)

    def ts(self, eng, out, a, s1, s2, op0, op1=None, r=(), w=()):
        if op1 is None:
            return self.op(eng, lambda e: e.tensor_scalar(out=out, in0=a, scalar1=s1, scalar2=None, op0=op0), r, w)
        return self.op(eng, lambda e: e.tensor_scalar(out=out, in0=a, scalar1=s1, scalar2=s2, op0=op0, op1=op1), r, w)

    def cp(self, eng, out, in_, r=(), w=()):
        if eng == "act":
            return self.op("act", lambda e: e.copy(out=out, in_=in_), r, w)
        return self.op(eng, lambda e: e.tensor_copy(out=out, in_=in_), r, w)

    def emit(self, final_ops=()):
        nc = self.nc
        ops = self.ops
        for o in ops:
            nd = set()
            best = {}
            for d in o.deps:
                pr = ops[d]
                if pr.eng == "pe" and o.eng == "pe" and pr.dma_sem is None and o.dma_sem is None:
                    continue
                if pr.dma_sem is None:
                    if best.get(pr.key, -1) < d:
                        best[pr.key] = d
                else:
                    nd.add(d)
            nd.update(best.values())
            for d in nd:
                ops[d].waited_on = True
            o.deps = nd
        for d in final_ops:
            d.waited_on = True
        for o in ops:
            if o.dma_sem is not None:
                o.waited_on = True
        cnt = {}
        for o in ops:
            if not o.waited_on:
                continue
            key = o.key
            inc = 16 if o.dma_sem is not None else 1
            cnt[key] = cnt.get(key, 0) + inc
            o.val = (key, cnt[key], inc)
        self.sem_max = dict(cnt)
        sems = {k: self.stack.enter_context(nc.semaphore(k)) for k in cnt}
        per = {e: [] for e in ENGS}
        for o in ops:
            per[o.eng].append(o)
        final = {}
        for d in final_ops:
            k, v, _ = d.val
            final[k] = max(final.get(k, 0), v)

        def run(engname, eobj):
            waited = {}
            for o in per[engname]:
                need = {}
                for d in o.deps:
                    k, v, _ = ops[d].val
                    if waited.get(k, 0) >= v:
                        continue
                    need[k] = max(need.get(k, 0), v)
                for k, v in need.items():
                    eobj.wait_ge(sems[k], v)
                    waited[k] = v
                ins = o.fn(eobj)
                if o.val is not None:
                    ins.then_inc(sems[o.val[0]], o.val[2])
            if engname == "sp":
                for k, v in final.items():
                    if waited.get(k, 0) < v:
                        eobj.wait_ge(sems[k], v)

        with nc.Block() as block:
            @block.tensor
            def _(e):
                run("pe", e)

            @block.scalar
            def _(e):
                run("act", e)

            @block.vector
            def _(e):
                run("dve", e)

            @block.gpsimd
            def _(e):
                run("pool", e)

            @block.sync
            def _(e):
                run("sp", e)
        self.stats = {e: len(per[e]) for e in ENGS}
        self.stack.close()


class Consts:
    def __init__(self, p, sc, cdram):
        self.t = Trk()
        self.idf = sc.sbuf("idf", [128, 128], F32)
        self.idb = sc.sbuf("idb", [128, 128], BF16)
        self.ones = sc.sbuf("ones", [128, 128], F32)
        self.tri = sc.sbuf("tri", [128, 128], F32)
        p.dma("sp", self.idf[:], cdram["c_idf"], w=[self.t], sem="cst0")
        p.dma("sp", self.ones[:], cdram["c_ones"], w=[Trk()], sem="cst0")
        p.dma("sp", self.tri[:], cdram["c_tri"], w=[Trk()], sem="cst0")
        o = p.dma("pool", self.idb[:], cdram["c_idf"], w=[Trk()], sem="cst1")
        p.barrier()


def load_rowbc(p, sc, name, dram_row_ap, n, eng="sp"):
    t = sc.sbuf(name, [128, n], F32)
    tk = Trk()
    p.dma(eng, t[:], dram_row_ap.partition_broadcast(128), w=[tk])
    return t, tk


def rms_rstd(p, ss, rs, tk_ss, tk_rs, n):
    p.ts("dve", rs, ss, 1.0 / n, EPS, ALU.mult, ALU.add, r=[tk_ss], w=[tk_rs])
    p.act(rs, rs, AF.Sqrt, r=[tk_rs], w=[tk_rs])
    p.op("dve", lambda e: e.reciprocal(out=rs, in_=rs), r=[tk_rs], w=[tk_rs])


def norm_to_featmajor(p, cst, x_dram, hT, t_hT, A, B, t_AB, nblk=NB, ps_banks=(0, 1), nslots=3):
    with p.scope() as sc:
        xt = [sc.sbuf("xt", [128, D], F32) for _ in range(nslots)]
        junk = sc.sbuf("junk", [128, D], BF16)
        ss = [sc.sbuf("ss", [128, 1], F32) for _ in range(nslots)]
        rs = [sc.sbuf("rs", [128, 1], F32) for _ in range(nslots)]
        t_x, t_ss, t_rs = trks(nslots), trks(nslots), trks(nslots)
        t_j = Trk()
        for b in range(nblk):
            s = b % nslots
            p.dma("sp", xt[s][:], x_dram[b * 128:(b + 1) * 128, :], w=[t_x[s]], sem=f"nx{s}")
            p.act(junk[:], xt[s][:], AF.Square, r=[t_x[s]], w=[t_j, t_ss[s]], accum_out=ss[s][:])
            rms_rstd(p, ss[s][:], rs[s][:], t_ss[s], t_rs[s], D)
            p.ts("dve", xt[s][:], xt[s][:], rs[s][:, 0:1], None, ALU.mult, r=[t_x[s], t_rs[s]], w=[t_x[s]])
            for c4 in range(DC // 4):
                bk = ps_banks[c4 % 2]
                for j in range(4):
                    c = c4 * 4 + j
                    p.tr(p.ps[bk][:, j * 128:(j + 1) * 128], xt[s][:, c * 128:(c + 1) * 128], cst.idf[:],
                         r=[t_x[s], cst.t], w=[p.tps[bk]])
                for j in range(4):
                    c = c4 * 4 + j
                    o_ap = hT[:, c, b * 128:(b + 1) * 128]
                    i_ap = p.ps[bk][:, j * 128:(j + 1) * 128]
                    if j % 2 == 0:
                        if B is None:
                            p.act(o_ap, i_ap, AF.Copy, r=[p.tps[bk], t_AB], w=[t_hT], scale=A[:, c:c + 1])
                        else:
                            p.act(o_ap, i_ap, AF.Identity, r=[p.tps[bk], t_AB], w=[t_hT],
                                  scale=A[:, c:c + 1], bias=B[:, c:c + 1])
                    else:
                        if B is None:
                            p.ts("dve", o_ap, i_ap, A[:, c:c + 1], None, ALU.mult, r=[p.tps[bk], t_AB], w=[t_hT])
                        else:
                            p.ts("dve", o_ap, i_ap, A[:, c:c + 1], B[:, c:c + 1], ALU.mult, ALU.add,
                                 r=[p.tps[bk], t_AB], w=[t_hT])


class WRing:
    def __init__(self, p, sc, nslots=4, kc=16, name="wr", dt=BF16):
        self.p = p
        self.kc = kc
        self.tiles = [sc.sbuf(name, [128, kc, 512], dt) for _ in range(nslots)]
        self.trk = trks(nslots)
        self.i = 0
        self.name = name
        self.dt = dt

    def load(self, wv, k0, kc, c0, nc_):
        s = self.i % len(self.tiles)
        self.i += 1
        eng = "pool" if self.dt == BF16 else "sp"
        self.p.dma(eng, self.tiles[s][:, 0:kc, 0:nc_], wv[:, k0:k0 + kc, c0:c0 + nc_],
                   w=[self.trk[s]], sem=f"{self.name}{s}")
        return self.tiles[s], self.trk[s]


def kunits(KC, ku=16):
    return [(k0, min(ku, KC - k0)) for k0 in range(0, KC, ku)]


def linear(p, ring, wv, KC, col_groups, act_fn, mode, evac, ntok=T, pf=2):
    units = []
    for gi, (c0, ncol) in enumerate(col_groups):
        for (k0, kc) in kunits(KC, ring.kc):
            units.append((gi, c0, ncol, k0, kc))
    loaded = {}
    ntt = ntok // 512
    nblk = ntok // 128
    for i in range(-pf, len(units)):
        j = i + pf
        if j < len(units):
            gi, c0, ncol, k0, kc = units[j]
            loaded[j] = ring.load(wv, k0, kc, c0, ncol)
        if i < 0:
            continue
        gi, c0, ncol, k0, kc = units[i]
        wt, tw = loaded.pop(i)
        first = (k0 == 0)
        last = (k0 + kc == KC)
        if mode == "F":
            nch = (ncol + 127) // 128
            for c in range(nch):
                m = min(128, ncol - c * 128)
                for tt in range(ntt):
                    bk = c * ntt + tt
                    for k in range(kc):
                        a_ap, ta = act_fn(k0 + k)
                        p.mm(p.ps[bk][0:m, :], wt[:, k, c * 128:c * 128 + m], a_ap[:, tt * 512:(tt + 1) * 512],
                             start=(first and k == 0), stop=(last and k == kc - 1), r=[tw, ta], w=[p.tps[bk]])
        else:
            for b in range(nblk):
                for k in range(kc):
                    a_ap, ta = act_fn(k0 + k)
                    p.mm(p.ps[b][:, 0:ncol], a_ap[:, b * 128:(b + 1) * 128], wt[:, k, 0:ncol],
                         start=(first and k == 0), stop=(last and k == kc - 1), r=[tw, ta], w=[p.tps[b]])
        if last:
            evac(gi, c0, ncol)


def cgroups(n0, n, step=512):
    return [(c, min(step, n0 + n - c)) for c in range(n0, n0 + n, step)]


def wview(w_ap):
    return w_ap.rearrange("(c p) n -> p c n", p=128)


def resid_pass(p, x_in, y_dram, G, t_G, x_out, nblk=NB):
    with p.scope() as sc:
        xt = [sc.sbuf("rx", [128, D], F32) for _ in range(4)]
        yt = [sc.sbuf("ry", [128, D], F32) for _ in range(4)]
        junk = sc.sbuf("rjunk", [128, D], BF16)
        ss = [sc.sbuf("rss", [128, 1], F32) for _ in range(4)]
        rs = [sc.sbuf("rrs", [128, 1], F32) for _ in range(4)]
        t_x, t_y, t_ss, t_rs = trks(4), trks(4), trks(4), trks(4)
        t_j = Trk()
        outs = []
        for b in range(nblk):
            s = b % 4
            p.dma("sp", xt[s][:], x_in[b * 128:(b + 1) * 128, :], w=[t_x[s]], sem=f"rpx{s}")
            p.dma("sp", yt[s][:], y_dram[b * 128:(b + 1) * 128, :], w=[t_y[s]], sem=f"rpy{s}")
            p.act(junk[:], yt[s][:], AF.Square, r=[t_y[s]], w=[t_j, t_ss[s]], accum_out=ss[s][:])
            rms_rstd(p, ss[s][:], rs[s][:], t_ss[s], t_rs[s], D)
            p.op("dve", lambda e, o_=yt[s][:], sc_=rs[s][:, 0:1], g_=G[:]: e.scalar_tensor_tensor(out=o_, in0=o_, scalar=sc_, in1=g_,
                                                               op0=ALU.mult, op1=ALU.mult),
                 r=[t_y[s], t_rs[s], t_G], w=[t_y[s]])
            p.tt("pool", xt[s][:], xt[s][:], yt[s][:], ALU.add, r=[t_x[s], t_y[s]], w=[t_x[s]])
            o = p.dma("sp", x_out[b * 128:(b + 1) * 128, :], xt[s][:], r=[t_x[s]], w=[], sem=f"rpo{s}")
            t_x[s].r.append(o.idx)
            outs.append(o)
        return outs


def y_evac_to_dram(p, sc, y_dram, name="yst"):
    st = [sc.sbuf(name, [128, 512], F32) for _ in range(4)]
    t_st = trks(4)
    cnt = [0]

    def evac(gi, c0, ncol):
        for b in range(NB):
            s = cnt[0] % 4
            cnt[0] += 1
            if b % 2 == 0:
                p.cp("dve", st[s][:, 0:ncol], p.ps[b][:, 0:ncol], r=[p.tps[b]], w=[t_st[s]])
            else:
                p.cp("act", st[s][:, 0:ncol], p.ps[b][:, 0:ncol], r=[p.tps[b]], w=[t_st[s]])
            o = p.dma("sp", y_dram[b * 128:(b + 1) * 128, c0:c0 + ncol], st[s][:, 0:ncol], r=[t_st[s]], sem=f"{name}{s}")
            t_st[s].r.append(o.idx)
    return evac


def ffn_stage(p, cst, x_in, x_out, wg, wu, wd, A, B, t_AB, G, t_G, hid_dram, y_dram):
    with p.scope() as sc:
        hT = sc.sbuf("hT", [128, DC, T], BF16)
        t_hT = Trk()
        norm_to_featmajor(p, cst, x_in, hT, t_hT, A, B, t_AB)
        with p.scope() as sc2:
            ring = WRing(p, sc2, nslots=4, kc=16, name="wf")
            gt = sc2.sbuf("gt", [128, 8, 512], BF16)
            t_gt = trks(8)
            hs = [sc2.sbuf("hs", [128, 512], BF16) for _ in range(4)]
            t_hs = trks(4)
            cnt = [0]
            hv = hid_dram.rearrange("(c p) t -> p c t", p=128)

            def act_fn(k):
                return hT[:, k, :], t_hT

            def evac_g(gi, c0, ncol):
                for c in range((ncol + 127) // 128):
                    for tt in range(2):
                        bk = c * 2 + tt
                        p.act(gt[:, bk, :], p.ps[bk][:], AF.Silu, r=[p.tps[bk]], w=[t_gt[bk]])

            def evac_u(gi, c0, ncol):
                for c in range((ncol + 127) // 128):
                    for tt in range(2):
                        bk = c * 2 + tt
                        s = cnt[0] % 4
                        cnt[0] += 1
                        p.tt("dve", hs[s][:], p.ps[bk][:], gt[:, bk, :], ALU.mult, r=[p.tps[bk], t_gt[bk]], w=[t_hs[s]])
                        o = p.dma("sp", hv[:, c0 // 128 + c, tt * 512:(tt + 1) * 512], hs[s][:], r=[t_hs[s]], sem=f"hs{s}")
                        t_hs[s].r.append(o.idx)

            wgv, wuv = wview(wg), wview(wu)
            for (c0, ncol) in cgroups(0, FFN):
                linear(p, ring, wgv, DC, [(c0, ncol)], act_fn, "F", evac_g)
                linear(p, ring, wuv, DC, [(c0, ncol)], act_fn, "F", evac_u)
    with p.scope() as sc:
        ring = WRing(p, sc, nslots=3, kc=16, name="wd")
        hring = [sc.sbuf("hr", [128, 16, T], BF16) for _ in range(3)]
        t_hr = trks(3)
        hv = hid_dram.rearrange("(c p) t -> p c t", p=128)
        kus = kunits(FC, 16)
        evac = y_evac_to_dram(p, sc, y_dram)
        wdv = wview(wd)
        units = []
        for gi, (c0, ncol) in enumerate(cgroups(0, D)):
            for (k0, kc) in kus:
                units.append((gi, c0, ncol, k0, kc))
        loaded = {}
        hi = [0]

        def load(j):
            gi, c0, ncol, k0, kc = units[j]
            wt, tw = ring.load(wdv, k0, kc, c0, ncol)
            s = hi[0] % 3
            hi[0] += 1
            p.dma("sp", hring[s][:, 0:kc, :], hv[:, k0:k0 + kc, :], w=[t_hr[s]], sem=f"hr{s}")
            return wt, tw, hring[s], t_hr[s]

        pf = 2
        for i in range(-pf, len(units)):
            if i + pf < len(units):
                loaded[i + pf] = load(i + pf)
            if i < 0:
                continue
            gi, c0, ncol, k0, kc = units[i]
            wt, tw, ht, th = loaded.pop(i)
            for b in range(NB):
                for k in range(kc):
                    p.mm(p.ps[b][:, 0:ncol], ht[:, k, b * 128:(b + 1) * 128], wt[:, k, 0:ncol],
                         start=(k0 == 0 and k == 0), stop=(k0 + kc == FC and k == kc - 1), r=[tw, th], w=[p.tps[b]])
            if k0 + kc == FC:
                evac(gi, c0, ncol)
    return resid_pass(p, x_in, y_dram, G, t_G, x_out)


def dram_in(nc, name, shape, dt=F32):
    return nc.dram_tensor(name, list(shape), dt, kind="ExternalInput").ap()


def dram_out(nc, name, shape, dt=F32):
    return nc.dram_tensor(name, list(shape), dt, kind="ExternalOutput").ap()


def dram_tmp(nc, name, shape, dt=F32):
    return nc.dram_tensor(name, list(shape), dt, kind="Internal").ap()


CONST_SPECS = {"c_idf": [128, 128], "c_ones": [128, 128], "c_tri": [128, 128]}


def const_arrays():
    tri = (np.arange(128)[:, None] <= np.arange(128)[None, :]).astype(np.float32)
    return {"c_idf": np.eye(128, dtype=np.float32), "c_ones": np.ones((128, 128), np.float32), "c_tri": tri}


def build_ada():
    NCOL = 6 * D // NCORE
    nc = bass.Bass("TRN2", target_bir_lowering=False)
    w = dram_in(nc, "w", [2, D, NCOL])
    bias = dram_in(nc, "b", [2, NCOL])
    c_pc = dram_in(nc, "c_pc", [128, DC])
    out = dram_out(nc, "mod", [2, NCOL])
    p = Prog(nc)
    finals = []
    with p.scope() as sc:
        cs = sc.sbuf("cs", [128, DC], F32)
        t_cs = Trk()
        p.dma("sp", cs[:], c_pc, w=[t_cs])
        p.act(cs[:], cs[:], AF.Silu, r=[t_cs], w=[t_cs])
        bt = sc.sbuf("bt", [1, 2, NCOL], F32)
        t_bt = Trk()
        p.dma("sp", bt[:], bias.rearrange("(o l) n -> o l n", o=1), w=[t_bt])
        ring = WRing(p, sc, nslots=3, kc=16, name="wa", dt=F32)
        ot = [sc.sbuf("ot", [1, 512], F32) for _ in range(2)]
        t_ot = trks(2)
        units = []
        for l in range(2):
            wv = wview(w[l])
            for (c0, ncol) in cgroups(0, NCOL):
                for (k0, kc) in kunits(DC, 16):
                    units.append((l, wv, c0, ncol, k0, kc))
        loaded = {}
        pf = 2
        n = 0
        for i in range(-pf, len(units)):
            if i + pf < len(units):
                l, wv, c0, ncol, k0, kc = units[i + pf]
                loaded[i + pf] = ring.load(wv, k0, kc, c0, ncol)
            if i < 0:
                continue
            l, wv, c0, ncol, k0, kc = units[i]
            wt, tw = loaded.pop(i)
            bk = (n // 2) % 2
            for k in range(kc):
                p.mm(p.ps[bk][0:1, 0:ncol], cs[:, k0 + k:k0 + k + 1], wt[:, k, 0:ncol],
                     start=(k0 == 0 and k == 0), stop=(k0 + kc == DC and k == kc - 1), r=[tw, t_cs], w=[p.tps[bk]])
            n += 1
            if k0 + kc == DC:
                s = bk
                p.tt("dve", ot[s][:, 0:ncol], p.ps[bk][0:1, 0:ncol], bt[:, l, c0:c0 + ncol], ALU.add,
                     r=[p.tps[bk], t_bt], w=[t_ot[s]])
                o = p.dma("sp", out[l:l + 1, c0:c0 + ncol], ot[s][:, 0:ncol], r=[t_ot[s]], sem=f"ao{s}")
                t_ot[s].r.append(o.idx)
                finals.append(o)
    p.emit(finals)
    return nc


T2 = 512
NCH = T2 // 64


def prep_affine(p, sc, vec, t_vec, ig, isc, ish, name):
    A = sc.sbuf(name, [128, DC], F32)
    tA = Trk()
    p.ts("dve", A[:], vec[:, isc, :], 1.0, None, ALU.add, r=[t_vec], w=[tA])
    p.tt("dve", A[:], A[:], vec[:, ig, :], ALU.mult, r=[tA, t_vec], w=[tA])
    return A, vec[:, ish, :], tA


def hgrn2_pass(p, cst, H, x_rows, tok0, first_pass, finals):
    A, B, t_AB = H["A"], H["B"], H["t_AB"]
    lb, oml, t_lb = H["lb"], H["oml"], H["t_lb"]
    smask, t_sm = H["smask"], H["t_sm"]
    Sall, t_S, Bc, t_Bc = H["Sall"], H["t_S"], H["Bc"], H["t_Bc"]
    wv, olocT, qhatT, sgT = H["wv"], H["olocT"], H["qhatT"], H["sgT"]
    with p.scope() as sc:
        hT = sc.sbuf("hT", [128, DC, T2], BF16)
        t_hT = Trk()
        norm_to_featmajor(p, cst, x_rows, hT, t_hT, A, B, t_AB, nblk=T2 // 128, ps_banks=(0, 1), nslots=2)
        ring = WRing(p, sc, nslots=5, kc=8, name="wi")
        qt = sc.sbuf("qt", [128, 4, T2], BF16)
        fl = sc.sbuf("fl", [128, 4, T2], F32)
        kt = sc.sbuf("kt", [128, 4, T2], BF16)
        kh = sc.sbuf("kh", [64, NCH, 4, 128], BF16)
        vt = sc.sbuf("vt", [64, NCH, 512], BF16)
        vb = [sc.sbuf("vb", [128, 512], BF16) for _ in range(2)]
        t_vb = trks(2)
        gs = [sc.sbuf("gs", [128, T2], BF16) for _ in range(2)]
        ot = [sc.sbuf("ot", [128, T2], F32) for _ in range(4)]
        lf = sc.sbuf("lf", [128, T2], F32)
        bt = sc.sbuf("bt", [128, T2], F32)
        eb = sc.sbuf("eb", [128, T2], F32)
        khT = sc.sbuf("khT", [128, T2], BF16)
        qh = [sc.sbuf("qh", [128, T2], BF16) for _ in range(2)]
        bl = sc.sbuf("bl", [128, NCH], F32)
        ebl = sc.sbuf("ebl", [128, 4, NCH], F32)
        binc = sc.sbuf("binc", [128, NCH], F32)
        EB = sc.sbuf("EB", [128, NCH], F32)
        Sb = sc.sbuf("Sb", [128, 4, 128], BF16)
        at = [sc.sbuf("at", [64, 64], BF16) for _ in range(4)]
        t_qt, t_fl, t_kt, t_kh, t_ot, t_Sb, t_at, t_ebl = trks(4), trks(4), trks(4), trks(4), trks(4), trks(4), trks(4), trks(4)
        t_vt, t_lf, t_bt, t_eb, t_khT, t_bl, t_binc, t_EB = Trk(), Trk(), Trk(), Trk(), Trk(), Trk(), Trk(), Trk()
        t_gs, t_qh = trks(2), trks(2)
        t_pA, t_pO, t_pS = trks(4), trks(4), trks(4)
        psT = p.ps[7][:].bitcast(BF16)
        gcnt = [0]

        def act_fn(k):
            return hT[:, k, :], t_hT

        for hgp in range(HG_H // 4):
            c0 = hgp * 512

            def evac_q(gi, cc0, ncol):
                for c in range(4):
                    p.act(qt[:, c, :], p.ps[c][:], AF.Silu, r=[p.tps[c]], w=[t_qt[c]])

            def evac_f(gi, cc0, ncol):
                for c in range(4):
                    h = hgp * 4 + c
                    p.act(fl[:, c, :], p.ps[c][:], AF.Sigmoid, r=[p.tps[c]], w=[t_fl[c]])
                    p.ts("dve", fl[:, c, :], fl[:, c, :], oml[:, h:h + 1], lb[:, h:h + 1], ALU.mult, ALU.add,
                         r=[t_fl[c], t_lb], w=[t_fl[c]])

            def evac_g(gi, cc0, ncol):
                for c in range(4):
                    h = hgp * 4 + c
                    s = gcnt[0] % 2
                    gcnt[0] += 1
                    p.act(gs[s][:], p.ps[c][:], AF.Silu, r=[p.tps[c]], w=[t_gs[s]])
                    o = p.dma("sp", sgT[h * 128:(h + 1) * 128, tok0:tok0 + T2], gs[s][:], r=[t_gs[s]], sem=f"gs{s}")
                    t_gs[s].r.append(o.idx)
                    finals.append(o)

            linear(p, ring, wv, DC, [(c0, 512)], act_fn, "F", evac_q, ntok=T2)
            linear(p, ring, wv, DC, [(D + c0, 512)], act_fn, "F", evac_f, ntok=T2)
            linear(p, ring, wv, DC, [(3 * D + c0, 512)], act_fn, "F", evac_g, ntok=T2)
            wts = [ring.load(wv, k0, 8, 2 * D + c0, 512) for k0 in range(0, DC, 8)]
            for j in range(NCH):
                bk = j % 2
                for k in range(DC):
                    wt, tw = wts[k // 8]
                    p.mm(p.ps[bk][0:64, :], hT[:, k, j * 64:(j + 1) * 64], wt[:, k % 8, :],
                         start=(k == 0), stop=(k == DC - 1), r=[tw, t_hT], w=[p.tps[bk]])
                if j % 2 == 0:
                    p.cp("act", vt[:, j, :], p.ps[bk][0:64, :], r=[p.tps[bk]], w=[t_vt])
                else:
                    p.cp("dve", vt[:, j, :], p.ps[bk][0:64, :], r=[p.tps[bk]], w=[t_vt])
            for c in range(4):
                h = hgp * 4 + c
                p.act(lf[:], fl[:, c, :], AF.Ln, r=[t_fl[c]], w=[t_lf])
                p.op("dve", lambda e, o_=bt[:], d0=smask[:], d1=lf[:]: e.tensor_tensor_scan(out=o_, data0=d0, data1=d1, initial=0.0,
                                                            op0=ALU.mult, op1=ALU.add),
                     r=[t_lf, t_sm], w=[t_bt])
                p.ts("pool", fl[:, c, :], fl[:, c, :], -1.0, 1.0, ALU.mult, ALU.add, r=[t_fl[c]], w=[t_fl[c]])
                p.act(eb[:], bt[:], AF.Exp, r=[t_bt], w=[t_eb])
                p.tt("dve", qt[:, c, :], qt[:, c, :], eb[:], ALU.mult, r=[t_qt[c], t_eb], w=[t_qt[c]])
                p.act(lf[:], bt[:], AF.Exp, r=[t_bt], w=[t_lf], scale=-1.0)
                p.tt("dve", fl[:, c, :], fl[:, c, :], lf[:], ALU.mult, r=[t_fl[c], t_lf], w=[t_fl[c]])
                p.cp("pool", kt[:, c, :], fl[:, c, :], r=[t_fl[c]], w=[t_kt[c]])
                btv = bt[:].rearrange("p (j t) -> p j t", t=64)
                p.cp("dve", bl[:], btv[:, :, 63], r=[t_bt], w=[t_bl])
                p.act(ebl[:, c, :], bl[:], AF.Exp, r=[t_bl], w=[t_ebl[c]])
                p.tt("dve", khT[:].rearrange("p (j t) -> p j t", t=64), fl[:, c, :].rearrange("p (j t) -> p j t", t=64),
                     ebl[:, c, :].unsqueeze(2).broadcast_to([128, NCH, 64]), ALU.mult,
                     r=[t_fl[c], t_ebl[c]], w=[t_khT])
                p.op("dve", lambda e, o_=binc[:], d0=cst.ones[:, 0:NCH], d1=bl[:], ini=Bc[:, h:h + 1]: e.tensor_tensor_scan(
                    out=o_, data0=d0, data1=d1, initial=ini, op0=ALU.mult, op1=ALU.add),
                     r=[t_bl, t_Bc[h], cst.t], w=[t_binc])
                p.tt("dve", EB[:], binc[:], bl[:], ALU.subtract, r=[t_binc, t_bl], w=[t_EB])
                p.act(EB[:], EB[:], AF.Exp, r=[t_EB], w=[t_EB])
                p.cp("dve", Bc[:, h:h + 1], binc[:, NCH - 1:NCH], r=[t_binc], w=[t_Bc[h]])
                s = h % 2
                if qhatT is not None:
                    p.tt("pool", qh[s][:].rearrange("p (j t) -> p j t", t=64), qt[:, c, :].rearrange("p (j t) -> p j t", t=64),
                         EB[:].unsqueeze(2).broadcast_to([128, NCH, 64]), ALU.mult, r=[t_qt[c], t_EB], w=[t_qh[s]])
                    o = p.dma("sp", qhatT[h * 128:(h + 1) * 128, tok0:tok0 + T2], qh[s][:], r=[t_qh[s]], sem=f"qh{s}")
                    t_qh[s].r.append(o.idx)
                    finals.append(o)
                for j in range(NCH):
                    p.tr(psT[0:64, j * 128:(j + 1) * 128], khT[:, j * 64:(j + 1) * 64], cst.idb[:],
                         r=[t_khT, cst.t], w=[p.tps[7]])
                p.cp("act", kh[:, :, c, :], psT[0:64, 0:NCH * 128].rearrange("p (j d) -> p j d", d=128),
                     r=[p.tps[7]], w=[t_kh[c]])
                p.cp("pool", Sb[:, c, :], Sall[:, h, :], r=[t_S[h]], w=[t_Sb[c]])
            for j in range(NCH):
                first_chunk = (first_pass and j == 0)
                for c in range(4):
                    h = hgp * 4 + c
                    tsl = slice(j * 64, (j + 1) * 64)
                    pA = p.ps[4][0:64, c * 64:(c + 1) * 64]
                    pO = p.ps[5][:, c * 64:(c + 1) * 64]
                    pS = p.ps[6][:, c * 128:(c + 1) * 128]
                    p.mm(pA, kt[:, c, tsl], qt[:, c, tsl], True, True, r=[t_kt[c], t_qt[c], p.tps[4]], w=[t_pA[c]])
                    p.tt("dve", at[c][:], pA, cst.tri[0:64, 0:64], ALU.mult, r=[t_pA[c], cst.t], w=[t_at[c]])
                    if not first_chunk:
                        p.mm(pO, Sb[:, c, :], qt[:, c, tsl], True, False, r=[t_Sb[c], t_qt[c], p.tps[5]], w=[t_pO[c]])
                    p.mm(pO, vt[:, j, c * 128:(c + 1) * 128], at[c][:], first_chunk, True,
                         r=[t_vt, t_at[c], p.tps[5]], w=[t_pO[c]])
                    p.cp("act", ot[c][:, tsl], pO, r=[t_pO[c]], w=[t_ot[c]])
                    p.mm(pS, kh[:, j, c, :], vt[:, j, c * 128:(c + 1) * 128], True, True,
                         r=[t_kh[c], t_vt, p.tps[6]], w=[t_pS[c]])
                    if first_chunk:
                        p.cp("dve", Sall[:, h, :], pS, r=[t_pS[c]], w=[t_S[h]])
                    else:
                        p.op("dve", lambda e, o_=Sall[:, h, :], sc_=ebl[:, c, j:j + 1], pS=pS: e.scalar_tensor_tensor(
                            out=o_, in0=o_, scalar=sc_, in1=pS,
                            op0=ALU.mult, op1=ALU.add), r=[t_S[h], t_ebl[c], t_pS[c]], w=[t_S[h]])
                    if j < NCH - 1:
                        p.cp("pool", Sb[:, c, :], Sall[:, h, :], r=[t_S[h]], w=[t_Sb[c]])
            for c in range(4):
                h = hgp * 4 + c
                o = p.dma("sp", olocT[h * 128:(h + 1) * 128, tok0:tok0 + T2], ot[c][:], r=[t_ot[c]], sem=f"ot{c}")
                t_ot[c].r.append(o.idx)
                finals.append(o)


def build_l0a():
    nc = bass.Bass("TRN2", target_bir_lowering=False)
    x = dram_in(nc, "x", [T, D])
    vec_d = dram_in(nc, "vec_pc", [128, 5, DC])
    w_in = dram_in(nc, "w_in", [D, 4 * D])
    smask_d = dram_in(nc, "c_smask", [128, T2])
    cd = {k: dram_in(nc, k, v) for k, v in CONST_SPECS.items()}
    olocT = dram_out(nc, "olocT", [D, T])
    qhatT = dram_out(nc, "qhatT", [D, T], BF16)
    sgT = dram_out(nc, "sgT", [D, T], BF16)
    S_end = dram_out(nc, "S_end", [128, HG_H, 128])
    ebtot = dram_out(nc, "ebtot", [128, HG_H])
    p = Prog(nc)
    finals = []
    wv = wview(w_in)
    with p.scope() as sc0:
        cst = Consts(p, sc0, cd)
        vec = sc0.sbuf("vec", [128, 5, DC], F32)
        t_vec = Trk()
        p.dma("sp", vec[:], vec_d, w=[t_vec])
        smask = sc0.sbuf("smask", [128, T2], F32)
        t_sm = Trk()
        p.dma("sp", smask[:], smask_d, w=[t_sm])
        A, B, t_AB = prep_affine(p, sc0, vec, t_vec, 0, 1, 2, "A0")
        lb = sc0.sbuf("lb", [128, DC], F32)
        oml = sc0.sbuf("oml", [128, DC], F32)
        t_lb = Trk()
        p.tt("dve", lb[:], vec[:, 3, :], vec[:, 4, :], ALU.subtract, r=[t_vec], w=[t_lb])
        p.act(lb[:], lb[:], AF.Sigmoid, r=[t_lb], w=[t_lb])
        p.ts("dve", oml[:], lb[:], -1.0, 1.0, ALU.mult, ALU.add, r=[t_lb], w=[t_lb])
        Sall = sc0.sbuf("Sall", [128, HG_H, 128], F32)
        t_S = trks(HG_H)
        Bc = sc0.sbuf("Bc", [128, HG_H], F32)
        t_Bc = trks(HG_H)
        p.op("dve", lambda e: e.memset(Bc[:], 0.0), w=t_Bc)
        p.op("pool", lambda e: e.memset(Sall[:], 0.0), w=t_S)
        p.barrier()
        H = dict(A=A, B=B, t_AB=t_AB, lb=lb, oml=oml, t_lb=t_lb, smask=smask, t_sm=t_sm, Sall=Sall, t_S=t_S, Bc=Bc, t_Bc=t_Bc,
                 wv=wv, olocT=olocT, qhatT=qhatT, sgT=sgT)
        for tp in range(T // T2):
            hgrn2_pass(p, cst, H, x[tp * T2:(tp + 1) * T2, :], tp * T2, tp == 0, finals)
        o = p.dma("sp", S_end, Sall[:], r=t_S, sem="fin0")
        finals.append(o)
        p.act(Bc[:], Bc[:], AF.Exp, r=t_Bc, w=t_Bc)
        o = p.dma("sp", ebtot, Bc[:], r=t_Bc, sem="fin1")
        finals.append(o)
    p.emit(finals)
    return nc


def linear_T_streamed(p, sc, wv, av, KC, col_groups, evac, name="ls"):
    ring = WRing(p, sc, nslots=3, kc=16, name=name + "w")
    hring = [sc.sbuf(name + "h", [128, 16, T], BF16) for _ in range(3)]
    t_hr = trks(3)
    units = []
    for gi, (c0, ncol) in enumerate(col_groups):
        for (k0, kc) in kunits(KC, 16):
            units.append((gi, c0, ncol, k0, kc))
    loaded = {}
    hi = [0]

    def load(j):
        gi, c0, ncol, k0, kc = units[j]
        wt, tw = ring.load(wv, k0, kc, c0, ncol)
        s = hi[0] % 3
        hi[0] += 1
        p.dma("sp", hring[s][:, 0:kc, :], av[:, k0:k0 + kc, :], w=[t_hr[s]], sem=f"{name}h{s}")
        return wt, tw, hring[s], t_hr[s]

    pf = 2
    for i in range(-pf, len(units)):
        if i + pf < len(units):
            loaded[i + pf] = load(i + pf)
        if i < 0:
            continue
        gi, c0, ncol, k0, kc = units[i]
        wt, tw, ht, th = loaded.pop(i)
        for b in range(NB):
            for k in range(kc):
                p.mm(p.ps[b][:, 0:ncol], ht[:, k, b * 128:(b + 1) * 128], wt[:, k, 0:ncol],
                     start=(k0 == 0 and k == 0), stop=(k0 + kc == KC and k == kc - 1), r=[tw, th], w=[p.tps[b]])
        if k0 + kc == KC:
            evac(gi, c0, ncol)


def tokmajor_norm_transpose(p, cst, sc, src, t_src, ncols, gain, t_gain, outT, t_outT, b, name):
    nchunk = ncols // 128
    junk = sc.sbuf(name + "j", [128, ncols], BF16)
    ss = sc.sbuf(name + "ss", [128, 1], F32)
    rs = sc.sbuf(name + "rs", [128, 1], F32)
    nb = sc.sbuf(name + "nb", [128, ncols], BF16)
    t_j, t_ss, t_rs, t_nb = Trk(), Trk(), Trk(), Trk()
    p.act(junk[:], src, AF.Square, r=[t_src], w=[t_j, t_ss], accum_out=ss[:])
    rms_rstd(p, ss[:], rs[:], t_ss, t_rs, ncols)
    p.op("dve", lambda e, o_=nb[:], i_=src, s_=rs[:, 0:1], g_=gain: e.scalar_tensor_tensor(
        out=o_, in0=i_, scalar=s_, in1=g_, op0=ALU.mult, op1=ALU.mult), r=[t_src, t_rs, t_gain], w=[t_nb])
    psT = p.ps[7][:].bitcast(BF16)
    for c in range(nchunk):
        p.tr(psT[:, c * 128:(c + 1) * 128], nb[:, c * 128:(c + 1) * 128], cst.idb[:], r=[t_nb, cst.t], w=[p.tps[7]])
    p.cp("act", outT[:, 0:nchunk, b * 128:(b + 1) * 128], psT[:, 0:nchunk * 128].rearrange("p (c t) -> p c t", t=128),
         r=[p.tps[7]], w=[t_outT])


def latents_stage(p, cst, x2, kvin_pc, t_vec, w_dkv, kvg_d, pos_d, invf_d, ckvT_dst, kpeT_dst, finals):
    with p.scope() as sc:
        hT = sc.sbuf("hT", [128, DC, T], BF16)
        t_hT = Trk()
        norm_to_featmajor(p, cst, x2, hT, t_hT, kvin_pc, None, t_vec)
        ring = WRing(p, sc, nslots=4, kc=16, name="wk")
        ck = sc.sbuf("ck", [128, NB, KVL + 64], F32)
        t_ck = trks(NB)
        kvg, t_kvg = load_rowbc(p, sc, "kvg", kvg_d, KVL)
        ckT = sc.sbuf("ckT", [128, KVL // 128, T], BF16)
        t_ckT = Trk()

        def act_fn(k):
            return hT[:, k, :], t_hT

        def evac_kv(gi, c0, ncol):
            for b in range(NB):
                p.cp("dve" if b % 2 else "act", ck[:, b, c0:c0 + ncol], p.ps[b][:, 0:ncol], r=[p.tps[b]], w=[t_ck[b]])

        linear(p, ring, wview(w_dkv), DC, cgroups(0, KVL + 64), act_fn, "T", evac_kv)
        for b in range(NB):
            with p.scope() as scb:
                tokmajor_norm_transpose(p, cst, scb, ck[:, b, 0:KVL], t_ck[b], KVL, kvg[:], t_kvg, ckT, t_ckT, b, "kvn")
        o = p.dma("sp", ckvT_dst, ckT[:], r=[t_ckT], sem="fo0")
        finals.append(o)
        posi = sc.sbuf("posi", [128, NB], I32)
        posf = sc.sbuf("posf", [128, NB], F32)
        invf = sc.sbuf("invf", [128, 32], F32)
        ang = sc.sbuf("ang", [128, NB, 64], F32)
        kk = sc.sbuf("kk", [128, NB, 64], F32)
        tmp = sc.sbuf("tmp", [128, NB, 64], F32)
        kr = sc.sbuf("kr", [128, NB, 64], F32)
        krb = sc.sbuf("krb", [128, NB, 64], BF16)
        kpT = sc.sbuf("kpT", [64, T], BF16)
        t_r = Trk()
        p.dma("sp", posi[:], pos_d, w=[t_r])
        p.dma("sp", invf[:], invf_d, w=[t_r])
        p.cp("dve", posf[:], posi[:], r=[t_r], w=[t_r])
        p.tt("dve", ang[:, :, 0:32], posf[:].unsqueeze(2).broadcast_to([128, NB, 32]),
             invf[:].unsqueeze(1).broadcast_to([128, NB, 32]), ALU.mult, r=[t_r], w=[t_r])
        p.ts("dve", ang[:, :, 32:64], ang[:, :, 0:32], float(np.pi / 2), None, ALU.add, r=[t_r], w=[t_r])
        rope_reduce_sin(p, ang[:], kk[:], tmp[:], t_r)
        sn = ang[:, :, 0:32]
        cs = ang[:, :, 32:64]
        k1 = ck[:, :, KVL:KVL + 32]
        k2 = ck[:, :, KVL + 32:KVL + 64]
        p.tt("dve", kr[:, :, 0:32], k1, cs, ALU.mult, r=[t_r] + t_ck, w=[t_r])
        p.tt("dve", tmp[:, :, 0:32], k2, sn, ALU.mult, r=[t_r], w=[t_r])
        p.tt("dve", kr[:, :, 0:32], kr[:, :, 0:32], tmp[:, :, 0:32], ALU.subtract, r=[t_r], w=[t_r])
        p.tt("dve", kr[:, :, 32:64], k2, cs, ALU.mult, r=[t_r], w=[t_r])
        p.tt("dve", tmp[:, :, 0:32], k1, sn, ALU.mult, r=[t_r], w=[t_r])
        p.tt("dve", kr[:, :, 32:64], kr[:, :, 32:64], tmp[:, :, 0:32], ALU.add, r=[t_r], w=[t_r])
        p.cp("dve", krb[:], kr[:], r=[t_r], w=[t_r])
        psT = p.ps[7][:].bitcast(BF16)
        for b in range(NB):
            p.tr(psT[0:64, b * 128:(b + 1) * 128], krb[:, b, :], cst.idb[:], r=[t_r, cst.t], w=[p.tps[7]])
        p.cp("act", kpT[:], psT[0:64, 0:T], r=[p.tps[7]], w=[t_r])
        o = p.dma("sp", kpeT_dst, kpT[:], r=[t_r], sem="fo1")
        finals.append(o)


def qlat_stage(p, cst, x2, A_1, B_1, t_AB1, w_dq, qng_d, qlatT_dst, finals):
    with p.scope() as sc:
        hT = sc.sbuf("hT", [128, DC, T], BF16)
        t_hT = Trk()
        norm_to_featmajor(p, cst, x2, hT, t_hT, A_1, B_1, t_AB1)
        ring = WRing(p, sc, nslots=4, kc=16, name="wq")
        ql = sc.sbuf("ql", [128, NB, QL], F32)
        t_ql = trks(NB)
        qng, t_qng = load_rowbc(p, sc, "qng", qng_d, QL)
        qlT = sc.sbuf("qlT", [128, QL // 128, T], BF16)
        t_qlT = Trk()

        def act_fn(k):
            return hT[:, k, :], t_hT

        def evac_q(gi, c0, ncol):
            for b in range(NB):
                p.cp("dve" if b % 2 else "act", ql[:, b, c0:c0 + ncol], p.ps[b][:, 0:ncol], r=[p.tps[b]], w=[t_ql[b]])

        linear(p, ring, wview(w_dq), DC, cgroups(0, QL), act_fn, "T", evac_q)
        for b in range(NB):
            with p.scope() as scb:
                tokmajor_norm_transpose(p, cst, scb, ql[:, b, :], t_ql[b], QL, qng[:], t_qng, qlT, t_qlT, b, "qn")
        o = p.dma("sp", qlatT_dst, qlT[:], r=[t_qlT], sem="fo2")
        finals.append(o)


NV0B = 12


def build_l0b():
    nc = bass.Bass("TRN2", target_bir_lowering=False)
    x = dram_in(nc, "x", [T, D])
    olocT = dram_in(nc, "olocT", [D, T])
    qhatT = dram_in(nc, "qhatT", [D, T], BF16)
    sgT = dram_in(nc, "sgT", [D, T], BF16)
    Sprev = dram_in(nc, "Sprev", [NCORE - 1, 128, HG_H, 128])
    ebprev = dram_in(nc, "ebprev", [128, NCORE - 1, HG_H])
    vec_d = dram_in(nc, "vec_pc", [128, 7, DC])
    hng_d = dram_in(nc, "hng", [128, 1])
    rows_d = dram_in(nc, "rows", [4, D])
    kvg_d = dram_in(nc, "kvg", [KVL])
    qng_d = dram_in(nc, "qng", [QL])
    pos_d = dram_in(nc, "pos_pb", [128, NB], I32)
    invf_d = dram_in(nc, "c_invf", [128, 32])
    w_out = dram_in(nc, "w_out", [D, D])
    wg = dram_in(nc, "wg", [D, FFN])
    wu = dram_in(nc, "wu", [D, FFN])
    wd = dram_in(nc, "wd", [FFN, D])
    w_dkv = dram_in(nc, "w_dkv", [D, KVL + 64])
    w_dq = dram_in(nc, "w_dq", [D, QL])
    cd = {k: dram_in(nc, k, v) for k, v in CONST_SPECS.items()}
    x2 = dram_out(nc, "x2", [T, D])
    ckvT_o = dram_out(nc, "ckvT", [KVL, T], BF16)
    kpeT_o = dram_out(nc, "kpeT", [64, T], BF16)
    qlatT_o = dram_out(nc, "qlatT", [QL, T], BF16)
    y_s = dram_tmp(nc, "y_s", [T, D])
    x1_s = dram_tmp(nc, "x1_s", [T, D])
    hid_s = dram_tmp(nc, "hid_s", [FFN, T], BF16)
    onT_s = dram_tmp(nc, "onT_s", [D, T], BF16)
    p = Prog(nc)
    finals = []
    with p.scope() as sc0:
        cst = Consts(p, sc0, cd)
        vec = sc0.sbuf("vec", [128, 7, DC], F32)
        t_vec = Trk()
        p.dma("sp", vec[:], vec_d, w=[t_vec])
        hng = sc0.sbuf("hng", [128, 1], F32)
        p.dma("sp", hng[:], hng_d, w=[t_vec])
        A_f, B_f, t_ABf = prep_affine(p, sc0, vec, t_vec, 0, 1, 2, "Af")
        A_1, B_1, t_AB1 = prep_affine(p, sc0, vec, t_vec, 4, 5, 6, "A1")
        p.barrier()
        with p.scope() as sc:
            Sin = sc.sbuf("Sin", [128, HG_H, 128], F32)
            Sinb = sc.sbuf("Sinb", [128, HG_H, 128], BF16)
            ebp = sc.sbuf("ebp", [128, NCORE - 1, HG_H], F32)
            Sp = [sc.sbuf("Sp", [128, HG_H, 128], F32) for _ in range(2)]
            t_Sin, t_ebp = Trk(), Trk()
            t_Sp = trks(2)
            p.dma("sp", ebp[:], ebprev, w=[t_ebp])
            p.op("dve", lambda e, o_=Sin[:]: e.memset(o_, 0.0), w=[t_Sin])
            for j in range(NCORE - 1):
                s = j % 2
                p.dma("sp", Sp[s][:], Sprev[j], w=[t_Sp[s]], sem=f"sp{s}")
                p.tt("dve", Sin[:], Sin[:], ebp[:, j, :].unsqueeze(2).broadcast_to([128, HG_H, 128]), ALU.mult,
                     r=[t_ebp], w=[t_Sin])
                p.tt("dve", Sin[:], Sin[:], Sp[s][:], ALU.add, r=[t_Sp[s]], w=[t_Sin])
            p.cp("act", Sinb[:], Sin[:], r=[t_Sin], w=[t_Sin])
            qh = [sc.sbuf("qh", [128, 512], BF16) for _ in range(6)]
            ol = [sc.sbuf("ol", [128, 512], F32) for _ in range(6)]
            sg = [sc.sbuf("sg", [128, 512], BF16) for _ in range(6)]
            sq = [sc.sbuf("sq", [128, 512], F32) for _ in range(6)]
            rt = [sc.sbuf("rt", [128, 512], F32) for _ in range(6)]
            on = [sc.sbuf("on", [128, 512], BF16) for _ in range(6)]
            t_qh, t_ol, t_sg, t_sq, t_rt, t_on = trks(6), trks(6), trks(6), trks(6), trks(6), trks(6)
            items = [(h, tt) for h in range(HG_H) for tt in range(T // 512)]
            NS = 6

            def phase_a(n):
                h, tt = items[n]
                s = n % NS
                rsl = slice(h * 128, (h + 1) * 128)
                tsl = slice(tt * 512, (tt + 1) * 512)
                p.dma("pool", qh[s][:], qhatT[rsl, tsl], w=[t_qh[s]], sem=f"bq{s}")
                p.dma("sp", ol[s][:], olocT[rsl, tsl], w=[t_ol[s]], sem=f"bo{s}")
                p.dma("pool", sg[s][:], sgT[rsl, tsl], w=[t_sg[s]], sem=f"bg{s}")
                b0 = n % 4
                p.mm(p.ps[b0][:], Sinb[:, h, :], qh[s][:], True, True, r=[t_Sin, t_qh[s]], w=[p.tps[b0]])
                p.tt("dve", ol[s][:], ol[s][:], p.ps[b0][:], ALU.add, r=[t_ol[s], p.tps[b0]], w=[t_ol[s]])
                p.act(sq[s][:], ol[s][:], AF.Square, r=[t_ol[s]], w=[t_sq[s]])

            def phase_b(n):
                s = n % NS
                b1 = 4 + n % 4
                p.mm(p.ps[b1][:], cst.ones[:], sq[s][:], True, True, r=[cst.t, t_sq[s]], w=[p.tps[b1]])
                p.ts("dve", rt[s][:], p.ps[b1][:], 1.0 / 128, EPS, ALU.mult, ALU.add, r=[p.tps[b1]], w=[t_rt[s]])
                p.act(rt[s][:], rt[s][:], AF.Sqrt, r=[t_rt[s]], w=[t_rt[s]])

            def phase_c(n):
                s = n % NS
                p.op("dve", lambda e, o_=rt[s][:]: e.reciprocal(out=o_, in_=o_), r=[t_rt[s]], w=[t_rt[s]])
                p.op("dve", lambda e, o_=ol[s][:], s_=hng[:, 0:1], r_=rt[s][:]: e.scalar_tensor_tensor(
                    out=o_, in0=o_, scalar=s_, in1=r_, op0=ALU.mult, op1=ALU.mult),
                    r=[t_ol[s], t_rt[s], t_vec], w=[t_ol[s]])

            def phase_d(n):
                h, tt = items[n]
                s = n % NS
                rsl = slice(h * 128, (h + 1) * 128)
                tsl = slice(tt * 512, (tt + 1) * 512)
                p.tt("pool", on[s][:], ol[s][:], sg[s][:], ALU.mult, r=[t_ol[s], t_sg[s]], w=[t_on[s]])
                o = p.dma("sp", onT_s[rsl, tsl], on[s][:], r=[t_on[s]], sem=f"bn{s}")
                t_on[s].r.append(o.idx)

            NI = len(items)
            for t in range(NI + 3):
                if t < NI:
                    phase_a(t)
                if 0 <= t - 1 < NI:
                    phase_b(t - 1)
                if 0 <= t - 2 < NI:
                    phase_c(t - 2)
                if 0 <= t - 3 < NI:
                    phase_d(t - 3)
        with p.scope() as sc:
            evac = y_evac_to_dram(p, sc, y_s)
            linear_T_streamed(p, sc, wview(w_out), onT_s.rearrange("(c p) t -> p c t", p=128), DC, cgroups(0, D), evac, name="wo")
        with p.scope() as sc:
            G, t_G = load_rowbc(p, sc, "Gm", rows_d[0], D)
            G2, t_G2 = load_rowbc(p, sc, "Gm2", rows_d[1], D)
            p.tt("dve", G[:], G[:], G2[:], ALU.mult, r=[t_G, t_G2], w=[t_G])
            resid_pass(p, x, y_s, G, t_G, x1_s)
        with p.scope() as sc:
            G, t_G = load_rowbc(p, sc, "Gf", rows_d[2], D)
            with p.scope() as scg:
                G2, t_G2 = load_rowbc(p, scg, "Gf2", rows_d[3], D)
                p.tt("dve", G[:], G[:], G2[:], ALU.mult, r=[t_G, t_G2], w=[t_G])
            finals += ffn_stage(p, cst, x1_s, x2, wg, wu, wd, A_f, B_f, t_ABf, G, t_G, hid_s, y_s)
        latents_stage(p, cst, x2, vec[:, 3, :], t_vec, w_dkv, kvg_d, pos_d, invf_d,
                      ckvT_o.rearrange("(c p) t -> p c t", p=128), kpeT_o, finals)
        qlat_stage(p, cst, x2, A_1, B_1, t_AB1, w_dq, qng_d, qlatT_o.rearrange("(c p) t -> p c t", p=128), finals)
    p.emit(finals)
    return nc


TWO_PI_HI = 6.28125
TWO_PI_LO = float(2 * np.pi - 6.28125)
MAGIC = 12582912.0


def rope_reduce_sin(p, ang, kk, tmp, t_r, eng="dve"):
    p.ts(eng, kk, ang, float(1.0 / (2 * np.pi)), MAGIC, ALU.mult, ALU.add, r=[t_r], w=[t_r])
    p.ts(eng, kk, kk, MAGIC, None, ALU.subtract, r=[t_r], w=[t_r])
    p.ts(eng, tmp, kk, TWO_PI_HI, None, ALU.mult, r=[t_r], w=[t_r])
    p.tt(eng, ang, ang, tmp, ALU.subtract, r=[t_r], w=[t_r])
    p.ts(eng, tmp, kk, TWO_PI_LO, None, ALU.mult, r=[t_r], w=[t_r])
    p.tt(eng, ang, ang, tmp, ALU.subtract, r=[t_r], w=[t_r])
    p.ts(eng, ang, ang, 3.141592, -3.141592, ALU.min, ALU.max, r=[t_r], w=[t_r])
    p.act(ang, ang, AF.Sin, r=[t_r], w=[t_r])


NQT = S // 512
NKB = S // 128


def build_attn():
    nc = bass.Bass("TRN2", target_bir_lowering=False)
    qlatT = dram_in(nc, "qlatT", [QL, S], BF16)
    ckvT = dram_in(nc, "ckvT", [KVL, S], BF16)
    kpeT = dram_in(nc, "kpeT", [64, S], BF16)
    pos_d = dram_in(nc, "pos", [S], I32)
    invf_d = dram_in(nc, "invf_p", [64, 1])
    w_qn = dram_in(nc, "w_qn", [QL, HPC * 128])
    w_qp = dram_in(nc, "w_qp", [QL, HPC * 64])
    w_k = dram_in(nc, "w_k", [KVL, HPC * 128])
    w_v = dram_in(nc, "w_v", [KVL, HPC * 128])
    rm_d = dram_in(nc, "c_rm", [64, 64])
    cd = {k: dram_in(nc, k, v) for k, v in CONST_SPECS.items()}
    oT = dram_out(nc, "oT", [HPC * 128, S], BF16)
    tabs = dram_tmp(nc, "tabs", [2, 64, S])
    p = Prog(nc)
    finals = []
    with p.scope() as sc0:
        cst = Consts(p, sc0, cd)
        ckv = sc0.sbuf("ckv", [128, 4, S], BF16)
        kpe = sc0.sbuf("kpe", [64, S], BF16)
        wqn = sc0.sbuf("wqn", [128, 8, HPC * 128], BF16)
        wqp = sc0.sbuf("wqp", [128, 8, HPC * 64], BF16)
        wk = sc0.sbuf("wk", [128, 4, HPC * 128], BF16)
        wvv = sc0.sbuf("wvv", [128, 4, HPC * 128], BF16)
        rm = sc0.sbuf("rm", [64, 64], BF16)
        tribf = sc0.sbuf("tribf", [128, 128], BF16)
        t_c = Trk()
        p.dma("sp", ckv[:], ckvT.rearrange("(c p) t -> p c t", p=128), w=[t_c], sem="a0")
        p.dma("sp", kpe[:], kpeT, w=[Trk()], sem="a1")
        p.dma("pool", wqn[:], wview(w_qn), w=[Trk()], sem="a2")
        p.dma("pool", wqp[:], wview(w_qp), w=[Trk()], sem="a3")
        p.dma("pool", wk[:], wview(w_k), w=[Trk()], sem="a4")
        p.dma("pool", wvv[:], wview(w_v), w=[Trk()], sem="a5")
        p.dma("pool", rm[:], rm_d, w=[Trk()], sem="a6")
        p.dma("pool", tribf[:], cd["c_tri"], w=[Trk()], sem="a7")
        onesb = sc0.sbuf("onesb", [128, 128], BF16)
        p.dma("pool", onesb[:], cd["c_ones"], w=[Trk()], sem="a10")
        with p.scope() as sc:
            invf = sc.sbuf("invf", [64, 1], F32)
            t_r = Trk()
            p.dma("sp", invf[:], invf_d, w=[t_r])
            posi = sc.sbuf("posi", [64, 2048], I32)
            ang = sc.sbuf("ang", [64, 2, 2048], F32)
            kk = sc.sbuf("kk", [64, 2, 2048], F32)
            tmp = sc.sbuf("tmp", [64, 2, 2048], F32)
            for q4 in range(S // 2048):
                tsl = slice(q4 * 2048, (q4 + 1) * 2048)
                p.dma("sp", posi[:], pos_d[tsl].partition_broadcast(64), w=[t_r], sem="a8")
                p.cp("dve", ang[:, 0, :], posi[:], r=[t_r], w=[t_r])
                p.ts("dve", ang[:, 0, :], ang[:, 0, :], invf[:, 0:1], None, ALU.mult, r=[t_r], w=[t_r])
                p.ts("dve", ang[:, 1, :], ang[:, 0, :], float(np.pi / 2), None, ALU.add, r=[t_r], w=[t_r])
                rope_reduce_sin(p, ang[:], kk[:], tmp[:], t_r)
                o = p.dma("sp", tabs[:, :, tsl].rearrange("a p t -> p a t"), ang[:], r=[t_r], sem="a9")
                t_r.r.append(o.idx)
        p.barrier()
        KnT = sc0.sbuf("KnT", [128, S], BF16)
        V = sc0.sbuf("V", [128, NKB, 128], BF16)
        t_Kn, t_V = Trk(), Trk()
        qlr = [sc0.sbuf("qlr", [128, 8, 512], BF16) for _ in range(2)]
        t_qlr = trks(2)
        sct = [sc0.sbuf("sct", [64, 2, 512], F32) for _ in range(2)]
        t_sct = trks(2)
        qn = [sc0.sbuf("qn", [128, 512], BF16) for _ in range(2)]
        qp = [sc0.sbuf("qp", [64, 512], BF16) for _ in range(2)]
        t_qn, t_qp = trks(2), trks(2)
        qpf = sc0.sbuf("qpf", [64, 512], F32)
        qpb = sc0.sbuf("qpb", [64, 512], BF16)
        t1 = sc0.sbuf("t1", [64, 512], F32)
        t2 = sc0.sbuf("t2", [64, 512], F32)
        t_qpf, t_qpb, t_t1, t_t2 = Trk(), Trk(), Trk(), Trk()
        pT = [sc0.sbuf("pT", [128, 512], BF16) for _ in range(4)]
        t_pT = trks(4)
        acc = [sc0.sbuf("acc", [128, 512], F32) for _ in range(2)]
        t_acc = trks(2)
        rden = sc0.sbuf("rden", [128, 512], F32)
        t_rden = Trk()
        ob = [sc0.sbuf("ob", [128, 512], BF16) for _ in range(2)]
        t_ob = trks(2)
        dsb = sc0.sbuf("dsb", [128, 512], F32)
        t_dsb = Trk()
        nq = 0
        for h in range(HPC):
            hs = slice(h * 128, (h + 1) * 128)
            for kt in range(S // 512):
                bk = kt % 2
                for kc in range(4):
                    p.mm(p.ps[bk][:], wk[:, kc, hs], ckv[:, kc, kt * 512:(kt + 1) * 512], kc == 0, kc == 3,
                         r=[t_c], w=[p.tps[bk]])
                p.cp("act" if kt % 2 else "dve", KnT[:, kt * 512:(kt + 1) * 512], p.ps[bk][:], r=[p.tps[bk]], w=[t_Kn])
            for k4 in range(NKB // 4):
                bk = k4 % 2
                for j in range(4):
                    kb = k4 * 4 + j
                    for kc in range(4):
                        p.mm(p.ps[bk][:, j * 128:(j + 1) * 128], ckv[:, kc, kb * 128:(kb + 1) * 128], wvv[:, kc, hs],
                             kc == 0, kc == 3, r=[t_c], w=[p.tps[bk]])
                p.cp("act" if k4 % 2 else "dve", V[:, k4 * 4:(k4 + 1) * 4, :],
                     p.ps[bk][:].rearrange("p (j v) -> p j v", v=128), r=[p.tps[bk]], w=[t_V])
            def prologue_dma(i, s):
                tsl = slice(i * 512, (i + 1) * 512)
                p.dma("sp", qlr[s][:], qlatT.rearrange("(c p) t -> p c t", p=128)[:, :, tsl], w=[t_qlr[s]], sem=f"ql{s}")
                p.dma("sp", sct[s][:], tabs[:, :, tsl].rearrange("a p t -> p a t"), w=[t_sct[s]], sem=f"sc{s}")

            def prologue(i, s):
                for kc in range(8):
                    p.mm(p.ps[0][:], wqn[:, kc, hs], qlr[s][:, kc, :], kc == 0, kc == 7, r=[t_qlr[s]], w=[p.tps[0]])
                p.act(qn[s][:], p.ps[0][:], AF.Copy, r=[p.tps[0]], w=[t_qn[s]], scale=SCALE)
                for kc in range(8):
                    p.mm(p.ps[1][0:64, :], wqp[:, kc, h * 64:(h + 1) * 64], qlr[s][:, kc, :], kc == 0, kc == 7,
                         r=[t_qlr[s]], w=[p.tps[1]])
                p.ts("dve", qpf[:], p.ps[1][0:64, :], SCALE, None, ALU.mult, r=[p.tps[1]], w=[t_qpf])
                p.cp("pool", qpb[:], qpf[:], r=[t_qpf], w=[t_qpb])
                p.mm(p.ps[1][0:64, :], rm[:], qpb[:], True, True, r=[t_qpb], w=[p.tps[1]])
                p.tt("dve", t1[:], qpf[:], sct[s][:, 1, :], ALU.mult, r=[t_qpf, t_sct[s]], w=[t_t1])
                p.tt("dve", t2[:], p.ps[1][0:64, :], sct[s][:, 0, :], ALU.mult, r=[p.tps[1], t_sct[s]], w=[t_t2])
                p.tt("pool", qp[s][:], t1[:], t2[:], ALU.add, r=[t_t1, t_t2], w=[t_qp[s]])

            slot0 = nq
            steps = [(i, kb) for i in range(NQT) for kb in range(4 * i + 4)]
            LAG = 2
            prologue_dma(0, slot0 % 2)
            prologue(0, slot0 % 2)
            prologue_dma(1, (slot0 + 1) % 2)

            def s_step(g):
                i, kb = steps[g]
                s = (slot0 + i) % 2
                j = kb - 4 * i
                c0 = max(j, 0) * 128
                bk = 2 + g % 3
                ks = slice(kb * 128, (kb + 1) * 128)
                p.mm(p.ps[bk][:, c0:512], KnT[:, ks], qn[s][:, c0:512], True, False, r=[t_Kn, t_qn[s]], w=[p.tps[bk]])
                p.mm(p.ps[bk][:, c0:512], kpe[:, ks], qp[s][:, c0:512], False, True, r=[t_c, t_qp[s]], w=[p.tps[bk]])
                p.act(pT[g % 4][:, c0:512], p.ps[bk][:, c0:512], AF.Exp, r=[p.tps[bk]], w=[t_pT[g % 4]])
                if j >= 0:
                    p.tt("dve", pT[g % 4][:, c0:c0 + 128], pT[g % 4][:, c0:c0 + 128], tribf[:], ALU.mult,
                         r=[t_pT[g % 4]], w=[t_pT[g % 4]])
                if kb == (4 * i + 4) // 2 and i + 1 < NQT:
                    prologue(i + 1, (slot0 + i + 1) % 2)
                    if i + 2 < NQT:
                        prologue_dma(i + 2, (slot0 + i + 2) % 2)

            def pv_step(g):
                i, kb = steps[g]
                s = (slot0 + i) % 2
                nkb = 4 * i + 4
                po = 6 + (i % 2)
                j = kb - 4 * i
                c0 = max(j, 0) * 128
                p.mm(p.ps[po][:, c0:512], V[:, kb, :], pT[g % 4][:, c0:512], kb == 0, kb == nkb - 1,
                     r=[t_V, t_pT[g % 4]], w=[p.tps[po]])
                p.mm(p.ps[5][:, c0:512], onesb[:], pT[g % 4][:, c0:512], kb == 0, kb == nkb - 1,
                     r=[t_pT[g % 4]], w=[p.tps[5]])
                if kb == nkb - 1:
                    tsl = slice(i * 512, (i + 1) * 512)
                    p.cp("act", dsb[:], p.ps[5][:], r=[p.tps[5]], w=[t_dsb])
                    p.op("dve", lambda e, o_=rden[:], i_=dsb[:]: e.reciprocal(out=o_, in_=i_), r=[t_dsb], w=[t_rden])
                    p.tt("dve", ob[s][:], p.ps[po][:], rden[:], ALU.mult, r=[p.tps[po], t_rden], w=[t_ob[s]])
                    o = p.dma("sp", oT[hs, tsl], ob[s][:], r=[t_ob[s]], sem=f"ob{s}")
                    t_ob[s].r.append(o.idx)
                    finals.append(o)

            for g in range(len(steps) + LAG):
                if g < len(steps):
                    s_step(g)
                if g - LAG >= 0:
                    pv_step(g - LAG)
            nq += NQT
    p.emit(finals)
    return nc


def build_l1b():
    nc = bass.Bass("TRN2", target_bir_lowering=False)
    x2 = dram_in(nc, "x2", [T, D])
    oT = dram_in(nc, "oT", [MLA_H * 128, T], BF16)
    vec_d = dram_in(nc, "vec_pc", [128, 3, DC])
    rows_d = dram_in(nc, "rows", [4, D])
    w_o = dram_in(nc, "w_o", [MLA_H * 128, D])
    wg = dram_in(nc, "wg", [D, FFN])
    wu = dram_in(nc, "wu", [D, FFN])
    wd = dram_in(nc, "wd", [FFN, D])
    cd = {k: dram_in(nc, k, v) for k, v in CONST_SPECS.items()}
    out = dram_out(nc, "out", [T, D])
    y_s = dram_tmp(nc, "y_s", [T, D])
    x3_s = dram_tmp(nc, "x3_s", [T, D])
    hid_s = dram_tmp(nc, "hid_s", [FFN, T], BF16)
    p = Prog(nc)
    finals = []
    with p.scope() as sc0:
        cst = Consts(p, sc0, cd)
        vec = sc0.sbuf("vec", [128, 3, DC], F32)
        t_vec = Trk()
        p.dma("sp", vec[:], vec_d, w=[t_vec])
        A_f, B_f, t_ABf = prep_affine(p, sc0, vec, t_vec, 0, 1, 2, "Af")
        p.barrier()
        with p.scope() as sc:
            evac = y_evac_to_dram(p, sc, y_s)
            linear_T_streamed(p, sc, wview(w_o), oT.rearrange("(c p) t -> p c t", p=128), MLA_H, cgroups(0, D), evac, name="wo")
        with p.scope() as sc:
            G, t_G = load_rowbc(p, sc, "Gm", rows_d[0], D)
            G2, t_G2 = load_rowbc(p, sc, "Gm2", rows_d[1], D)
            p.tt("dve", G[:], G[:], G2[:], ALU.mult, r=[t_G, t_G2], w=[t_G])
            resid_pass(p, x2, y_s, G, t_G, x3_s)
        with p.scope() as sc:
            G, t_G = load_rowbc(p, sc, "Gf", rows_d[2], D)
            with p.scope() as scg:
                G2, t_G2 = load_rowbc(p, scg, "Gf2", rows_d[3], D)
                p.tt("dve", G[:], G[:], G2[:], ALU.mult, r=[t_G, t_G2], w=[t_G])
            finals += ffn_stage(p, cst, x3_s, out, wg, wu, wd, A_f, B_f, t_ABf, G, t_G, hid_s, y_s)
    p.emit(finals)
    return nc


def _pc(v):
    return np.ascontiguousarray(np.asarray(v, np.float32).reshape(-1, 128).T)


def _run(nc, in_maps):
    res = run_bass_kernel_spmd(nc, in_maps, core_ids=list(range(NCORE)))
    return [{k: np.asarray(v) for k, v in r.items()} for r in res.results]


def _inv_freq():
    return (1.0 / (10000.0 ** (np.arange(0, 64, 2, dtype=np.float32) / 64))).astype(np.float32)


def kernel_unfused(x, c, positions, ada_w, ada_b, norm_g, ffn_w_gate, ffn_w_up, ffn_w_down,
           hg_w_in, hg_lb_logits, hg_norm_g, hg_w_out,
           mla_w_dq, mla_q_norm_g, mla_w_uq, mla_w_o,
           kv_norm_in_g, kv_w_dkv, kv_norm_g, kv_w_ukv):
    f32 = np.float32
    x = np.asarray(x, f32)
    cst = const_arrays()
    ng = np.asarray(norm_g, f32)
    ca = np.ascontiguousarray
    NCOL = 6 * D // NCORE
    c_pc = _pc(np.asarray(c, f32)[0])
    ada_w = np.asarray(ada_w, f32)
    ada_b = np.asarray(ada_b, f32)
    maps = [{"w": ca(ada_w[:, :, i * NCOL:(i + 1) * NCOL]), "b": ca(ada_b[:, i * NCOL:(i + 1) * NCOL]), "c_pc": c_pc}
            for i in range(NCORE)]
    r = _run(build_ada(), maps)
    mod = np.concatenate([o["mod"] for o in r], axis=1)
    sh_m0, sc_m0, g_m0, sh_f0, sc_f0, g_f0 = np.split(mod[0], 6)
    sh_m1, sc_m1, g_m1, sh_f1, sc_f1, g_f1 = np.split(mod[1], 6)
    lbl = np.asarray(hg_lb_logits, f32)
    vec = ca(np.stack([_pc(ng[0, 0]), _pc(sc_m0), _pc(sh_m0), _pc(lbl[0]), _pc(lbl[1])], axis=1))
    smask = np.ones((128, T2), f32)
    smask[:, ::64] = 0
    w_in = np.asarray(hg_w_in, f32)[0]
    maps = []
    for i in range(NCORE):
        m = {"x": ca(x[0, i * T:(i + 1) * T]), "vec_pc": vec, "w_in": w_in, "c_smask": smask}
        m.update(cst)
        maps.append(m)
    l0a = _run(build_l0a(), maps)
    vec = ca(np.stack([_pc(ng[0, 2]), _pc(sc_f0), _pc(sh_f0), _pc(kv_norm_in_g), _pc(ng[1, 0]), _pc(sc_m1), _pc(sh_m1)], axis=1))
    rows = ca(np.stack([g_m0, ng[0, 1], g_f0, ng[0, 3]]).astype(f32))
    invf = ca(np.tile(_inv_freq()[None, :], (128, 1)))
    pos = np.asarray(positions, np.int32)
    maps = []
    for ci in range(NCORE):
        Sprev = np.zeros((NCORE - 1, 128, HG_H, 128), f32)
        ebprev = np.zeros((128, NCORE - 1, HG_H), f32)
        for j in range(NCORE - 1):
            src = ci - (NCORE - 1) + j
            if src >= 0:
                Sprev[j] = l0a[src]["S_end"]
                ebprev[:, j, :] = l0a[src]["ebtot"]
        m = {"x": ca(x[0, ci * T:(ci + 1) * T]), "olocT": l0a[ci]["olocT"], "qhatT": l0a[ci]["qhatT"], "sgT": l0a[ci]["sgT"],
             "Sprev": Sprev, "ebprev": ebprev, "vec_pc": vec, "hng": ca(np.asarray(hg_norm_g, f32)[0].reshape(128, 1)),
             "rows": rows, "kvg": np.asarray(kv_norm_g, f32), "qng": np.asarray(mla_q_norm_g, f32)[0],
             "pos_pb": ca(pos[0, ci * T:(ci + 1) * T].reshape(NB, 128).T), "c_invf": invf,
             "w_out": np.asarray(hg_w_out, f32)[0], "wg": np.asarray(ffn_w_gate, f32)[0], "wu": np.asarray(ffn_w_up, f32)[0],
             "wd": np.asarray(ffn_w_down, f32)[0], "w_dkv": np.asarray(kv_w_dkv, f32), "w_dq": np.asarray(mla_w_dq, f32)[0]}
        m.update(cst)
        maps.append(m)
    l0b = _run(build_l0b(), maps)
    del l0a
    qlatT = np.concatenate([o["qlatT"] for o in l0b], axis=1)
    ckvT = np.concatenate([o["ckvT"] for o in l0b], axis=1)
    kpeT = np.concatenate([o["kpeT"] for o in l0b], axis=1)
    inv = _inv_freq()
    invf_p = np.concatenate([inv, inv]).reshape(64, 1).astype(f32)
    rm = np.zeros((64, 64), f32)
    for i in range(32):
        rm[i + 32, i] = -1.0
        rm[i, i + 32] = 1.0
    wuq = np.asarray(mla_w_uq, f32)[0].reshape(QL, MLA_H, 192)
    wukv = np.asarray(kv_w_ukv, f32).reshape(KVL, MLA_H, 256)
    maps = []
    for ci in range(NCORE):
        hs = slice(HPC * ci, HPC * (ci + 1))
        m = {"qlatT": qlatT, "ckvT": ckvT, "kpeT": kpeT, "pos": ca(pos[0]), "invf_p": invf_p,
             "w_qn": ca(wuq[:, hs, :128]).reshape(QL, HPC * 128), "w_qp": ca(wuq[:, hs, 128:]).reshape(QL, HPC * 64),
             "w_k": ca(wukv[:, hs, :128]).reshape(KVL, HPC * 128), "w_v": ca(wukv[:, hs, 128:]).reshape(KVL, HPC * 128),
             "c_rm": rm}
        m.update(cst)
        maps.append(m)
    att = _run(build_attn(), maps)
    oT_all = np.concatenate([o["oT"] for o in att], axis=0)
    del att
    vec = ca(np.stack([_pc(ng[1, 2]), _pc(sc_f1), _pc(sh_f1)], axis=1))
    rows = ca(np.stack([g_m1, ng[1, 1], g_f1, ng[1, 3]]).astype(f32))
    maps = []
    for ci in range(NCORE):
        m = {"x2": l0b[ci]["x2"], "oT": ca(oT_all[:, ci * T:(ci + 1) * T]), "vec_pc": vec, "rows": rows,
             "w_o": np.asarray(mla_w_o, f32)[0], "wg": np.asarray(ffn_w_gate, f32)[1], "wu": np.asarray(ffn_w_up, f32)[1],
             "wd": np.asarray(ffn_w_down, f32)[1]}
        m.update(cst)
        maps.append(m)
    l1b = _run(build_l1b(), maps)
    out = np.concatenate([o["out"] for o in l1b], axis=0).reshape(1, S, D).astype(f32)
    return out


def ada_stage(p, cst, w, bias, c_pc, mod_s, ncols):
    with p.scope() as sc:
        cs = sc.sbuf("cs", [128, DC], F32)
        t_cs = Trk()
        p.dma("sp", cs[:], c_pc, w=[t_cs])
        p.act(cs[:], cs[:], AF.Silu, r=[t_cs], w=[t_cs])
        ring = WRing(p, sc, nslots=3, kc=16, name="wa", dt=F32)
        ot = [sc.sbuf("ot", [1, 512], F32) for _ in range(2)]
        bt = [sc.sbuf("bt", [1, 512], F32) for _ in range(2)]
        t_ot, t_bt = trks(2), trks(2)
        units = []
        for l in range(2):
            wv = wview(w[l])
            for (c0, ncol) in cgroups(0, ncols):
                for (k0, kc) in kunits(DC, 16):
                    units.append((l, wv, c0, ncol, k0, kc))
        loaded = {}
        pf = 2
        n = 0
        for i in range(-pf, len(units)):
            if i + pf < len(units):
                l, wv, c0, ncol, k0, kc = units[i + pf]
                loaded[i + pf] = ring.load(wv, k0, kc, c0, ncol)
            if i < 0:
                continue
            l, wv, c0, ncol, k0, kc = units[i]
            wt, tw = loaded.pop(i)
            bk = (n // 2) % 2
            if k0 == 0:
                p.dma("sp", bt[bk][:, 0:ncol], bias[l:l + 1, c0:c0 + ncol], w=[t_bt[bk]], sem=f"ab{bk}")
            for k in range(kc):
                p.mm(p.ps[bk][0:1, 0:ncol], cs[:, k0 + k:k0 + k + 1], wt[:, k, 0:ncol],
                     start=(k0 == 0 and k == 0), stop=(k0 + kc == DC and k == kc - 1), r=[tw, t_cs], w=[p.tps[bk]])
            n += 1
            if k0 + kc == DC:
                s = bk
                p.tt("dve", ot[s][:, 0:ncol], p.ps[bk][0:1, 0:ncol], bt[s][:, 0:ncol], ALU.add,
                     r=[p.tps[bk], t_bt[s]], w=[t_ot[s]])
                o = p.dma("sp", mod_s[l:l + 1, c0:c0 + ncol], ot[s][:, 0:ncol], r=[t_ot[s]], sem=f"ao{s}")
                t_ot[s].r.append(o.idx)


def gate_stage(p, cst, olocT, sgT, onT_s, hng, t_hng):
    with p.scope() as sc:
        ol = [sc.sbuf("ol", [128, 512], F32) for _ in range(2)]
        sg = [sc.sbuf("sg", [128, 512], BF16) for _ in range(2)]
        sq = [sc.sbuf("sq", [128, 512], F32) for _ in range(2)]
        rt = [sc.sbuf("rt", [128, 512], F32) for _ in range(2)]
        on = [sc.sbuf("on", [128, 512], BF16) for _ in range(2)]
        t_ol, t_sg, t_sq, t_rt, t_on = trks(2), trks(2), trks(2), trks(2), trks(2)
        n = 0
        for h in range(HG_H):
            for tt in range(T // 512):
                s = n % 2
                n += 1
                rsl = slice(h * 128, (h + 1) * 128)
                tsl = slice(tt * 512, (tt + 1) * 512)
                p.dma("sp", ol[s][:], olocT[rsl, tsl], w=[t_ol[s]], sem=f"bo{s}")
                p.dma("sp", sg[s][:], sgT[rsl, tsl], w=[t_sg[s]], sem=f"bg{s}")
                b1 = 2 + s
                p.act(sq[s][:], ol[s][:], AF.Square, r=[t_ol[s]], w=[t_sq[s]])
                p.mm(p.ps[b1][:], cst.ones[:], sq[s][:], True, True, r=[cst.t, t_sq[s]], w=[p.tps[b1]])
                p.ts("dve", rt[s][:], p.ps[b1][:], 1.0 / 128, EPS, ALU.mult, ALU.add, r=[p.tps[b1]], w=[t_rt[s]])
                p.act(rt[s][:], rt[s][:], AF.Sqrt, r=[t_rt[s]], w=[t_rt[s]])
                p.op("dve", lambda e, o_=rt[s][:]: e.reciprocal(out=o_, in_=o_), r=[t_rt[s]], w=[t_rt[s]])
                p.op("dve", lambda e, o_=ol[s][:], s_=hng[:, 0:1], r_=rt[s][:]: e.scalar_tensor_tensor(
                    out=o_, in0=o_, scalar=s_, in1=r_, op0=ALU.mult, op1=ALU.mult),
                    r=[t_ol[s], t_rt[s], t_hng], w=[t_ol[s]])
                p.tt("pool", on[s][:], ol[s][:], sg[s][:], ALU.mult, r=[t_ol[s], t_sg[s]], w=[t_on[s]])
                o = p.dma("sp", onT_s[rsl, tsl], on[s][:], r=[t_on[s]], sem=f"bn{s}")
                t_on[s].r.append(o.idx)


def select_pass(p, x_src, x_acc, selcol, t_sel, first):
    with p.scope() as sc:
        a = [sc.sbuf("sa", [128, D], F32) for _ in range(2)]
        o = [sc.sbuf("so", [128, D], F32) for _ in range(2)]
        t_a, t_o = trks(2), trks(2)
        for b in range(NB):
            s = b % 2
            rows = slice(b * 128, (b + 1) * 128)
            p.dma("sp", a[s][:], x_src[rows, :], w=[t_a[s]], sem=f"sa{s}")
            if first:
                p.ts("dve", o[s][:], a[s][:], selcol, None, ALU.mult, r=[t_a[s], t_sel], w=[t_o[s]])
            else:
                p.dma("sp", o[s][:], x_acc[rows, :], w=[t_o[s]], sem=f"so{s}")
                p.op("dve", lambda e, o_=o[s][:], a_=a[s][:], c_=selcol: e.scalar_tensor_tensor(
                    out=o_, in0=a_, scalar=c_, in1=o_, op0=ALU.mult, op1=ALU.add), r=[t_a[s], t_sel, t_o[s]], w=[t_o[s]])
            d = p.dma("sp", x_acc[rows, :], o[s][:], r=[t_o[s]], sem=f"sw{s}")
            t_o[s].r.append(d.idx)


def attn_own_stage(p, cst, qlatT_own, ckvT_all, kpeT_all, pos_own_d, invf_p_d, w_qn, w_qp, w_k, w_v, rm_d, dm_d, thr_d, tri_d, oT_own):
    NT_Q = T // 512
    with p.scope() as sc0:
        ckv = sc0.sbuf("ckv", [128, 4, S], BF16)
        kpe = sc0.sbuf("kpe", [64, S], BF16)
        qlat = sc0.sbuf("qlat", [128, 8, T], BF16)
        rm = sc0.sbuf("rm", [64, 64], BF16)
        Dm = sc0.sbuf("Dm", [128, 512], F32)
        thr = sc0.sbuf("thr", [128, NT_Q * NKB], F32)
        tabs = sc0.sbuf("tabs", [64, 2, T], F32)
        t_c = Trk()
        p.dma("sp", ckv[:], ckvT_all.rearrange("(c p) t -> p c t", p=128), w=[t_c], sem="a0")
        p.dma("sp", kpe[:], kpeT_all, w=[Trk()], sem="a1")
        p.dma("sp", qlat[:], qlatT_own.rearrange("(c p) t -> p c t", p=128), w=[Trk()], sem="a2")
        p.dma("pool", rm[:], rm_d, w=[Trk()], sem="a6")
        p.dma("sp", Dm[:], dm_d, w=[Trk()], sem="a7")
        p.dma("sp", thr[:], thr_d, w=[Trk()], sem="a3")
        with p.scope() as sc:
            invf = sc.sbuf("invf", [64, 1], F32)
            t_r = Trk()
            p.dma("sp", invf[:], invf_p_d, w=[t_r])
            posi = sc.sbuf("posi", [64, T], I32)
            kk = sc.sbuf("kk", [64, 2, T], F32)
            tmp = sc.sbuf("tmp", [64, 2, T], F32)
            p.dma("sp", posi[:], pos_own_d.partition_broadcast(64), w=[t_r], sem="a8")
            p.cp("dve", tabs[:, 0, :], posi[:], r=[t_r], w=[t_r])
            p.ts("dve", tabs[:, 0, :], tabs[:, 0, :], invf[:, 0:1], None, ALU.mult, r=[t_r], w=[t_r])
            p.ts("dve", tabs[:, 1, :], tabs[:, 0, :], float(np.pi / 2), None, ALU.add, r=[t_r], w=[t_r])
            rope_reduce_sin(p, tabs[:], kk[:], tmp[:], t_r)
        p.barrier()
        KnT = sc0.sbuf("KnT", [128, S], BF16)
        V = sc0.sbuf("V", [128, NKB, 128], BF16)
        t_Kn, t_V = Trk(), Trk()
        wqn = [sc0.sbuf("wqn", [128, 8, 128], BF16) for _ in range(2)]
        wqp = [sc0.sbuf("wqp", [128, 8, 64], BF16) for _ in range(2)]
        wk = [sc0.sbuf("wk", [128, 4, 128], BF16) for _ in range(2)]
        wvv = [sc0.sbuf("wvv", [128, 4, 128], BF16) for _ in range(2)]
        t_w = trks(2)
        qn = [sc0.sbuf("qn", [128, 512], BF16) for _ in range(2)]
        qp = [sc0.sbuf("qp", [64, 512], BF16) for _ in range(2)]
        t_qn, t_qp = trks(2), trks(2)
        qpf = sc0.sbuf("qpf", [64, 512], F32)
        qpb = sc0.sbuf("qpb", [64, 512], BF16)
        t1 = sc0.sbuf("t1", [64, 512], F32)
        t2 = sc0.sbuf("t2", [64, 512], F32)
        t_qpf, t_qpb, t_t1, t_t2 = Trk(), Trk(), Trk(), Trk()
        pT = [sc0.sbuf("pT", [128, 512], BF16) for _ in range(4)]
        t_pT = trks(4)
        acc = sc0.sbuf("acc", [128, 512], F32)
        t_acc = Trk()
        rden = sc0.sbuf("rden", [128, 512], F32)
        t_rden = Trk()
        ob = [sc0.sbuf("ob", [128, 512], BF16) for _ in range(2)]
        t_ob = trks(2)
        nq = 0
        wqnv, wqpv, wkv, wvvw = wview(w_qn), wview(w_qp), wview(w_k), wview(w_v)
        for h in range(MLA_H):
            ws = h % 2
            hs = slice(h * 128, (h + 1) * 128)
            o1 = p.dma("pool", wqn[ws][:], wqnv[:, :, hs], w=[t_w[ws]], sem=f"hw{ws}")
            keep = list(t_w[ws].w)
            o2 = p.dma("pool", wqp[ws][:], wqpv[:, :, h * 64:(h + 1) * 64], sem=f"hw{ws}")
            o3 = p.dma("pool", wk[ws][:], wkv[:, :, hs], sem=f"hw{ws}")
            o4 = p.dma("pool", wvv[ws][:], wvvw[:, :, hs], sem=f"hw{ws}")
            for o_ in (o2, o3, o4):
                o_.deps |= o1.deps
            t_w[ws].w = keep + [o2.idx, o3.idx, o4.idx]
            for kt in range(S // 512):
                bk = kt % 2
                for kc in range(4):
                    p.mm(p.ps[bk][:], wk[ws][:, kc, :], ckv[:, kc, kt * 512:(kt + 1) * 512], kc == 0, kc == 3,
                         r=[t_c, t_w[ws]], w=[p.tps[bk]])
                p.cp("act" if kt % 2 else "dve", KnT[:, kt * 512:(kt + 1) * 512], p.ps[bk][:], r=[p.tps[bk]], w=[t_Kn])
            for k4 in range(NKB // 4):
                bk = k4 % 2
                for j in range(4):
                    kb = k4 * 4 + j
                    for kc in range(4):
                        p.mm(p.ps[bk][:, j * 128:(j + 1) * 128], ckv[:, kc, kb * 128:(kb + 1) * 128], wvv[ws][:, kc, :],
                             kc == 0, kc == 3, r=[t_c, t_w[ws]], w=[p.tps[bk]])
                p.cp("act" if k4 % 2 else "dve", V[:, k4 * 4:(k4 + 1) * 4, :],
                     p.ps[bk][:].rearrange("p (j v) -> p j v", v=128), r=[p.tps[bk]], w=[t_V])
            for i in range(NT_Q):
                s = nq % 2
                nq += 1
                tsl = slice(i * 512, (i + 1) * 512)
                for kc in range(8):
                    p.mm(p.ps[0][:], wqn[ws][:, kc, :], qlat[:, kc, tsl], kc == 0, kc == 7, r=[t_w[ws]], w=[p.tps[0]])
                p.act(qn[s][:], p.ps[0][:], AF.Copy, r=[p.tps[0]], w=[t_qn[s]], scale=SCALE)
                for kc in range(8):
                    p.mm(p.ps[1][0:64, :], wqp[ws][:, kc, :], qlat[:, kc, tsl], kc == 0, kc == 7, r=[t_w[ws]], w=[p.tps[1]])
                p.ts("dve", qpf[:], p.ps[1][0:64, :], SCALE, None, ALU.mult, r=[p.tps[1]], w=[t_qpf])
                p.cp("pool", qpb[:], qpf[:], r=[t_qpf], w=[t_qpb])
                p.mm(p.ps[1][0:64, :], rm[:], qpb[:], True, True, r=[t_qpb], w=[p.tps[1]])
                p.tt("dve", t1[:], qpf[:], tabs[:, 1, tsl], ALU.mult, r=[t_qpf], w=[t_t1])
                p.tt("dve", t2[:], p.ps[1][0:64, :], tabs[:, 0, tsl], ALU.mult, r=[p.tps[1]], w=[t_t2])
                p.tt("pool", qp[s][:], t1[:], t2[:], ALU.add, r=[t_t1, t_t2], w=[t_qp[s]])
                p.op("pool", lambda e, a_=acc[:]: e.memset(a_, 0.0), w=[t_acc])
                po = 6 + (nq % 2)
                LAG = 2

                def s_step(kb):
                    bk = 2 + kb % 4
                    ks = slice(kb * 128, (kb + 1) * 128)
                    col = i * NKB + kb
                    p.mm(p.ps[bk][:], KnT[:, ks], qn[s][:], True, False, r=[t_Kn, t_qn[s]], w=[p.tps[bk]])
                    p.mm(p.ps[bk][:], kpe[:, ks], qp[s][:], False, True, r=[t_c, t_qp[s]], w=[p.tps[bk]])
                    p.act(pT[kb % 4][:], p.ps[bk][:], AF.Exp, r=[p.tps[bk]], w=[t_pT[kb % 4]])
                    p.op("dve", lambda e, o_=pT[kb % 4][:], d_=Dm[:], c_=thr[:, col:col + 1]: e.scalar_tensor_tensor(
                        out=o_, in0=d_, scalar=c_, in1=o_, op0=ALU.is_ge, op1=ALU.mult), r=[t_pT[kb % 4]], w=[t_pT[kb % 4]])
                    p.tt("pool", acc[:], acc[:], pT[kb % 4][:], ALU.add, r=[t_pT[kb % 4]], w=[t_acc])

                def pv_step(kb):
                    p.mm(p.ps[po][:], V[:, kb, :], pT[kb % 4][:], kb == 0, kb == NKB - 1,
                         r=[t_V, t_pT[kb % 4]], w=[p.tps[po]])

                for n in range(NKB + LAG):
                    if n < NKB:
                        s_step(n)
                    if n - LAG >= 0:
                        pv_step(n - LAG)
                p.mm(p.ps[0][:], cst.ones[:], acc[:], True, True, r=[t_acc, cst.t], w=[p.tps[0]])
                p.op("dve", lambda e, o_=rden[:], i_=p.ps[0][:]: e.reciprocal(out=o_, in_=i_), r=[p.tps[0]], w=[t_rden])
                p.tt("dve", ob[s][:], p.ps[po][:], rden[:], ALU.mult, r=[p.tps[po], t_rden], w=[t_ob[s]])
                o = p.dma("sp", oT_own[hs, tsl], ob[s][:], r=[t_ob[s]], sem=f"ob{s}")
                t_ob[s].r.append(o.idx)


FUSED_IN = None


def build_fused():
    nc = bass.Bass("TRN2", target_bir_lowering=False)
    x = dram_in(nc, "x", [S, D])
    c_pc = dram_in(nc, "c_pc", [128, DC])
    ada_w = dram_in(nc, "ada_w", [2, D, 6 * D])
    ada_b = dram_in(nc, "ada_b", [2, 6 * D])
    ng_pc = dram_in(nc, "ng_pc", [128, 8, DC])
    ng_rows = dram_in(nc, "ng_rows", [8, D])
    lbl_pc = dram_in(nc, "lbl_pc", [128, 2, DC])
    kvin_pc = dram_in(nc, "kvin_pc", [128, DC])
    hng_d = dram_in(nc, "hng", [128, 1])
    kvg_d = dram_in(nc, "kvg", [KVL])
    qng_d = dram_in(nc, "qng", [QL])
    pos_pb = dram_in(nc, "pos_pb", [128, S // 128], I32)
    pos_own = dram_in(nc, "pos_own", [T], I32)
    invf_d = dram_in(nc, "c_invf", [128, 32])
    invf_p_d = dram_in(nc, "invf_p", [64, 1])
    smask_d = dram_in(nc, "c_smask", [128, T2])
    rm_d = dram_in(nc, "c_rm", [64, 64])
    dm_d = dram_in(nc, "c_dm", [128, 512])
    sel_d = dram_in(nc, "sel", [128, NCORE])
    thr_d = dram_in(nc, "thr", [128, (T // 512) * NKB])
    w_in = dram_in(nc, "w_in", [D, 4 * D])
    w_out = dram_in(nc, "w_out", [D, D])
    wg = dram_in(nc, "wg", [2, D, FFN])
    wu = dram_in(nc, "wu", [2, D, FFN])
    wd = dram_in(nc, "wd", [2, FFN, D])
    w_dkv = dram_in(nc, "w_dkv", [D, KVL + 64])
    w_dq = dram_in(nc, "w_dq", [D, QL])
    w_qn = dram_in(nc, "w_qn", [QL, MLA_H * 128])
    w_qp = dram_in(nc, "w_qp", [QL, MLA_H * 64])
    w_k = dram_in(nc, "w_k", [KVL, MLA_H * 128])
    w_v = dram_in(nc, "w_v", [KVL, MLA_H * 128])
    w_o = dram_in(nc, "w_o", [MLA_H * 128, D])
    cd = {k: dram_in(nc, k, v) for k, v in CONST_SPECS.items()}
    out = dram_out(nc, "out", [T, D])
    mod_s = dram_tmp(nc, "mod_s", [2, 6 * D])
    olocT_s = dram_tmp(nc, "olocT_s", [D, T])
    sgT_s = dram_tmp(nc, "sgT_s", [D, T], BF16)
    onT_s = dram_tmp(nc, "onT_s", [D, T], BF16)
    y_s = dram_tmp(nc, "y_s", [T, D])
    x1_s = dram_tmp(nc, "x1_s", [T, D])
    x2_s = dram_tmp(nc, "x2_s", [T, D])
    x2own = dram_tmp(nc, "x2own", [T, D])
    x3_s = dram_tmp(nc, "x3_s", [T, D])
    hid_s = dram_tmp(nc, "hid_s", [FFN, T], BF16)
    ckvT_all = dram_tmp(nc, "ckvT_all", [KVL, S], BF16)
    kpeT_all = dram_tmp(nc, "kpeT_all", [64, S], BF16)
    qlatT_own = dram_tmp(nc, "qlatT_own", [QL, T], BF16)
    oT_own = dram_tmp(nc, "oT_own", [MLA_H * 128, T], BF16)
    p = Prog(nc)
    dummy = []
    with p.scope() as sc0:
        cst = Consts(p, sc0, cd)
        ada_stage(p, cst, ada_w, ada_b, c_pc, mod_s, 6 * D)
        modpc = sc0.sbuf("modpc", [128, 2, 192], F32)
        t_mod = Trk()
        with p.scope() as sc:
            mr = [sc.sbuf("mr", [96, 128], F32) for _ in range(2)]
            t_mr = trks(2)
            n = 0
            for l in range(2):
                mv = mod_s[l].rearrange("(r q) -> r q", q=128)
                for hf in range(2):
                    s = n % 2
                    n += 1
                    p.dma("sp", mr[s][:], mv[hf * 96:(hf + 1) * 96, :], w=[t_mr[s]], sem=f"mr{s}")
                    p.tr(p.ps[s][:, 0:96], mr[s][:], cst.idf[0:96, 0:96], r=[t_mr[s], cst.t], w=[p.tps[s]])
                    p.cp("dve", modpc[:, l, hf * 96:(hf + 1) * 96], p.ps[s][:, 0:96], r=[p.tps[s]], w=[t_mod])
        ngp = sc0.sbuf("ngp", [128, 8, DC], F32)
        lblp = sc0.sbuf("lblp", [128, 2, DC], F32)
        kvin = sc0.sbuf("kvin", [128, DC], F32)
        hng = sc0.sbuf("hng", [128, 1], F32)
        sel = sc0.sbuf("sel", [128, NCORE], F32)
        smask = sc0.sbuf("smask", [128, T2], F32)
        t_vec, t_sm = Trk(), Trk()
        p.dma("sp", ngp[:], ng_pc, w=[t_vec], sem="v0")
        p.dma("sp", lblp[:], lbl_pc, w=[Trk()], sem="v1")
        p.dma("sp", kvin[:], kvin_pc, w=[Trk()], sem="v2")
        p.dma("sp", hng[:], hng_d, w=[Trk()], sem="v3")
        p.dma("sp", sel[:], sel_d, w=[Trk()], sem="v4")
        p.dma("sp", smask[:], smask_d, w=[t_sm], sem="v5")
        p.barrier()

        def affine(l, k, isc, ish, name):
            A = sc0.sbuf(name, [128, DC], F32)
            tA = Trk()
            p.ts("dve", A[:], modpc[:, l, isc * 32:(isc + 1) * 32], 1.0, None, ALU.add, r=[t_mod], w=[tA])
            p.tt("dve", A[:], A[:], ngp[:, l * 4 + k, :], ALU.mult, r=[tA, t_vec], w=[tA])
            return A, modpc[:, l, ish * 32:(ish + 1) * 32], tA

        A_m0, B_m0, t_m0 = affine(0, 0, 1, 0, "Am0")
        A_f0, B_f0, t_f0 = affine(0, 2, 4, 3, "Af0")
        A_m1, B_m1, t_m1 = affine(1, 0, 1, 0, "Am1")
        A_f1, B_f1, t_f1 = affine(1, 2, 4, 3, "Af1")
        lb = sc0.sbuf("lb", [128, DC], F32)
        oml = sc0.sbuf("oml", [128, DC], F32)
        t_lb = Trk()
        p.tt("dve", lb[:], lblp[:, 0, :], lblp[:, 1, :], ALU.subtract, w=[t_lb])
        p.act(lb[:], lb[:], AF.Sigmoid, r=[t_lb], w=[t_lb])
        p.ts("dve", oml[:], lb[:], -1.0, 1.0, ALU.mult, ALU.add, r=[t_lb], w=[t_lb])
        Sall = sc0.sbuf("Sall", [128, HG_H, 128], F32)
        t_S = trks(HG_H)
        Bc = sc0.sbuf("Bc", [128, HG_H], F32)
        t_Bc = trks(HG_H)
        p.op("dve", lambda e: e.memset(Bc[:], 0.0), w=t_Bc)
        p.op("pool", lambda e: e.memset(Sall[:], 0.0), w=t_S)
        p.barrier()
        H = dict(A=A_m0, B=B_m0, t_AB=t_m0, lb=lb, oml=oml, t_lb=t_lb, smask=smask, t_sm=t_sm, Sall=Sall, t_S=t_S, Bc=Bc, t_Bc=t_Bc,
                 wv=wview(w_in), olocT=olocT_s, qhatT=None, sgT=sgT_s)

        def gvec(sc, l, iv, k, name):
            G, t_G = load_rowbc(p, sc, name, mod_s[l, iv * D:(iv + 1) * D], D)
            with p.scope() as scg:
                G2, t_G2 = load_rowbc(p, scg, name + "2", ng_rows[l * 4 + k], D)
                p.tt("dve", G[:], G[:], G2[:], ALU.mult, r=[t_G, t_G2], w=[t_G])
            return G, t_G

        for j in range(NCORE):
            xj = x[j * T:(j + 1) * T, :]
            for tp in range(T // T2):
                hgrn2_pass(p, cst, H, xj[tp * T2:(tp + 1) * T2, :], tp * T2, (j == 0 and tp == 0), dummy)
            gate_stage(p, cst, olocT_s, sgT_s, onT_s, hng, t_vec)
            with p.scope() as sc:
                evac = y_evac_to_dram(p, sc, y_s)
                linear_T_streamed(p, sc, wview(w_out), onT_s.rearrange("(c p) t -> p c t", p=128), DC, cgroups(0, D), evac, name="wo")
            with p.scope() as sc:
                G, t_G = gvec(sc, 0, 2, 1, "Gm")
                resid_pass(p, xj, y_s, G, t_G, x1_s)
            with p.scope() as sc:
                G, t_G = gvec(sc, 0, 5, 3, "Gf")
                ffn_stage(p, cst, x1_s, x2_s, wg[0], wu[0], wd[0], A_f0, B_f0, t_f0, G, t_G, hid_s, y_s)
            p.barrier()
            select_pass(p, x2_s, x2own, sel[:, j:j + 1], t_vec, j == 0)
            latents_stage(p, cst, x2_s, kvin[:], t_vec, w_dkv, kvg_d, pos_pb[:, j * NB:(j + 1) * NB], invf_d,
                          ckvT_all.rearrange("(c p) t -> p c t", p=128)[:, :, j * T:(j + 1) * T], kpeT_all[:, j * T:(j + 1) * T], dummy)
        qlat_stage(p, cst, x2own, A_m1, B_m1, t_m1, w_dq, qng_d, qlatT_own.rearrange("(c p) t -> p c t", p=128), dummy)
        attn_own_stage(p, cst, qlatT_own, ckvT_all, kpeT_all, pos_own, invf_p_d, w_qn, w_qp, w_k, w_v, rm_d, dm_d, thr_d,
                       cd["c_tri"], oT_own)
        with p.scope() as sc:
            evac = y_evac_to_dram(p, sc, y_s)
            linear_T_streamed(p, sc, wview(w_o), oT_own.rearrange("(c p) t -> p c t", p=128), MLA_H, cgroups(0, D), evac, name="wo1")
        with p.scope() as sc:
            G, t_G = gvec(sc, 1, 2, 1, "Gm1")
            resid_pass(p, x2own, y_s, G, t_G, x3_s)
        with p.scope() as sc:
            G, t_G = gvec(sc, 1, 5, 3, "Gf1")
            finals = ffn_stage(p, cst, x3_s, out, wg[1], wu[1], wd[1], A_f1, B_f1, t_f1, G, t_G, hid_s, y_s)
    p.emit(finals)
    return nc


def kernel_fused(x, c, positions, ada_w, ada_b, norm_g, ffn_w_gate, ffn_w_up, ffn_w_down,
                 hg_w_in, hg_lb_logits, hg_norm_g, hg_w_out,
                 mla_w_dq, mla_q_norm_g, mla_w_uq, mla_w_o,
                 kv_norm_in_g, kv_w_dkv, kv_norm_g, kv_w_ukv):
    f32 = np.float32
    ca = np.ascontiguousarray
    ng = np.asarray(norm_g, f32)
    pos = np.asarray(positions, np.int32)[0]
    inv = _inv_freq()
    rm = np.zeros((64, 64), f32)
    for i in range(32):
        rm[i + 32, i] = -1.0
        rm[i, i + 32] = 1.0
    dm = (np.arange(512)[None, :] - np.arange(128)[:, None]).astype(f32)
    smask = np.ones((128, T2), f32)
    smask[:, ::64] = 0
    wuq = np.asarray(mla_w_uq, f32)[0].reshape(QL, MLA_H, 192)
    wukv = np.asarray(kv_w_ukv, f32).reshape(KVL, MLA_H, 256)
    lbl = np.asarray(hg_lb_logits, f32)
    shared = {
        "x": ca(np.asarray(x, f32)[0]), "c_pc": _pc(np.asarray(c, f32)[0]),
        "ada_w": np.asarray(ada_w, f32), "ada_b": np.asarray(ada_b, f32),
        "ng_pc": ca(np.stack([_pc(ng[l, k]) for l in range(2) for k in range(4)], axis=1)),
        "ng_rows": ca(ng.reshape(8, D)), "lbl_pc": ca(np.stack([_pc(lbl[0]), _pc(lbl[1])], axis=1)),
        "kvin_pc": _pc(kv_norm_in_g), "hng": ca(np.asarray(hg_norm_g, f32)[0].reshape(128, 1)),
        "kvg": np.asarray(kv_norm_g, f32), "qng": np.asarray(mla_q_norm_g, f32)[0],
        "pos_pb": ca(pos.reshape(S // 128, 128).T), "c_invf": ca(np.tile(inv[None, :], (128, 1))),
        "invf_p": np.concatenate([inv, inv]).reshape(64, 1).astype(f32), "c_smask": smask, "c_rm": rm, "c_dm": dm,
        "w_in": np.asarray(hg_w_in, f32)[0], "w_out": np.asarray(hg_w_out, f32)[0],
        "wg": np.asarray(ffn_w_gate, f32), "wu": np.asarray(ffn_w_up, f32), "wd": np.asarray(ffn_w_down, f32),
        "w_dkv": np.asarray(kv_w_dkv, f32), "w_dq": np.asarray(mla_w_dq, f32)[0],
        "w_qn": ca(wuq[:, :, :128]).reshape(QL, MLA_H * 128), "w_qp": ca(wuq[:, :, 128:]).reshape(QL, MLA_H * 64),
        "w_k": ca(wukv[:, :, :128]).reshape(KVL, MLA_H * 128), "w_v": ca(wukv[:, :, 128:]).reshape(KVL, MLA_H * 128),
        "w_o": np.asarray(mla_w_o, f32)[0],
    }
    shared.update(const_arrays())
    maps = []
    for ci in range(NCORE):
        m = dict(shared)
        sel = np.zeros((128, NCORE), f32)
        sel[:, ci] = 1.0
        thr = np.zeros((128, (T // 512) * NKB), f32)
        for i in range(T // 512):
            for kb in range(NKB):
                thr[:, i * NKB + kb] = kb * 128 - (ci * T + i * 512)
        m["sel"] = sel
        m["thr"] = thr
        m["pos_own"] = ca(pos[ci * T:(ci + 1) * T])
        maps.append(m)
    r = _run(build_fused(), maps)
    return np.concatenate([o["out"] for o in r], axis=0).reshape(1, S, D).astype(f32)


kernel = kernel_unfused
```
